# Optimizing a Trainium2 kernel written in Bass

```python
import jax
import jax.numpy as jnp
from jax import lax
import numpy as np

D_MODEL = 1024
BATCH = 1
SEQ = 16384
DEPTH = 4

N_MIXERS = 2
N_LRU_LAYERS = (DEPTH + N_MIXERS - 1) // N_MIXERS
N_RWKV_LAYERS = DEPTH // N_MIXERS
D_RNN = 1280
LRU_BLOCKS = 10
LRU_BLOCK = D_RNN // LRU_BLOCKS
CONV_WIDTH = 4
LRU_C = 8.0
LRU_MIN_RAD = 0.9
LRU_MAX_RAD = 0.999
RW_HEAD = 64
RW_HEADS = D_MODEL // RW_HEAD
LORA_DECAY = 64
LORA_AAA = 64
LORA_MV = 32
LORA_GATE = 160
N_SHIFT_MIX = 6
RW_GN_EPS = 64e-5
D_FF = ((8 * D_MODEL // 3 + 255) // 256) * 256
D_PLE = 256
RMS_EPS = 1e-6

kernel_name = "hawk_rwkv7_interleaved_trunk"


def rms_norm(x, g):
    xf = x.astype(jnp.float32)
    y = xf * lax.rsqrt(jnp.mean(xf * xf, axis=-1, keepdims=True) + RMS_EPS)
    return (y * g.astype(jnp.float32)).astype(x.dtype)


def causal_depthwise_conv(x, w, b):
    K = w.shape[0]
    T = x.shape[1]
    xp = jnp.pad(x, ((0, 0), (K - 1, 0), (0, 0)))
    y = b
    for k in range(K):
        y = y + xp[:, k:k + T] * w[k]
    return y


def block_diag_linear(x, w, b):
    B, T, _ = x.shape
    xb = x.reshape(B, T, LRU_BLOCKS, LRU_BLOCK)
    y = jnp.einsum('btnc,ncd->btnd', xb, w.astype(x.dtype))
    return y.reshape(B, T, D_RNN) + b.astype(x.dtype)


def rg_lru(x, wa, ba, wx, bx, lam):
    xf = x.astype(jnp.float32)
    r = jax.nn.sigmoid(block_diag_linear(xf, wa, ba))
    i = jax.nn.sigmoid(block_diag_linear(xf, wx, bx))
    log_a = -LRU_C * r * jax.nn.softplus(-lam.astype(jnp.float32))
    a = jnp.exp(log_a)
    b = jnp.sqrt(-jnp.expm1(2.0 * log_a)) * (i * xf)

    def combine(c1, c2):
        a1, b1 = c1
        a2, b2 = c2
        return a1 * a2, a2 * b1 + b2

    _, h = lax.associative_scan(combine, (a, b), axis=1)
    return h.astype(x.dtype)


def hawk_block(x, w_in, b_in, conv_w, conv_b, wa, ba, wx, bx, lam, w_out, b_out):
    gx = x @ w_in + b_in
    gate_branch, rec_branch = jnp.split(gx, 2, axis=-1)
    gate_branch = jax.nn.gelu(gate_branch)
    rec = causal_depthwise_conv(rec_branch, conv_w, conv_b)
    rec = rg_lru(rec, wa, ba, wx, bx, lam)
    return (rec * gate_branch) @ w_out + b_out


def token_shift(x):
    return jnp.pad(x[:, :-1], ((0, 0), (1, 0), (0, 0)))


def rwkv7_scan(r, w, k, v, a_vec, b_vec):
    B, T, H, N = r.shape
    xs = tuple(jnp.moveaxis(t, 1, 0) for t in (r, w, k, v, a_vec, b_vec))

    def step(S, inp):
        r_t, w_t, k_t, v_t, a_t, b_t = inp
        sa = jnp.einsum('bhvk,bhk->bhv', S, a_t)
        S = S * w_t[:, :, None, :] + sa[..., None] * b_t[:, :, None, :] + v_t[..., None] * k_t[:, :, None, :]
        y = jnp.einsum('bhvk,bhk->bhv', S, r_t)
        return S, y

    S0 = jnp.zeros((B, H, N, N), jnp.float32)
    _, y = lax.scan(step, S0, xs)
    return jnp.moveaxis(y, 0, 1)


def rwkv7_time_mix(x, v_first, mu, w_rkv, w_o, w0, w1, w2, a0, a1, a2, g1, g2,
                   k_k, k_a, r_k, ln_w, ln_b, vres):
    B, T, D = x.shape
    f32 = jnp.float32
    xx = token_shift(x) - x
    xs = x[None] + xx[None] * mu[:, None, None, :]
    xr, xw, xk, xv, xa, xg = xs
    r, k, v = jnp.einsum('nbtd,nde->nbte', jnp.stack([xr, xk, xv]), w_rkv)
    w = -jax.nn.softplus(-(w0 + jnp.tanh(xw @ w1) @ w2)) - 0.5
    a = jax.nn.sigmoid(a0 + (xa @ a1) @ a2)
    g = jax.nn.sigmoid(xg @ g1) @ g2

    def heads(t):
        return t.reshape(B, T, RW_HEADS, RW_HEAD).astype(f32)

    kk = heads(k * k_k)
    kk = kk / jnp.maximum(jnp.sqrt(jnp.sum(kk * kk, axis=-1, keepdims=True)), 1e-12)
    k = k * (1.0 + (a - 1.0) * k_a)
    if vres is None:
        v_first = v
    else:
        v0, v1, v2 = vres
        v = v + (v_first - v) * jax.nn.sigmoid(v0 + (xv @ v1) @ v2)
    decay = jnp.exp(-jnp.exp(w.astype(f32)))
    rh, kh, vh, ah = heads(r), heads(k), heads(v), heads(a)
    y = rwkv7_scan(rh, heads(decay), kh, vh, -kk, kk * ah)
    mean = jnp.mean(y, axis=-1, keepdims=True)
    var = jnp.mean(jnp.square(y - mean), axis=-1, keepdims=True)
    yn = ((y - mean) * lax.rsqrt(var + RW_GN_EPS)).reshape(B, T, D)
    yn = yn * ln_w.astype(f32) + ln_b.astype(f32)
    bonus = jnp.sum(rh * kh * r_k.astype(f32), axis=-1, keepdims=True) * vh
    out = (yn + bonus.reshape(B, T, D)).astype(x.dtype) * g
    return out @ w_o, v_first


def swiglu(x, w_in, w_out):
    gu = x @ w_in
    gate, up = jnp.split(gu, 2, axis=-1)
    return (jax.nn.silu(gate) * up) @ w_out


def setup_inputs(seed: int = 0) -> dict:
    key = jax.random.key(seed)
    ks = iter(jax.random.split(key, 64))
    f32 = jnp.float32

    def nrm(shape, scale):
        return scale * jax.random.normal(next(ks), shape, f32)

    def gain(shape):
        return 1.0 + nrm(shape, 0.02)

    D, DR, NL, NR, L = D_MODEL, D_RNN, N_LRU_LAYERS, N_RWKV_LAYERS, DEPTH
    x = nrm((BATCH, SEQ, D), 1.0)
    p = nrm((L, BATCH, SEQ, D_PLE), 1.0)
    u = jax.random.uniform(next(ks), (NL, DR), f32, LRU_MIN_RAD, LRU_MAX_RAD)
    lru_lambda = jnp.log(u) - jnp.log1p(-u)
    rw_w0 = jax.random.uniform(next(ks), (NR, D), f32, -6.0, 1.0)
    rw_mu = jax.random.uniform(next(ks), (NR, N_SHIFT_MIX, D), f32)
    return {
        "x": x,
        "p": p,
        "norm_mix": gain((L, D)),
        "norm_ffn": gain((L, D)),
        "norm_ple": gain((L, D)),
        "norm_final": gain((D,)),
        "lru_w_in": nrm((NL, D, 2 * DR), D ** -0.5),
        "lru_b_in": nrm((NL, 2 * DR), 0.01),
        "lru_conv_w": nrm((NL, CONV_WIDTH, DR), CONV_WIDTH ** -0.5),
        "lru_conv_b": nrm((NL, DR), 0.01),
        "lru_w_gate_a": nrm((NL, LRU_BLOCKS, LRU_BLOCK, LRU_BLOCK), LRU_BLOCK ** -0.5),
        "lru_b_gate_a": nrm((NL, DR), 0.01),
        "lru_w_gate_x": nrm((NL, LRU_BLOCKS, LRU_BLOCK, LRU_BLOCK), LRU_BLOCK ** -0.5),
        "lru_b_gate_x": nrm((NL, DR), 0.01),
        "lru_lambda": lru_lambda,
        "lru_w_out": nrm((NL, DR, D), DR ** -0.5),
        "lru_b_out": nrm((NL, D), 0.01),
        "rw_mu": rw_mu,
        "rw_w_rkv": nrm((NR, 3, D, D), D ** -0.5),
        "rw_w_o": nrm((NR, D, D), D ** -0.5),
        "rw_w0": rw_w0,
        "rw_w1": nrm((NR, D, LORA_DECAY), D ** -0.5),
        "rw_w2": nrm((NR, LORA_DECAY, D), 0.1 * LORA_DECAY ** -0.5),
        "rw_a0": nrm((NR, D), 0.5),
        "rw_a1": nrm((NR, D, LORA_AAA), D ** -0.5),
        "rw_a2": nrm((NR, LORA_AAA, D), 0.1 * LORA_AAA ** -0.5),
        "rw_v0": 1.0 + nrm((NR - 1, D), 0.1),
        "rw_v1": nrm((NR - 1, D, LORA_MV), D ** -0.5),
        "rw_v2": nrm((NR - 1, LORA_MV, D), 0.1 * LORA_MV ** -0.5),
        "rw_g1": nrm((NR, D, LORA_GATE), D ** -0.5),
        "rw_g2": nrm((NR, LORA_GATE, D), LORA_GATE ** -0.5),
        "rw_k_k": 0.85 + nrm((NR, D), 0.05),
        "rw_k_a": 1.0 + nrm((NR, D), 0.05),
        "rw_r_k": nrm((NR, RW_HEADS, RW_HEAD), 0.1),
        "rw_ln_w": gain((NR, D)),
        "rw_ln_b": nrm((NR, D), 0.01),
        "ffn_w_in": nrm((L, D, 2 * D_FF), D ** -0.5),
        "ffn_w_out": nrm((L, D_FF, D), D_FF ** -0.5),
        "ple_w_proj": nrm((L, D_PLE, D), D_PLE ** -0.5),
        "ple_w_gate": nrm((L, D, D), D ** -0.5),
    }


def reference(x, p, norm_mix, norm_ffn, norm_ple, norm_final,
              lru_w_in, lru_b_in, lru_conv_w, lru_conv_b, lru_w_gate_a, lru_b_gate_a,
              lru_w_gate_x, lru_b_gate_x, lru_lambda, lru_w_out, lru_b_out,
              rw_mu, rw_w_rkv, rw_w_o, rw_w0, rw_w1, rw_w2, rw_a0, rw_a1, rw_a2,
              rw_v0, rw_v1, rw_v2, rw_g1, rw_g2, rw_k_k, rw_k_a, rw_r_k, rw_ln_w, rw_ln_b,
              ffn_w_in, ffn_w_out, ple_w_proj, ple_w_gate):
    h = x
    v_first = None
    for i in range(DEPTH):
        j = i // N_MIXERS
        xn = rms_norm(h, norm_mix[i])
        if i % N_MIXERS == 0:
            mix = hawk_block(xn, lru_w_in[j], lru_b_in[j], lru_conv_w[j], lru_conv_b[j],
                             lru_w_gate_a[j], lru_b_gate_a[j], lru_w_gate_x[j], lru_b_gate_x[j],
                             lru_lambda[j], lru_w_out[j], lru_b_out[j])
        else:
            vres = None if j == 0 else (rw_v0[j - 1], rw_v1[j - 1], rw_v2[j - 1])
            mix, v_first = rwkv7_time_mix(xn, v_first, rw_mu[j], rw_w_rkv[j], rw_w_o[j],
                                          rw_w0[j], rw_w1[j], rw_w2[j], rw_a0[j], rw_a1[j], rw_a2[j],
                                          rw_g1[j], rw_g2[j], rw_k_k[j], rw_k_a[j], rw_r_k[j],
                                          rw_ln_w[j], rw_ln_b[j], vres)
        h = h + mix
        h = h + swiglu(rms_norm(h, norm_ffn[i]), ffn_w_in[i], ffn_w_out[i])
        ple = (p[i] @ ple_w_proj[i]) * jax.nn.sigmoid(rms_norm(h, norm_ple[i]) @ ple_w_gate[i])
        h = h + ple
    return rms_norm(h, norm_final)
```

```python
import contextlib
import numpy as np
import concourse.bass as bass
import concourse.mybir as mybir
from concourse.bass_utils import run_bass_kernel_spmd

F32 = mybir.dt.float32
BF16 = mybir.dt.bfloat16
AF = mybir.ActivationFunctionType
ALU = mybir.AluOpType
NCORES = 8
D, DC, DR, RC, DFF, FC, DPLE = 1024, 8, 1280, 10, 2816, 22, 256
HALO = 4
EPS = 1e-6


class Sched:
    ENGS = ('pe', 'act', 'dve', 'pool', 'sp')

    def __init__(self):
        self.ops = {e: [] for e in self.ENGS}
        self.cnt = {e: 0 for e in self.ENGS}
        self.waited = {e: {} for e in self.ENGS}
        self.lastw = {}
        self.readers = {}
        self.NS = 8
        self.dma_uses = {}
        self.dma_n = {q: 0 for q in self.ENGS}
        self.ncc = 0
        self.semkeys = set()

    def _deps(self, reads, writes):
        deps = []
        for k in reads:
            t = self.lastw.get(k)
            if t:
                deps.append(t)
        for k in writes:
            t = self.lastw.get(k)
            if t:
                deps.append(t)
            deps.extend(self.readers.get(k, {}).items())
        return deps

    def _waits(self, eng, deps):
        w = {}
        for (sk, v) in deps:
            if eng == 'pe' and sk == ('eng', 'pe'):
                continue
            if v > self.waited[eng].get(sk, 0) and v > w.get(sk, 0):
                w[sk] = v
        for sk, v in w.items():
            self.waited[eng][sk] = v
        return list(w.items())

    def _commit(self, tok, reads, writes):
        for k in writes:
            self.lastw[k] = tok
            self.readers[k] = {}
        for k in reads:
            if k not in writes:
                r = self.readers.setdefault(k, {})
                if tok[1] > r.get(tok[0], 0):
                    r[tok[0]] = tok[1]

    def op(self, eng, fn, reads=(), writes=(), *args, **kwargs):
        fn = (fn, args, kwargs)
        waits = self._waits(eng, self._deps(reads, writes))
        self.cnt[eng] += 1
        tok = (('eng', eng), self.cnt[eng])
        self.semkeys.add(tok[0])
        self.ops[eng].append((waits, fn, tok, 1))
        self._commit(tok, reads, writes)

    def dma(self, q, reads=(), writes=(), **kwargs):
        fn = ('dma_start', (), kwargs)
        slot = self.dma_n[q] % self.NS
        self.dma_n[q] += 1
        sk = ('dma', q, slot)
        self.semkeys.add(sk)
        uses = self.dma_uses.get(sk, 0)
        deps = self._deps(reads, writes)
        if uses:
            deps.append((sk, 16 * uses))
        waits = self._waits(q, deps)
        self.dma_uses[sk] = uses + 1
        tok = (sk, 16 * (uses + 1))
        self.ops[q].append((waits, fn, tok, 16))
        self._commit(tok, reads, writes)

    def cc(self, reads=(), writes=(), *args, **kwargs):
        fn = ('collective_compute', args, kwargs)
        sk = ('cc', self.ncc)
        self.ncc += 1
        self.semkeys.add(sk)
        waits = self._waits('pool', self._deps(reads, writes))
        tok = (sk, 1)
        self.ops['pool'].append((waits, fn, tok, None))
        self._commit(tok, reads, writes)

    def barrier(self):
        toks = [(('eng', e), self.cnt[e]) for e in self.ENGS if self.cnt[e]]
        toks += [(sk, 16 * u) for sk, u in self.dma_uses.items()]
        cct = [(('cc', i), 1) for i in range(self.ncc)]
        for e in self.ENGS:
            waits = self._waits(e, toks + (cct if e == 'pool' else []))
            if waits:
                self.ops[e].append((waits, None, None, 0))

    def finish(self, eng, keys):
        waits = self._waits(eng, self._deps(keys, ()))
        self.ops[eng].append((waits, None, None, 0))

    def emit(self, nc, stack):
        sems = {}
        for i, sk in enumerate(sorted(self.semkeys, key=str)):
            sems[sk] = stack.enter_context(nc.semaphore("s%d" % i))
        block = stack.enter_context(nc.Block())
        engmap = {'pe': block.tensor, 'act': block.scalar, 'dve': block.vector,
                  'pool': block.gpsimd, 'sp': block.sync}
        for e in self.ENGS:
            def body(eng, ops=self.ops[e]):
                for waits, fn, tok, inc in ops:
                    for sk, v in waits:
                        eng.wait_ge(sems[sk], v)
                    if fn is None:
                        continue
                    try:
                        ins = getattr(eng, fn[0])(*fn[1], **fn[2])
                    except Exception:
                        print("EMIT FAIL", fn[0], [str(a)[:160] for a in fn[1]], {k: str(v)[:160] for k, v in fn[2].items()})
                        raise
                    if inc is None:
                        ins.then_inc(sems[tok[0]])
                    else:
                        ins.then_inc(sems[tok[0]], inc)
            engmap[e](body)


def build(TL, prog, first=False):
    nc = bass.Bass("TRN2", target_bir_lowering=False)
    S = Sched()
    stack = contextlib.ExitStack()
    TT = min(512, TL)
    NT = TL // TT
    ST = min(512, TL)
    NST = TL // ST
    TPS = ST // TT
    NB = TL // 128

    declared = {}
    SHAPES = {
        "x": [TL, D], "p": [4, TL, DPLE], "vecs": [128, NVEC], "cvec": [128, 17], "ident_in": [128, 128],
        "ffn_w_in": [4, D, 2 * DFF], "ffn_w_out": [4, DFF, D], "ple_w_proj": [4, DPLE, D], "ple_w_gate": [4, D, D],
        "lru_w_in": [2, D, 2 * DR], "lru_w_gate_a": [2, RC * 128, 128], "lru_w_gate_x": [2, RC * 128, 128],
        "lru_w_out": [2, DR, D],
    }

    def W(name):
        if name not in declared:
            declared[name] = nc.dram_tensor(name, list(SHAPES[name]), F32, kind="ExternalInput").ap()
        return declared[name]

    outs_decl = {}

    def OUT(name, shape):
        if name not in outs_decl:
            outs_decl[name] = nc.dram_tensor(name, list(shape), F32, kind="ExternalOutput").ap()
        return outs_decl[name]

    SHAPES.update({"h_in": [128, DC, TL], "halo": [128, 32], "gsumm": [128, NCORES, 32],
                   "rw_w_rkv": [2, 3, D, D], "rw_w_o": [2, D, D], "rw_w1": [2, D, 64], "rw_w2": [2, 64, D],
                   "rw_a1": [2, D, 64], "rw_a2": [2, 64, D], "rw_v1": [1, D, 32], "rw_v2": [1, 32, D],
                   "rw_g1": [2, D, 160], "rw_g2": [2, 160, D], "rconst": [128, 640],
                   "grsumm": [NCORES, 16, 64, 128], "vfirst": [128, DC, TL]})
    out_keys = []

    def sb(name, shape, dt=F32):
        return stack.enter_context(nc.sbuf_tensor(name, list(shape), dt))

    hT = sb("hT", [128, DC, TL])
    xn = sb("xn", [128, DC, HALO + TL], BF16)
    vecs = sb("vecs_sb", [128, NVEC])
    cvec = sb("cvec_sb", [128, 17])
    gath = sb("gath", [128, NCORES, 32])
    smal = sb("smal", [128, 128])
    ident = sb("ident", [128, 128])
    ones_bf = sb("ones_bf", [128, 128], BF16)
    NWB = 3
    wbufs = [sb("wb%d" % i, [128, 4096], BF16) for i in range(NWB)]
    stage = [sb("stage%d" % i, [128, D]) for i in range(2)]
    sq = sb("sq", [128, DC, TT], BF16)
    rs = [sb("rs%d" % i, [128, TT]) for i in range(2)]
    tmpA = [sb("tmpA%d" % i, [128, TT]) for i in range(3)]
    ARENA = max(FC * ST * 2 + 2 * TL * 2, 5 * (TL + 4) * 4 + 2 * TL * 2 + 64, 5 * TL * 2 + 4 * 9400)
    arena = sb("arena", [128, ARENA // 2], BF16)
    big = arena[:, :FC * ST]
    pT = arena[:, FC * ST:FC * ST + 2 * TL].rearrange("p (k t) -> p k t", k=2)
    psums = [stack.enter_context(nc.psum_tensor("ps%d" % i, [128, 512], F32)) for i in range(8)]
    st = {'ps': 0, 'wb': 0, 'stage': 0, 'rs': 0, 'tmpA': 0}

    def rot(name, n):
        i = st[name]
        st[name] = (i + 1) % n
        return i

    def vcol(name, c=0):
        o = VOFF[name] + c
        return vecs[:, o:o + 1]

    S.dma('sp', (), ('vecs',), out=vecs[:], in_=W("vecs"))
    S.dma('sp', (), ('cvec',), out=cvec[:], in_=W("cvec"))
    S.dma('sp', (), ('ident',), out=ident[:], in_=W("ident_in"))
    S.op('dve', 'memset', (), ('ones',), ones_bf[:], 1.0)

    def tsl(t):
        return slice(t * TT, (t + 1) * TT)

    def xsl(t, sh=0):
        return slice(HALO + t * TT - sh, HALO + (t + 1) * TT - sh)

    def wview(i, kc, mb):
        return wbufs[i][:, :kc * mb].rearrange("p (k m) -> p k m", k=kc)

    def wload(i, kc, mb, src, m0=0, mw=None):
        mw = mb if mw is None else mw
        dst = wview(i, kc, mb)[:, :, m0:m0 + mw]
        S.dma('pool', (), (('wb', i),), out=dst, in_=src.rearrange("(k p) m -> p k m", p=128))

    def MM(pi, out, lhsT, rhs, start, stop, reads):
        S.op('pe', 'matmul', reads, (('ps', pi),), out, lhsT, rhs, start=start, stop=stop)

    def TR(pi, out, in_, reads, b0=0):
        np_ = in_.shape[0]
        S.op('pe', 'transpose', tuple(reads) + ('ident',), (('ps', pi),), out, in_, ident[b0:b0 + np_, b0:b0 + np_])

    def load_xT():
        for tb in range(NB):
            si = rot('stage', 2)
            S.dma('sp', (), (('stage', si),), out=stage[si][:], in_=W("x")[tb * 128:(tb + 1) * 128, :])
            for half in range(2):
                pi = rot('ps', 6)
                for j in range(4):
                    c = half * 4 + j
                    TR(pi, psums[pi][:, j * 128:(j + 1) * 128], stage[si][:, c * 128:(c + 1) * 128], (('stage', si),))
                S.op('dve', 'tensor_copy', (('ps', pi),), tuple(('h', half * 4 + j, tb * 128 // TT) for j in range(4)),
                     out=hT[:, half * 4:(half + 1) * 4, tb * 128:(tb + 1) * 128],
                     in_=psums[pi][:].rearrange("p (j t) -> p j t", j=4))

    def rms_tile(t):
        S.op('act', 'activation', tuple(('h', c, t) for c in range(DC)), ('sq',),
             out=sq[:], in_=hT[:, :, tsl(t)], func=AF.Square)
        pi = rot('ps', 6)
        for c in range(DC):
            MM(pi, psums[pi][:, :TT], ones_bf[:], sq[:, c, :], c == 0, c == DC - 1, ('sq', 'ones'))
        ri = rot('rs', 2)
        S.op('act', 'activation', (('ps', pi), 'vecs'), (('rs', ri),),
             out=rs[ri][:], in_=psums[pi][:, :TT], func=AF.Sqrt, bias=vcol('eps'), scale=1.0 / D)
        S.op('dve', 'reciprocal', (('rs', ri),), (('rs', ri),), out=rs[ri][:], in_=rs[ri][:])
        return ri

    def rmsnorm_xn(gname):
        for t in range(NT):
            ri = rms_tile(t)
            for c in range(DC):
                S.op('dve', 'scalar_tensor_tensor', (('h', c, t), ('rs', ri), 'vecs'), (('xn', c, t),),
                     out=xn[:, c, xsl(t)], in0=hT[:, c, tsl(t)], scalar=vcol(gname, c),
                     in1=rs[ri][:], op0=ALU.mult, op1=ALU.mult)

    def ffn(l):
        rmsnorm_xn('norm_ffn%d' % l)
        hid = big[:, :FC * ST].rearrange("p (f t) -> p f t", f=FC)
        for s_ in range(NST):
            for g0 in range(0, FC, 2):
                gw = min(2, FC - g0)
                wi = rot('wb', NWB)
                wload(wi, DC, 512, W("ffn_w_in")[l][:, g0 * 128:(g0 + gw) * 128], 0, gw * 128)
                wload(wi, DC, 512, W("ffn_w_in")[l][:, DFF + g0 * 128:DFF + (g0 + gw) * 128], 256, gw * 128)
                wv = wview(wi, DC, 512)
                for j in range(gw):
                    f = g0 + j
                    for tt in range(TPS):
                        t = s_ * TPS + tt
                        pa, pb = rot('ps', 6), rot('ps', 6)
                        for (pi, off) in ((pa, 0), (pb, 256)):
                            for c in range(DC):
                                MM(pi, psums[pi][:, :TT], wv[:, c, off + j * 128:off + (j + 1) * 128], xn[:, c, xsl(t)],
                                   c == 0, c == DC - 1, (('wb', wi), ('xn', c, t)))
                        ai = rot('tmpA', 3)
                        S.op('act', 'activation', (('ps', pa),), (('tmpA', ai),),
                             out=tmpA[ai][:], in_=psums[pa][:, :TT], func=AF.Silu)
                        S.op('dve', 'tensor_tensor', (('ps', pb), ('tmpA', ai)), (('big', f, tt),),
                             out=hid[:, f, tt * TT:(tt + 1) * TT], in0=psums[pb][:, :TT], in1=tmpA[ai][:], op=ALU.mult)
            for m0 in range(0, DC, 1):
                wi = rot('wb', NWB)
                wload(wi, FC, 128, W("ffn_w_out")[l][:, m0 * 128:(m0 + 1) * 128])
                wv = wview(wi, FC, 128)
                for j in range(1):
                    mo = m0 + j
                    for tt in range(TPS):
                        t = s_ * TPS + tt
                        pi = rot('ps', 6)
                        for f in range(FC):
                            MM(pi, psums[pi][:, :TT], wv[:, f, j * 128:(j + 1) * 128], hid[:, f, tt * TT:(tt + 1) * TT],
                               f == 0, f == FC - 1, (('wb', wi), ('big', f, tt)))
                        S.op('dve', 'tensor_tensor', (('ps', pi), ('h', mo, t)), (('h', mo, t),),
                             out=hT[:, mo, tsl(t)], in0=psums[pi][:, :TT], in1=hT[:, mo, tsl(t)], op=ALU.add)

    def ple(l):
        rmsnorm_xn('norm_ple%d' % l)
        for tb in range(NB):
            si = rot('stage', 2)
            S.dma('sp', (), (('stage', si),), out=stage[si][:, :DPLE], in_=W("p")[l, tb * 128:(tb + 1) * 128, :])
            pi = rot('ps', 6)
            for j in range(2):
                TR(pi, psums[pi][:, j * 128:(j + 1) * 128], stage[si][:, j * 128:(j + 1) * 128], (('stage', si),))
            S.op('act', 'activation', (('ps', pi),), (('pT', tb * 128 // TT),),
                 out=pT[:, :, tb * 128:(tb + 1) * 128], in_=psums[pi][:, :256].rearrange("p (j t) -> p j t", j=2), func=AF.Copy)
        wp = rot('wb', NWB)
        wload(wp, 2, 1024, W("ple_w_proj")[l])
        wpv = wview(wp, 2, 1024)
        for mg in range(2):
            wi = rot('wb', NWB)
            wload(wi, DC, 512, W("ple_w_gate")[l][:, mg * 512:(mg + 1) * 512])
            wv = wview(wi, DC, 512)
            for j in range(4):
                mo = mg * 4 + j
                for t in range(NT):
                    pa, pb = rot('ps', 6), rot('ps', 6)
                    for kc in range(2):
                        MM(pa, psums[pa][:, :TT], wpv[:, kc, mo * 128:(mo + 1) * 128], pT[:, kc, tsl(t)],
                           kc == 0, kc == 1, (('wb', wp), ('pT', t)))
                    for c in range(DC):
                        MM(pb, psums[pb][:, :TT], wv[:, c, j * 128:(j + 1) * 128], xn[:, c, xsl(t)],
                           c == 0, c == DC - 1, (('wb', wi), ('xn', c, t)))
                    ai, bi = rot('tmpA', 3), rot('tmpA', 3)
                    S.op('act', 'activation', (('ps', pb),), (('tmpA', ai),),
                         out=tmpA[ai][:], in_=psums[pb][:, :TT], func=AF.Sigmoid)
                    S.op('dve', 'tensor_tensor', (('ps', pa), ('tmpA', ai)), (('tmpA', bi),),
                         out=tmpA[bi][:], in0=psums[pa][:, :TT], in1=tmpA[ai][:], op=ALU.mult)
                    S.op('dve', 'tensor_tensor', (('tmpA', bi), ('h', mo, t)), (('h', mo, t),),
                         out=hT[:, mo, tsl(t)], in0=hT[:, mo, tsl(t)], in1=tmpA[bi][:], op=ALU.add)

    def load_halo():
        S.dma('sp', (), ('smal',), out=smal[:, :DC * HALO], in_=W("halo"))
        S.op('dve', 'tensor_copy', ('smal',), ('xnhalo',), out=xn[:, :, 0:HALO],
             in_=smal[:, :DC * HALO].rearrange("p (c k) -> p c k", c=DC))

    def tail_out():
        tl = smal[:, :DC * HALO].rearrange("p (c k) -> p c k", c=DC)
        S.op('dve', 'tensor_copy', tuple(('xn', c, NT - 1) for c in range(DC)) + ('smal',), ('smal',), out=tl, in_=xn[:, :, TL:TL + HALO])
        S.dma('sp', ('smal',), ('o_tail',), out=OUT("tail", [128, 32]), in_=smal[:, :32])
        out_keys.append('o_tail')

    def hawk(l, final_pass):
        j = l // 2
        V = lambda n, c=0: vcol('h%d_%s' % (j, n), c)
        rmsnorm_xn('norm_mix%d' % l)
        load_halo()
        S.barrier()
        fa = arena[:].bitcast(F32)
        W4 = TL + 4
        recb = fa[:, 0:W4]
        cv = fa[:, W4:W4 + TL]
        Abuf = fa[:, 2 * W4:2 * W4 + TL]
        Hl = fa[:, 3 * W4:3 * W4 + TL]
        o16 = 5 * W4 * 2
        cvb = arena[:, o16:o16 + TL]
        P1 = arena[:, o16 + TL:o16 + 2 * TL]
        c8 = smal[:, 64:74]
        c16 = smal[:, 74:84]
        rsum = smal[:, 84:84 + NT]
        rtot = smal[:, 92:93]
        hin = smal[:, 96:106]
        carry = smal[:, 106:116]
        S.op('act', 'activation', ('vecs',), ('c8',), out=c8, in_=vecs[:, VOFF['h%d_lam' % j]:VOFF['h%d_lam' % j] + RC], func=AF.Exp, scale=-1.0)
        S.op('act', 'activation', ('c8', 'vecs'), ('c8',), out=c8, in_=c8, func=AF.Ln, bias=vcol('one'))
        S.op('dve', 'tensor_scalar', ('c8',), ('c16',), out=c16, in0=c8, scalar1=-16.0, scalar2=None, op0=ALU.mult)
        S.op('dve', 'tensor_scalar', ('c8', 'c16'), ('c8',), out=c8, in0=c8, scalar1=-8.0, scalar2=None, op0=ALU.mult)
        if True:
            if final_pass:
                S.dma('sp', (), ('gath',), out=gath[:], in_=W("gsumm"))
                S.op('dve', 'memset', (), ('carry',), carry, 0.0)
                S.op('dve', 'memset', (), ('hin',), hin, 0.0)
                for r in range(NCORES):
                    S.op('dve', 'scalar_tensor_tensor', ('carry', 'cvec', 'hin'), ('hin',),
                         out=hin, in0=carry, scalar=cvec[:, r:r + 1], in1=hin, op0=ALU.mult, op1=ALU.add)
                    if r < NCORES - 1:
                        S.op('dve', 'tensor_tensor', ('carry', 'gath'), ('carry',), out=carry, in0=carry, in1=gath[:, r, RC:2 * RC], op=ALU.mult)
                        S.op('dve', 'tensor_tensor', ('carry', 'gath'), ('carry',), out=carry, in0=carry, in1=gath[:, r, 0:RC], op=ALU.add)
            for ct in range(RC):
                wi = rot('wb', NWB)
                wb_ = wbufs[wi]
                wrec = wb_[:, 0:1024].rearrange("p (k m) -> p k m", k=DC)
                wgat = wb_[:, 1024:2048].rearrange("p (k m) -> p k m", k=DC)
                wa = wb_[:, 2048:2176]
                wx = wb_[:, 2176:2304]
                wo = wb_[:, 2304:3328]
                wkey = ('wb', wi)
                S.dma('pool', (), (wkey,), out=wrec, in_=W("lru_w_in")[j][:, DR + ct * 128:DR + (ct + 1) * 128].rearrange("(k p) m -> p k m", p=128))
                S.dma('pool', (), (wkey,), out=wa, in_=W("lru_w_gate_a")[j][ct * 128:(ct + 1) * 128, :])
                S.dma('pool', (), (wkey,), out=wx, in_=W("lru_w_gate_x")[j][ct * 128:(ct + 1) * 128, :])
                if final_pass:
                    S.dma('pool', (), (wkey,), out=wgat, in_=W("lru_w_in")[j][:, ct * 128:(ct + 1) * 128].rearrange("(k p) m -> p k m", p=128))
                    S.dma('pool', (), (wkey,), out=wo, in_=W("lru_w_out")[j][ct * 128:(ct + 1) * 128, :])
                pi = rot('ps', 6)
                for c in range(DC):
                    MM(pi, psums[pi][:, :HALO], wrec[:, c, :], xn[:, c, 0:HALO], c == 0, c == DC - 1, (wkey, 'xnhalo'))
                S.op('dve', 'tensor_scalar', (('ps', pi), 'vecs', 'cvec'), ('recb',), out=recb[:, 0:HALO], in0=psums[pi][:, :HALO],
                     scalar1=V('br', ct), scalar2=cvec[:, 16:17], op0=ALU.add, op1=ALU.mult)
                for t in range(NT):
                    pi = rot('ps', 6)
                    for c in range(DC):
                        MM(pi, psums[pi][:, :TT], wrec[:, c, :], xn[:, c, xsl(t)], c == 0, c == DC - 1, (wkey, ('xn', c, t)))
                    S.op('act', 'activation', (('ps', pi), 'vecs'), ('recb',), out=recb[:, HALO + t * TT:HALO + (t + 1) * TT],
                         in_=psums[pi][:, :TT], func=AF.Identity, bias=V('br', ct))
                S.op('dve', 'tensor_scalar', ('recb', 'vecs'), ('cv',), out=cv, in0=recb[:, 1:1 + TL],
                     scalar1=V('cw0', ct), scalar2=V('cb', ct), op0=ALU.mult, op1=ALU.add)
                for k in range(1, 4):
                    S.op('dve', 'scalar_tensor_tensor', ('recb', 'vecs', 'cv'), ('cv',), out=cv, in0=recb[:, 1 + k:1 + k + TL],
                         scalar=V('cw%d' % k, ct), in1=cv, op0=ALU.mult, op1=ALU.add)
                S.op('act', 'activation', ('cv',), ('cvb',), out=cvb, in_=cv, func=AF.Copy)
                for t in range(NT):
                    pa, pb = rot('ps', 6), rot('ps', 6)
                    MM(pa, psums[pa][:, :TT], wa, cvb[:, tsl(t)], True, True, (wkey, 'cvb'))
                    MM(pb, psums[pb][:, :TT], wx, cvb[:, tsl(t)], True, True, (wkey, 'cvb'))
                    ri_, ii_, mi_ = rot('tmpA', 3), rot('tmpA', 3), rot('tmpA', 3)
                    S.op('act', 'activation', (('ps', pa), 'vecs'), (('tmpA', ri_),), out=tmpA[ri_][:], in_=psums[pa][:, :TT],
                         func=AF.Sigmoid, bias=V('ba', ct))
                    S.op('act', 'activation', (('ps', pb), 'vecs'), (('tmpA', ii_),), out=tmpA[ii_][:], in_=psums[pb][:, :TT],
                         func=AF.Sigmoid, bias=V('bx', ct))
                    S.op('dve', 'tensor_tensor', ('cv', ('tmpA', ii_)), ('cv',), out=cv[:, tsl(t)], in0=cv[:, tsl(t)], in1=tmpA[ii_][:], op=ALU.mult)
                    S.op('act', 'activation', (('tmpA', ri_), 'c8'), ('Abuf',), out=Abuf[:, tsl(t)], in_=tmpA[ri_][:],
                         func=AF.Exp, scale=c8[:, ct:ct + 1])
                    S.op('act', 'activation', (('tmpA', ri_), 'c16'), (('tmpA', mi_),), out=tmpA[mi_][:], in_=tmpA[ri_][:],
                         func=AF.Exp, scale=c16[:, ct:ct + 1])
                    if not final_pass:
                        S.op('dve', 'tensor_reduce', (('tmpA', ri_),), ('rsum',), out=rsum[:, t:t + 1], in_=tmpA[ri_][:], axis=mybir.AxisListType.X, op=ALU.add)
                    S.op('act', 'activation', (('tmpA', mi_), 'vecs'), (('tmpA', mi_),), out=tmpA[mi_][:], in_=tmpA[mi_][:],
                         func=AF.Sqrt, scale=-1.0, bias=vcol('one'))
                    S.op('dve', 'tensor_tensor', ('cv', ('tmpA', mi_)), ('cv',), out=cv[:, tsl(t)], in0=cv[:, tsl(t)], in1=tmpA[mi_][:], op=ALU.mult)
                S.op('dve', 'tensor_tensor_scan', ('Abuf', 'cv', 'hin'), ('Hl',), out=Hl, data0=Abuf, data1=cv,
                     initial=(hin[:, ct:ct + 1] if final_pass else 0.0), op0=ALU.mult, op1=ALU.add)
                if not final_pass:
                    S.op('dve', 'tensor_copy', ('Hl',), ('summ',), out=smal[:, ct:ct + 1], in_=Hl[:, TL - 1:TL])
                    S.op('dve', 'tensor_reduce', ('rsum',), ('rtot',), out=rtot, in_=rsum, axis=mybir.AxisListType.X, op=ALU.add)
                    S.op('act', 'activation', ('rtot', 'c8'), ('summ',), out=smal[:, RC + ct:RC + ct + 1], in_=rtot, func=AF.Exp, scale=c8[:, ct:ct + 1])
                    continue
                for t in range(NT):
                    pi = rot('ps', 6)
                    for c in range(DC):
                        MM(pi, psums[pi][:, :TT], wgat[:, c, :], xn[:, c, xsl(t)], c == 0, c == DC - 1, (wkey, ('xn', c, t)))
                    gi = rot('tmpA', 3)
                    S.op('act', 'activation', (('ps', pi), 'vecs'), (('tmpA', gi),), out=tmpA[gi][:], in_=psums[pi][:, :TT],
                         func=AF.Gelu_apprx_tanh, bias=V('bg', ct))
                    S.op('dve', 'tensor_tensor', ('Hl', ('tmpA', gi)), ('P1',), out=P1[:, tsl(t)], in0=Hl[:, tsl(t)], in1=tmpA[gi][:], op=ALU.mult)
                    for mo in range(DC):
                        pi = rot('ps', 6)
                        MM(pi, psums[pi][:, :TT], wo[:, mo * 128:(mo + 1) * 128], P1[:, tsl(t)], True, True, (wkey, 'P1'))
                        if ct == 0:
                            S.op('dve', 'scalar_tensor_tensor', (('ps', pi), ('h', mo, t), 'vecs'), (('h', mo, t),), out=hT[:, mo, tsl(t)],
                                 in0=psums[pi][:, :TT], scalar=V('bo', mo), in1=hT[:, mo, tsl(t)], op0=ALU.add, op1=ALU.add)
                        else:
                            S.op('dve', 'tensor_tensor', (('ps', pi), ('h', mo, t)), (('h', mo, t),), out=hT[:, mo, tsl(t)],
                                 in0=psums[pi][:, :TT], in1=hT[:, mo, tsl(t)], op=ALU.add)
        if not final_pass:
            S.dma('sp', ('summ',), ('o_summ',), out=OUT("summ", [128, 32]), in_=smal[:, :32])
            out_keys.append('o_summ')
        S.barrier()

    def rwkv(l, final_pass):
        j = l // 2
        RT = 128
        NRT = TL // RT
        V = lambda n, c=0: vcol('r%d_%s' % (j, n), c)
        rmsnorm_xn('norm_mix%d' % l)
        load_halo()
        S.barrier()
        off = [0]

        def A32(n):
            o = off[0]
            off[0] += 2 * n
            assert off[0] <= ARENA // 2, (off[0], ARENA // 2)
            return arena[:, o:o + 2 * n].bitcast(F32)

        def A16(n):
            o = off[0]
            off[0] += n
            assert off[0] <= ARENA // 2, (off[0], ARENA // 2)
            return arena[:, o:o + n]

        LW1, LA1, LV1, LG1a, LG1b = A16(TL), A16(TL), A16(TL), A16(TL), A16(TL)
        WS = A32(1280)
        rc = A32(640)
        rmask, bdones, cmask, I2 = rc[:, 0:320], rc[:, 320:448], rc[:, 448:576], rc[:, 576:640]
        omu = A32(48)
        (R, K, Vv, LW, ALR, CUM, WC, WINV, WPRV, KKN, T1, T2, KT, BT, BH, KH) = [A32(RT) for _ in range(16)]
        AR = A32(2 * RT).rearrange("p (a t) -> p a t", a=2)
        TM = [A32(512) for _ in range(2)]
        GM = [A32(320) for _ in range(2)]
        NL = [[A32(128) for _ in range(2)] for _ in range(2)]
        Tm = [[A32(64) for _ in range(2)] for _ in range(2)]
        ZA, Vp = [A32(128) for _ in range(2)], [A32(64) for _ in range(2)]
        Us = [A32(128) for _ in range(2)]
        Sa = [A32(128) for _ in range(2)]
        WCE = A32(2)
        Ytm, YN, O1, O2, CB, STt = A32(128), A32(128), A32(128), A32(128), A32(128), A32(16)
        G8 = [A32(256).rearrange("p (h n) -> p h n", h=2) for _ in range(2)]
        carry, Sin = [A32(64) for _ in range(2)], [A32(64) for _ in range(2)]
        SUMT = A32(256).rearrange("p (h n) -> p h n", h=2)
        OTb = A16(128)
        i64 = ident[0:64, 0:64]
        cnt = {'TM': 0, 'us': 0, 'g8': 0}

        S.dma('sp', (), ('rc',), out=rc, in_=W("rconst"))
        mu0 = VOFF['r%d_mu0' % j]
        S.op('dve', 'tensor_scalar', ('vecs',), ('omu',), out=omu, in0=vecs[:, mu0:mu0 + 48], scalar1=-1.0, scalar2=1.0, op0=ALU.mult, op1=ALU.add)

        def mixed_weights(src, m, mi, dst1, dst2, key):
            wsv = WS[:, :DC * m].rearrange("p (k m) -> p k m", k=DC)
            S.dma('sp', (), ('WS',), out=wsv, in_=src.rearrange("(k p) m -> p k m", p=128))
            for c in range(DC):
                S.op('dve', 'tensor_scalar', ('WS', 'omu'), (key,), out=dst1[:, c, :], in0=wsv[:, c, :], scalar1=omu[:, mi * 8 + c:mi * 8 + c + 1], scalar2=None, op0=ALU.mult)
                S.op('pool', 'tensor_scalar', ('WS', 'vecs'), (key,), out=dst2[:, c, :], in0=wsv[:, c, :], scalar1=vecs[:, mu0 + mi * 8 + c:mu0 + mi * 8 + c + 1], scalar2=None, op0=ALU.mult)

        def proj16(pi, out, w1v, w2v, m0, m1, cols0, n, wkey):
            for c in range(DC):
                MM(pi, out, w1v[:, c, m0:m1], xn[:, c, HALO + cols0:HALO + cols0 + n], c == 0, False, (wkey, ('xn', c, cols0 // TT), 'xnhalo'))
            for c in range(DC):
                MM(pi, out, w2v[:, c, m0:m1], xn[:, c, HALO + cols0 - 1:HALO + cols0 + n - 1], False, c == DC - 1,
                   (wkey, ('xn', c, cols0 // TT), ('xn', c, max(cols0 - 1, 0) // TT), 'xnhalo'))

        lora_list = [('rw_w1', 64, 1, [(0, 64, LW1, AF.Tanh)]), ('rw_a1', 64, 4, [(0, 64, LA1, AF.Copy)]),
                     ('rw_g1', 160, 5, [(0, 128, LG1a, AF.Sigmoid), (128, 160, LG1b, AF.Sigmoid)])]
        if j == 1:
            lora_list.append(('rw_v1', 32, 3, [(0, 32, LV1, AF.Copy)]))
        for (wn, m, mi, outs) in lora_list:
            wi = rot('wb', NWB)
            wkey = ('wb', wi)
            d1 = wbufs[wi][:, 0:DC * m].rearrange("p (k m) -> p k m", k=DC)
            d2 = wbufs[wi][:, DC * m:2 * DC * m].rearrange("p (k m) -> p k m", k=DC)
            src = W(wn)[0 if wn == 'rw_v1' else j]
            mixed_weights(src, m, mi, d1, d2, wkey)
            for t in range(NT):
                for (m0, m1, dst, fn) in outs:
                    pi = rot('ps', 6)
                    proj16(pi, psums[pi][0:m1 - m0, :TT], d1, d2, m0, m1, t * TT, TT, wkey)
                    S.op('act', 'activation', (('ps', pi),), ('lora',), out=dst[0:m1 - m0, tsl(t)], in_=psums[pi][0:m1 - m0, :TT], func=fn)

        for ct in range(DC):
            wa_i, wb_i = rot('wb', NWB), rot('wb', NWB)
            ka, kb = ('wb', wa_i), ('wb', wb_i)
            bufa, bufb = wbufs[wa_i], wbufs[wb_i]
            wv_ = lambda buf, o: buf[:, o:o + 1024].rearrange("p (k m) -> p k m", k=DC)
            r1, r2, k1, k2 = wv_(bufa, 0), wv_(bufa, 1024), wv_(bufa, 2048), wv_(bufa, 3072)
            v1_, v2_ = wv_(bufb, 0), wv_(bufb, 1024)
            w2c, a2c, v2c = bufb[0:64, 2048:2176], bufb[0:64, 2176:2304], bufb[0:32, 2304:2432]
            g2a, g2b = bufb[:, 2432:2560], bufb[0:32, 2560:2688]
            wo = bufb[:, 2688:3712]
            cs_ = slice(ct * 128, (ct + 1) * 128)
            mixed_weights(W("rw_w_rkv")[j, 0][:, cs_], 128, 0, r1, r2, ka)
            mixed_weights(W("rw_w_rkv")[j, 1][:, cs_], 128, 2, k1, k2, ka)
            mixed_weights(W("rw_w_rkv")[j, 2][:, cs_], 128, 3, v1_, v2_, kb)
            S.dma('pool', (), (kb,), out=w2c, in_=W("rw_w2")[j][:, cs_])
            S.dma('pool', (), (kb,), out=a2c, in_=W("rw_a2")[j][:, cs_])
            if j == 1:
                S.dma('pool', (), (kb,), out=v2c, in_=W("rw_v2")[0][:, cs_])
            if final_pass:
                S.dma('pool', (), (kb,), out=g2a, in_=W("rw_g2")[j][0:128, cs_])
                S.dma('pool', (), (kb,), out=g2b, in_=W("rw_g2")[j][128:160, cs_])
                S.dma('pool', (), (kb,), out=wo, in_=W("rw_w_o")[j][cs_, :])
            S.op('dve', 'memset', (), (('Sa', 0, 0), ('Sa', 0, 1)), Sa[0], 0.0)
            if not final_pass:
                S.op('dve', 'tensor_copy', ('rc', ('Sa', 0, 0), ('Sa', 0, 1)), (('Sa', 0, 0), ('Sa', 0, 1)), out=Sa[0][:, 64:128], in_=I2)
            else:
                for h in range(2):
                    S.op('dve', 'memset', (), (('carry', h),), carry[h][0:64, :], 0.0)
                    S.op('dve', 'memset', (), (('Sin', h),), Sin[h][0:64, :], 0.0)
                for r in range(NCORES):
                    if r < NCORES - 1:
                        gi = cnt['g8'] % 2
                        cnt['g8'] += 1
                        S.dma('sp', (), (('G8', gi),), out=G8[gi][0:64, :, :], in_=W("grsumm")[r, 2 * ct:2 * ct + 2].rearrange("h k n -> k h n"))
                    for h in range(2):
                        S.op('dve', 'scalar_tensor_tensor', (('carry', h), ('Sin', h), 'cvec'), (('Sin', h),), out=Sin[h][0:64, :],
                             in0=carry[h][0:64, :], scalar=cvec[0:64, r:r + 1], in1=Sin[h][0:64, :], op0=ALU.mult, op1=ALU.add)
                        if r < NCORES - 1:
                            pi = rot('ps', 6)
                            MM(pi, psums[pi][0:64, 0:64], G8[gi][0:64, h, 64:128], carry[h][0:64, :], True, True, (('G8', gi), ('carry', h)))
                            S.op('dve', 'tensor_tensor', (('ps', pi), ('G8', gi)), (('carry', h),), out=carry[h][0:64, :],
                                 in0=psums[pi][0:64, 0:64], in1=G8[gi][0:64, h, 0:64], op=ALU.add)
                for h in range(2):
                    S.op('act', 'activation', (('Sin', h), ('Sa', 0, h)), (('Sa', 0, h),), out=Sa[0][h * 64:(h + 1) * 64, 0:64], in_=Sin[h][0:64, :], func=AF.Copy)
            par = [0]
            NA = 64 if final_pass else 128
            for t in range(NRT):
                c0 = t * RT
                ht = c0 // TT
                tcols = slice(c0, c0 + RT)
                for (w1v, w2v, dst, wkey) in ((r1, r2, R, ka), (k1, k2, K, ka), (v1_, v2_, Vv, kb)):
                    pi = rot('ps', 6)
                    proj16(pi, psums[pi][:, :RT], w1v, w2v, 0, 128, c0, RT, wkey)
                    S.op('act', 'activation', (('ps', pi),), (('fm', id(dst)),), out=dst, in_=psums[pi][:, :RT], func=AF.Copy)
                pi = rot('ps', 6)
                MM(pi, psums[pi][:, :RT], w2c, LW1[0:64, tcols], True, True, (kb, 'lora'))
                S.op('act', 'activation', (('ps', pi), 'vecs'), ('LW',), out=LW, in_=psums[pi][:, :RT], func=AF.Sigmoid, bias=V('w0', ct))
                S.op('dve', 'tensor_scalar', ('LW',), ('LW',), out=LW, in0=LW, scalar1=-0.6065306597126334, scalar2=None, op0=ALU.mult)
                pi = rot('ps', 6)
                MM(pi, psums[pi][:, :RT], a2c, LA1[0:64, tcols], True, True, (kb, 'lora'))
                S.op('act', 'activation', (('ps', pi), 'vecs'), ('ALR',), out=ALR, in_=psums[pi][:, :RT], func=AF.Sigmoid, bias=V('a0', ct))
                fmV = ('fm', id(Vv))
                fmK = ('fm', id(K))
                fmR = ('fm', id(R))
                if j == 1:
                    pi = rot('ps', 6)
                    MM(pi, psums[pi][:, :RT], v2c, LV1[0:32, tcols], True, True, (kb, 'lora'))
                    S.op('act', 'activation', (('ps', pi), 'vecs'), ('T2',), out=T2, in_=psums[pi][:, :RT], func=AF.Sigmoid, bias=V('v0', ct))
                    S.dma('sp', (), ('T1',), out=T1, in_=W("vfirst")[:, ct, tcols])
                    S.op('dve', 'tensor_tensor', ('T1', fmV), ('T1',), out=T1, in0=T1, in1=Vv, op=ALU.subtract)
                    S.op('dve', 'tensor_tensor', ('T1', 'T2'), ('T1',), out=T1, in0=T1, in1=T2, op=ALU.mult)
                    S.op('dve', 'tensor_tensor', ('T1', fmV), (fmV,), out=Vv, in0=Vv, in1=T1, op=ALU.add)
                elif final_pass:
                    S.dma('sp', (fmV,), (('o_vf', ct, t),), out=OUT("vfirst_out", [128, DC, TL])[:, ct, tcols], in_=Vv)
                    out_keys.append(('o_vf', ct, t))
                S.op('dve', 'tensor_scalar', (fmK, 'vecs'), ('KKN',), out=KKN, in0=K, scalar1=V('kk', ct), scalar2=None, op0=ALU.mult)
                S.op('dve', 'tensor_tensor', ('KKN',), ('T1',), out=T1, in0=KKN, in1=KKN, op=ALU.mult)
                pi = rot('ps', 6)
                MM(pi, psums[pi][:, :RT], bdones, T1, True, True, ('rc', 'T1'))
                S.op('act', 'activation', (('ps', pi),), ('T1',), out=T1, in_=psums[pi][:, :RT], func=AF.Sqrt)
                S.op('dve', 'tensor_scalar', ('T1',), ('T1',), out=T1, in0=T1, scalar1=1e-12, scalar2=None, op0=ALU.max)
                S.op('dve', 'reciprocal', ('T1',), ('T1',), out=T1, in_=T1)
                S.op('dve', 'tensor_tensor', ('KKN', 'T1'), ('KKN',), out=KKN, in0=KKN, in1=T1, op=ALU.mult)
                S.op('dve', 'tensor_scalar', ('ALR', 'vecs'), ('T1',), out=T1, in0=ALR, scalar1=-1.0, scalar2=V('ka', ct), op0=ALU.add, op1=ALU.mult)
                S.op('dve', 'scalar_tensor_tensor', ('T1', fmK), (fmK,), out=K, in0=T1, scalar=1.0, in1=K, op0=ALU.add, op1=ALU.mult)
                S.op('dve', 'tensor_tensor_scan', ('rc', 'LW'), ('CUM',), out=CUM, data0=cmask, data1=LW, initial=0.0, op0=ALU.mult, op1=ALU.add)
                S.op('act', 'activation', ('CUM',), ('WC',), out=WC, in_=CUM, func=AF.Exp)
                S.op('act', 'activation', ('CUM',), ('WINV',), out=WINV, in_=CUM, func=AF.Exp, scale=-1.0)
                S.op('dve', 'tensor_tensor', ('CUM', 'LW'), ('T1',), out=T1, in0=CUM, in1=LW, op=ALU.subtract)
                S.op('act', 'activation', ('T1',), ('WPRV',), out=WPRV, in_=T1, func=AF.Exp)
                S.op('dve', 'tensor_copy', ('WC',), ('WCE',), out=WCE, in_=WC.rearrange("p (c k) -> p c k", k=64)[:, :, 63])
                S.op('dve', 'scalar_tensor_tensor', ('KKN', 'WPRV'), ('AR0',), out=AR[:, 0, :], in0=KKN, scalar=-1.0, in1=WPRV, op0=ALU.mult, op1=ALU.mult)
                S.op('dve', 'tensor_tensor', (fmR, 'WC'), ('AR1',), out=AR[:, 1, :], in0=R, in1=WC, op=ALU.mult)
                S.op('dve', 'tensor_tensor', (fmK, 'WINV'), ('KT',), out=KT, in0=K, in1=WINV, op=ALU.mult)
                S.op('dve', 'tensor_tensor', ('KKN', 'ALR'), ('T1',), out=T1, in0=KKN, in1=ALR, op=ALU.mult)
                S.op('dve', 'tensor_tensor', ('T1', 'WINV'), ('BT',), out=BT, in0=T1, in1=WINV, op=ALU.mult)
                for c in range(2):
                    cs = slice(c * 64, (c + 1) * 64)
                    S.op('dve', 'tensor_scalar', ('BT', 'WCE'), ('BH',), out=BH[:, cs], in0=BT[:, cs], scalar1=WCE[:, c:c + 1], scalar2=None, op0=ALU.mult)
                    S.op('pool', 'tensor_scalar', ('KT', 'WCE'), ('KH',), out=KH[:, cs], in0=KT[:, cs], scalar1=WCE[:, c:c + 1], scalar2=None, op0=ALU.mult)
                if final_pass:
                    S.op('dve', 'tensor_tensor', (fmR, fmK), ('T2',), out=T2, in0=R, in1=K, op=ALU.mult)
                    S.op('dve', 'tensor_scalar', ('T2', 'vecs'), ('T2',), out=T2, in0=T2, scalar1=V('rk', ct), scalar2=None, op0=ALU.mult)
                    pi = rot('ps', 6)
                    MM(pi, psums[pi][:, :RT], bdones, T2, True, True, ('rc', 'T2'))
                    S.op('act', 'activation', (('ps', pi),), ('CB',), out=CB, in_=psums[pi][:, :RT], func=AF.Copy)
                    py = [6, 7]
                hb = [slice(0, 64), slice(64, 128)]
                csl = [slice(0, 64), slice(64, 128)]
                tms = []
                for c in range(2):
                    cs = csl[c]
                    pi = rot('ps', 6)
                    for q, (src, sk_) in enumerate(((Vv[:, cs], fmV), (AR[:, 0, cs], 'AR0'), (BH[:, cs], 'BH'), (KH[:, cs], 'KH'))):
                        TR(pi, psums[pi][0:64, q * 128:(q + 1) * 128], src, (sk_,))
                    tmi = cnt['TM'] % 2
                    cnt['TM'] += 1
                    tm = TM[tmi]
                    S.op('act', 'activation', (('ps', pi),), (('TM', tmi),), out=tm[0:64, :], in_=psums[pi][0:64, :], func=AF.Copy)
                    S.op('dve', 'tensor_copy', (('ps', pi),), (('TM', tmi),), out=tm[64:128, :], in_=psums[pi][0:64, :])
                    tms.append((tm, ('TM', tmi)))
                for c in range(2):
                    cs = csl[c]
                    for h in range(2):
                        bh = hb[h]
                        pi = rot('ps', 6)
                        MM(pi, psums[pi][bh, 0:128], BT[bh, cs], AR[bh, :, cs], True, True, ('BT', 'AR0', 'AR1'))
                        MM(pi, psums[pi][bh, 128:256], KT[bh, cs], AR[bh, :, cs], True, True, ('KT', 'AR0', 'AR1'))
                        MM(pi, psums[pi][bh, 256:320], AR[bh, 0, cs], BT[bh, cs], True, True, ('BT', 'AR0'))
                        S.op('dve', 'tensor_tensor', (('ps', pi), 'rc'), (('GM', c, h),), out=GM[c][bh, :], in0=psums[pi][bh, 0:320], in1=rmask[bh, :], op=ALU.mult)
                for c in range(2):
                    for h in range(2):
                        bh = hb[h]
                        S.op('dve', 'tensor_tensor', (('GM', c, h), 'rc'), (('Tm', c, 0, h),), out=Tm[c][0][bh, :], in0=GM[c][bh, 0:64], in1=I2[bh, :], op=ALU.add)
                Np = [GM[c][:, 0:64] for c in range(2)]
                Lp = [GM[c][:, 256:320] for c in range(2)]
                nlk = [('GM', c) for c in range(2)]
                for lev in range(1, 6):
                    nli = lev % 2
                    lo = 0 if lev < 5 else 64
                    for c in range(2):
                        for h in range(2):
                            bh = hb[h]
                            pi = rot('ps', 6)
                            if lev < 5:
                                MM(pi, psums[pi][bh, 0:64], Lp[c][bh, :], Np[c][bh, :], True, True, (nlk[c] + (h,),))
                            MM(pi, psums[pi][bh, 64:128], Np[c][bh, :], Lp[c][bh, :], True, True, (nlk[c] + (h,),))
                            S.op('act', 'activation', (('ps', pi),), (('NL', c, nli, h),), out=NL[c][nli][bh, lo:128], in_=psums[pi][bh, lo:128], func=AF.Copy)
                    for c in range(2):
                        Np[c], Lp[c], nlk[c] = NL[c][nli][:, 0:64], NL[c][nli][:, 64:128], ('NL', c, nli)
                    tp, tn = (lev - 1) % 2, lev % 2
                    for c in range(2):
                        for h in range(2):
                            bh = hb[h]
                            pi = rot('ps', 6)
                            MM(pi, psums[pi][bh, 0:64], Lp[c][bh, :], Tm[c][tp][bh, :], True, True, (nlk[c] + (h,), ('Tm', c, tp, h)))
                            S.op('dve', 'tensor_tensor', (('ps', pi), ('Tm', c, tp, h)), (('Tm', c, tn, h),), out=Tm[c][tn][bh, :], in0=psums[pi][bh, 0:64], in1=Tm[c][tp][bh, :], op=ALU.add)
                for c in range(2):
                    tm, tmk = tms[c]
                    for h in range(2):
                        bh = hb[h]
                        pi = rot('ps', 6)
                        MM(pi, psums[pi][bh, 0:64], GM[c][bh, 128:192], tm[bh, h * 64:(h + 1) * 64], True, True, (('GM', c, h), tmk))
                        MM(pi, psums[pi][bh, 64:128], tm[bh, 128 + h * 64:128 + (h + 1) * 64], Tm[c][1][bh, :], True, True, (tmk, ('Tm', c, 1, h)))
                        S.op('act', 'activation', (('ps', pi),), (('ZA', c, h),), out=ZA[c][bh, :], in_=psums[pi][bh, 0:128], func=AF.Copy)
                for c in range(2):
                    for h in range(2):
                        bh = hb[h]
                        pi = rot('ps', 6)
                        MM(pi, psums[pi][bh, 0:64], Tm[c][1][bh, :], ZA[c][bh, 0:64], True, True, (('Tm', c, 1, h), ('ZA', c, h)))
                        S.op('act', 'activation', (('ps', pi),), (('Vp', c, h),), out=Vp[c][bh, :], in_=psums[pi][bh, 0:64], func=AF.Copy)
                for c in range(2):
                    cs = csl[c]
                    tm, tmk = tms[c]
                    cur, nxt = par[0], 1 - par[0]
                    par[0] = nxt
                    sc, sn = Sa[cur], Sa[nxt]
                    ui = cnt['us'] % 2
                    cnt['us'] += 1
                    us = Us[ui]
                    for h in range(2):
                        bh = hb[h]
                        pi = rot('ps', 6)
                        MM(pi, psums[pi][bh, 0:NA], ZA[c][bh, 64:128], sc[bh, 0:NA], True, True, (('ZA', c, h), ('Sa', cur, h)))
                        S.op('dve', 'tensor_tensor', (('ps', pi), ('Vp', c, h)), (('Us', ui, h),), out=us[bh, 0:64], in0=psums[pi][bh, 0:64], in1=Vp[c][bh, :], op=ALU.add)
                        if not final_pass:
                            S.op('act', 'activation', (('ps', pi),), (('Us', ui, h),), out=us[bh, 64:128], in_=psums[pi][bh, 64:128], func=AF.Copy)
                    if final_pass:
                        for h in range(2):
                            bh = hb[h]
                            yo = psums[py[h]][c * 64:(c + 1) * 64, 0:64]
                            MM(py[h], yo, AR[bh, 1, cs], sc[bh, 0:64], True, False, ('AR1', ('Sa', cur, h)))
                            MM(py[h], yo, GM[c][bh, 64:128], us[bh, 0:64], False, False, (('GM', c, h), ('Us', ui, h)))
                            MM(py[h], yo, GM[c][bh, 192:256], tm[bh, h * 64:(h + 1) * 64], False, True, (('GM', c, h), tmk))
                    for h in range(2):
                        bh = hb[h]
                        pi = rot('ps', 6)
                        MM(pi, psums[pi][bh, 0:NA], tm[bh, 256 + h * 64:256 + (h + 1) * 64], us[bh, 0:NA], True, False, (tmk, ('Us', ui, h)))
                        MM(pi, psums[pi][bh, 0:64], tm[bh, 384 + h * 64:384 + (h + 1) * 64], tm[bh, h * 64:(h + 1) * 64], False, True, (tmk,))
                        S.op('dve', 'scalar_tensor_tensor', (('ps', pi), ('Sa', cur, h), 'WCE'), (('Sa', nxt, h),), out=sn[bh, 0:NA],
                             in0=sc[bh, 0:NA], scalar=WCE[bh, c:c + 1], in1=psums[pi][bh, 0:NA], op0=ALU.mult, op1=ALU.add)
                if not final_pass:
                    continue
                for h in range(2):
                    S.op('act', 'activation', (('ps', py[h]),), ('Ytm',), out=Ytm[:, h * 64:(h + 1) * 64], in_=psums[py[h]][:, 0:64], func=AF.Copy)
                for h in range(2):
                    hs = slice(h * 64, (h + 1) * 64)
                    S.op('dve', 'tensor_reduce', ('Ytm',), ('STa',), out=STt[:, h:h + 1], in_=Ytm[:, hs], axis=mybir.AxisListType.X, op=ALU.add)
                    S.op('act', 'activation', ('Ytm',), ('O2', 'STb'), out=O2[:, hs], in_=Ytm[:, hs], func=AF.Square, accum_out=STt[:, 2 + h:3 + h])
                S.op('dve', 'tensor_scalar', ('STa',), ('STa',), out=STt[:, 0:2], in0=STt[:, 0:2], scalar1=1.0 / 64, scalar2=None, op0=ALU.mult)
                S.op('dve', 'tensor_tensor', ('STa',), ('STc',), out=STt[:, 4:6], in0=STt[:, 0:2], in1=STt[:, 0:2], op=ALU.mult)
                S.op('dve', 'scalar_tensor_tensor', ('STb', 'STc'), ('STb',), out=STt[:, 2:4], in0=STt[:, 2:4], scalar=1.0 / 64, in1=STt[:, 4:6], op0=ALU.mult, op1=ALU.subtract)
                S.op('act', 'activation', ('STb', 'vecs'), ('STb',), out=STt[:, 2:4], in_=STt[:, 2:4], func=AF.Sqrt, bias=vcol('gneps'))
                S.op('dve', 'reciprocal', ('STb',), ('STb',), out=STt[:, 2:4], in_=STt[:, 2:4])
                S.op('dve', 'scalar_tensor_tensor', ('STa', 'STb'), ('STc',), out=STt[:, 4:6], in0=STt[:, 0:2], scalar=-1.0, in1=STt[:, 2:4], op0=ALU.mult, op1=ALU.mult)
                for h in range(2):
                    hs = slice(h * 64, (h + 1) * 64)
                    S.op('act', 'activation', ('Ytm', 'STb', 'STc'), ('YN',), out=YN[:, hs], in_=Ytm[:, hs], func=AF.Identity,
                         scale=STt[:, 2 + h:3 + h], bias=STt[:, 4 + h:5 + h])
                pi = rot('ps', 6)
                TR(pi, psums[pi][:, 0:128], YN, ('YN',))
                S.op('dve', 'tensor_scalar', (('ps', pi), 'vecs'), ('O1',), out=O1, in0=psums[pi][:, 0:128], scalar1=V('lnw', ct), scalar2=V('lnb', ct), op0=ALU.mult, op1=ALU.add)
                S.op('dve', 'tensor_tensor', ('CB', fmV), ('O2',), out=O2, in0=CB, in1=Vv, op=ALU.mult)
                S.op('dve', 'tensor_tensor', ('O1', 'O2'), ('O1',), out=O1, in0=O1, in1=O2, op=ALU.add)
                pi = rot('ps', 6)
                MM(pi, psums[pi][:, 0:RT], g2a, LG1a[:, tcols], True, False, (kb, 'lora'))
                MM(pi, psums[pi][:, 0:RT], g2b, LG1b[0:32, tcols], False, True, (kb, 'lora'))
                S.op('dve', 'tensor_tensor', (('ps', pi), 'O1'), ('OTb',), out=OTb, in0=psums[pi][:, 0:RT], in1=O1, op=ALU.mult)
                for half in range(2):
                    pi = rot('ps', 6)
                    for q in range(4):
                        mo = half * 4 + q
                        MM(pi, psums[pi][:, q * 128:(q + 1) * 128], wo[:, mo * 128:(mo + 1) * 128], OTb, True, True, (kb, 'OTb'))
                    hk = tuple(('h', half * 4 + q, ht) for q in range(4))
                    S.op('dve', 'tensor_tensor', (('ps', pi),) + hk, hk, out=hT[:, half * 4:(half + 1) * 4, tcols],
                         in0=psums[pi][:].rearrange("p (q t) -> p q t", q=4), in1=hT[:, half * 4:(half + 1) * 4, tcols], op=ALU.add)
            if not final_pass:
                sc = Sa[par[0]]
                pi = rot('ps', 6)
                for h in range(2):
                    TR(pi, psums[pi][0:64, h * 64:(h + 1) * 64], sc[h * 64:(h + 1) * 64, 64:128], (('Sa', par[0], h),), b0=h * 64)
                    S.op('act', 'activation', (('Sa', par[0], h), 'SUMT'), ('SUMT',), out=SUMT[0:64, h, 0:64], in_=sc[h * 64:(h + 1) * 64, 0:64], func=AF.Copy)
                S.op('dve', 'tensor_copy', (('ps', pi), 'SUMT'), ('SUMT',), out=SUMT[0:64, :, 64:128], in_=psums[pi][0:64, 0:128].rearrange("p (h n) -> p h n", h=2))
                S.dma('sp', ('SUMT',), (('o_rs', ct),), out=OUT("rsumm", [16, 64, 128])[2 * ct:2 * ct + 2].rearrange("h k n -> k h n"), in_=SUMT[0:64, :, :])
                out_keys.append(('o_rs', ct))
        S.barrier()

    def final():
        xo = big[:, :DC * TT * 2].bitcast(F32).rearrange("p (c t) -> p c t", c=DC)
        outs = []
        for t in range(NT):
            ri = rms_tile(t)
            for c in range(DC):
                S.op('dve', 'scalar_tensor_tensor', (('h', c, t), ('rs', ri), 'vecs'), (('xo', c),),
                     out=xo[:, c, :], in0=hT[:, c, tsl(t)], scalar=vcol('norm_final', c),
                     in1=rs[ri][:], op0=ALU.mult, op1=ALU.mult)
            for b in range(TT // 128):
                si = rot('stage', 2)
                for half in range(2):
                    pi = rot('ps', 6)
                    for j in range(4):
                        c = half * 4 + j
                        TR(pi, psums[pi][:, j * 128:(j + 1) * 128], xo[:, c, b * 128:(b + 1) * 128], (('xo', c),))
                    S.op('act', 'activation', (('ps', pi),), (('stage', si),),
                         out=stage[si][:, half * 512:(half + 1) * 512], in_=psums[pi][:], func=AF.Copy)
                r0 = t * TT + b * 128
                key = ('yout', r0)
                S.dma('sp', (('stage', si),), (key,), out=OUT("y", [TL, D])[r0:r0 + 128, :], in_=stage[si][:])
                out_keys.append(key)

    def load_h():
        for c in range(DC):
            S.dma('sp', (), tuple(('h', c, t) for t in range(NT)), out=hT[:, c, :], in_=W("h_in")[:, c, :])

    def store_h():
        for c in range(DC):
            S.dma('sp', tuple(('h', c, t) for t in range(NT)), (('o_h', c),), out=OUT("h_out", [128, DC, TL])[:, c, :], in_=hT[:, c, :])
            out_keys.append(('o_h', c))

    if first:
        load_xT()
    else:
        load_h()
    for step in prog:
        kind = step[0]
        if kind == 'tail':
            rmsnorm_xn('norm_mix%d' % step[1])
            tail_out()
        elif kind in ('mix1', 'mix2'):
            l = step[1]
            if l % 2 == 0:
                hawk(l, kind == 'mix2')
            else:
                rwkv(l, kind == 'mix2')
        elif kind == 'ffn':
            ffn(step[1])
        elif kind == 'ple':
            ple(step[1])
        elif kind == 'final':
            final()
        elif kind == 'store_h':
            store_h()
    S.finish('sp', out_keys)
    S.emit(nc, stack)
    stack.close()
    nc._declared_inputs = set(declared)
    nc._declared_outputs = set(outs_decl)
    return nc


VOFF = {}
NVEC = 0


def _layout_vecs():
    global NVEC
    o = 0

    def add(name, n):
        nonlocal o
        VOFF[name] = o
        o += n
    add('eps', 1)
    for l in range(4):
        add('norm_mix%d' % l, 8)
        add('norm_ffn%d' % l, 8)
        add('norm_ple%d' % l, 8)
    add('norm_final', 8)
    add('one', 1)
    for j in range(2):
        for n in ('bg', 'br', 'cw0', 'cw1', 'cw2', 'cw3', 'cb', 'ba', 'bx', 'lam'):
            add('h%d_%s' % (j, n), 10)
        add('h%d_bo' % j, 8)
    add('gneps', 1)
    for j in range(2):
        for i in range(6):
            add('r%d_mu%d' % (j, i), 8)
        for n in ('w0', 'a0', 'v0', 'kk', 'ka', 'rk', 'lnw', 'lnb'):
            add('r%d_%s' % (j, n), 8)
    NVEC = o


_layout_vecs()


def colmajor(v):
    v = np.asarray(v, np.float32).reshape(-1)
    return np.ascontiguousarray(v.reshape(-1, 128).T)


def pack_vecs(inp):
    vecs = np.zeros((128, NVEC), np.float32)

    def put(name, v):
        cm = colmajor(v)
        vecs[:, VOFF[name]:VOFF[name] + cm.shape[1]] = cm
    vecs[:, VOFF['eps']] = EPS
    for l in range(4):
        put('norm_mix%d' % l, inp['norm_mix'][l])
        put('norm_ffn%d' % l, inp['norm_ffn'][l])
        put('norm_ple%d' % l, inp['norm_ple'][l])
    put('norm_final', inp['norm_final'])
    vecs[:, VOFF['one']] = 1.0
    for j in range(2):
        put('h%d_bg' % j, inp['lru_b_in'][j][:DR])
        put('h%d_br' % j, inp['lru_b_in'][j][DR:])
        for k in range(4):
            put('h%d_cw%d' % (j, k), inp['lru_conv_w'][j][k])
        put('h%d_cb' % j, inp['lru_conv_b'][j])
        put('h%d_ba' % j, inp['lru_b_gate_a'][j])
        put('h%d_bx' % j, inp['lru_b_gate_x'][j])
        put('h%d_lam' % j, inp['lru_lambda'][j])
        put('h%d_bo' % j, inp['lru_b_out'][j])
    vecs[:, VOFF['gneps']] = 64e-5
    for j in range(2):
        for i in range(6):
            put('r%d_mu%d' % (j, i), inp['rw_mu'][j][i])
        put('r%d_w0' % j, inp['rw_w0'][j])
        put('r%d_a0' % j, inp['rw_a0'][j])
        if j == 1:
            put('r%d_v0' % j, inp['rw_v0'][0])
        put('r%d_kk' % j, inp['rw_k_k'][j])
        put('r%d_ka' % j, inp['rw_k_a'][j])
        put('r%d_rk' % j, np.asarray(inp['rw_r_k'][j]).reshape(-1))
        put('r%d_lnw' % j, inp['rw_ln_w'][j])
        put('r%d_lnb' % j, inp['rw_ln_b'][j])
    return vecs


def launch(TL, prog, first, per_core, shared):
    nc = build(TL, prog, first=first)
    in_maps = []
    for c in range(NCORES):
        m = dict(shared)
        m.update(per_core[c])
        in_maps.append({k: v for k, v in m.items() if k in nc._declared_inputs})
        missing = nc._declared_inputs - set(in_maps[-1])
        assert not missing, missing
    res = run_bass_kernel_spmd(nc, in_maps, core_ids=list(range(NCORES)))
    return res.results


def run(inp, TL, layers=(0, 1, 2, 3), parts=('mix', 'ffn', 'ple')):
    f32 = lambda a: np.ascontiguousarray(np.asarray(a, np.float32))
    x = f32(inp['x']).reshape(-1, D)
    p = f32(inp['p']).reshape(4, -1, DPLE)
    shared = {
        'vecs': pack_vecs(inp), 'ident_in': np.eye(128, dtype=np.float32),
        'ffn_w_in': f32(inp['ffn_w_in']), 'ffn_w_out': f32(inp['ffn_w_out']),
        'ple_w_proj': f32(inp['ple_w_proj']), 'ple_w_gate': f32(inp['ple_w_gate']),
        'lru_w_in': f32(inp['lru_w_in']), 'lru_w_out': f32(inp['lru_w_out']),
        'lru_w_gate_a': f32(inp['lru_w_gate_a']).reshape(2, RC * 128, 128),
        'lru_w_gate_x': f32(inp['lru_w_gate_x']).reshape(2, RC * 128, 128),
    }
    shared.update(rwkv_shared(inp))
    per_core = []
    for c in range(NCORES):
        cv = np.zeros((128, 17), np.float32)
        cv[:, c] = 1.0
        if c > 0:
            cv[:, 8 + c - 1] = 1.0
            cv[:, 16] = 1.0
        per_core.append({'x': np.ascontiguousarray(x[c * TL:(c + 1) * TL]),
                         'p': np.ascontiguousarray(p[:, c * TL:(c + 1) * TL]), 'cvec': cv})

    def route_halo(res):
        for c in range(NCORES):
            per_core[c]['halo'] = res[c - 1]['tail'] if c > 0 else np.zeros((128, 32), np.float32)

    def take_h(res):
        for c in range(NCORES):
            per_core[c]['h_in'] = res[c]['h_out']

    layers = list(layers)
    mix = 'mix' in parts
    tailsteps = lambda l: ([('tail', l)] if mix else [])
    post = lambda l: ([('ffn', l)] if 'ffn' in parts else []) + ([('ple', l)] if 'ple' in parts else [])
    if not layers:
        res = launch(TL, [('final',)], True, per_core, shared)
    else:
        res = launch(TL, tailsteps(layers[0]) + [('store_h',)], True, per_core, shared)
        take_h(res)
        for li, l in enumerate(layers):
            if mix:
                route_halo(res)
                res1 = launch(TL, [('mix1', l)], False, per_core, shared)
                gather_summ(l, res1, per_core)
            nxt = (tailsteps(layers[li + 1]) + [('store_h',)]) if li + 1 < len(layers) else [('final',)]
            res = launch(TL, ([('mix2', l)] if mix else []) + post(l) + nxt, False, per_core, shared)
            if li + 1 < len(layers):
                take_h(res)
                keep_state(l, res, per_core)
    y = np.concatenate([res[c]['y'] for c in range(NCORES)], axis=0)
    return y.reshape(1, NCORES * TL, D).astype(np.float32)


def gather_summ(l, res1, per_core):
    if l % 2 == 0:
        g = np.ascontiguousarray(np.stack([res1[c]['summ'] for c in range(NCORES)], axis=1))
        for c in range(NCORES):
            per_core[c]['gsumm'] = g
    else:
        g = np.ascontiguousarray(np.stack([res1[c]['rsumm'] for c in range(NCORES)], axis=0))
        for c in range(NCORES):
            per_core[c]['grsumm'] = g


def keep_state(l, res, per_core):
    if l == 1:
        for c in range(NCORES):
            per_core[c]['vfirst'] = res[c]['vfirst_out']


def rwkv_shared(inp):
    f32 = lambda a: np.ascontiguousarray(np.asarray(a, np.float32))
    rc = np.zeros((128, 640), np.float32)
    i = np.arange(64)
    su = (i[:, None] < i[None, :]).astype(np.float32)
    iu = (i[:, None] <= i[None, :]).astype(np.float32)
    sl = (i[:, None] > i[None, :]).astype(np.float32)
    rc[0:64, 0:320] = np.concatenate([su, iu, su, iu, sl], axis=1)
    rc[64:128, 0:320] = rc[0:64, 0:320]
    rc[0:64, 576:640] = np.eye(64, dtype=np.float32)
    rc[64:128, 576:640] = np.eye(64, dtype=np.float32)
    rc[0:64, 320:384] = 1.0
    rc[64:128, 384:448] = 1.0
    rc[:, 448:576] = 1.0
    rc[:, 448] = 0.0
    rc[:, 448 + 64] = 0.0
    d = {k: f32(inp[k]) for k in ('rw_w_rkv', 'rw_w_o', 'rw_w1', 'rw_w2', 'rw_a1', 'rw_a2', 'rw_v1', 'rw_v2', 'rw_g1', 'rw_g2')}
    d['rconst'] = rc
    return d


def kernel(**inputs):
    return run(inputs, 2048)
```

```python
import contextlib
import numpy as np
import concourse.bass as bass
import concourse.mybir as mybir
from concourse.bass_utils import run_bass_kernel_spmd

F32 = mybir.dt.float32
BF16 = mybir.dt.bfloat16
AF = mybir.ActivationFunctionType
ALU = mybir.AluOpType
NCORES = 8
D, DC, DR, RC, DFF, FC, DPLE = 1024, 8, 1280, 10, 2816, 22, 256
HALO = 4
EPS = 1e-6


class Sched:
    ENGS = ('pe', 'act', 'dve', 'pool', 'sp')

    def __init__(self):
        self.ops = {e: [] for e in self.ENGS}
        self.cnt = {e: 0 for e in self.ENGS}
        self.waited = {e: {} for e in self.ENGS}
        self.lastw = {}
        self.readers = {}
        self.NS = 8
        self.dma_uses = {}
        self.dma_n = {q: 0 for q in self.ENGS}
        self.ncc = 0
        self.semkeys = set()

    def _deps(self, reads, writes):
        deps = []
        for k in reads:
            t = self.lastw.get(k)
            if t:
                deps.append(t)
        for k in writes:
            t = self.lastw.get(k)
            if t:
                deps.append(t)
            deps.extend(self.readers.get(k, {}).items())
        return deps

    def _waits(self, eng, deps):
        w = {}
        for (sk, v) in deps:
            if eng == 'pe' and sk == ('eng', 'pe'):
                continue
            if v > self.waited[eng].get(sk, 0) and v > w.get(sk, 0):
                w[sk] = v
        for sk, v in w.items():
            self.waited[eng][sk] = v
        return list(w.items())

    def _commit(self, tok, reads, writes):
        for k in writes:
            self.lastw[k] = tok
            self.readers[k] = {}
        for k in reads:
            if k not in writes:
                r = self.readers.setdefault(k, {})
                if tok[1] > r.get(tok[0], 0):
                    r[tok[0]] = tok[1]

    def op(self, eng, fn, reads=(), writes=(), *args, **kwargs):
        fn = (fn, args, kwargs)
        waits = self._waits(eng, self._deps(reads, writes))
        self.cnt[eng] += 1
        tok = (('eng', eng), self.cnt[eng])
        self.semkeys.add(tok[0])
        self.ops[eng].append((waits, fn, tok, 1))
        self._commit(tok, reads, writes)

    def dma(self, q, reads=(), writes=(), **kwargs):
        fn = ('dma_start', (), kwargs)
        slot = self.dma_n[q] % self.NS
        self.dma_n[q] += 1
        sk = ('dma', q, slot)
        self.semkeys.add(sk)
        uses = self.dma_uses.get(sk, 0)
        deps = self._deps(reads, writes)
        if uses:
            deps.append((sk, 16 * uses))
        waits = self._waits(q, deps)
        self.dma_uses[sk] = uses + 1
        tok = (sk, 16 * (uses + 1))
        self.ops[q].append((waits, fn, tok, 16))
        self._commit(tok, reads, writes)

    def cc(self, reads=(), writes=(), *args, **kwargs):
        fn = ('collective_compute', args, kwargs)
        sk = ('cc', self.ncc)
        self.ncc += 1
        self.semkeys.add(sk)
        waits = self._waits('pool', self._deps(reads, writes))
        tok = (sk, 1)
        self.ops['pool'].append((waits, fn, tok, None))
        self._commit(tok, reads, writes)

    def barrier(self):
        toks = [(('eng', e), self.cnt[e]) for e in self.ENGS if self.cnt[e]]
        toks += [(sk, 16 * u) for sk, u in self.dma_uses.items()]
        cct = [(('cc', i), 1) for i in range(self.ncc)]
        for e in self.ENGS:
            waits = self._waits(e, toks + (cct if e == 'pool' else []))
            if waits:
                self.ops[e].append((waits, None, None, 0))

    def finish(self, eng, keys):
        waits = self._waits(eng, self._deps(keys, ()))
        self.ops[eng].append((waits, None, None, 0))

    def emit(self, nc, stack):
        sems = {}
        for i, sk in enumerate(sorted(self.semkeys, key=str)):
            sems[sk] = stack.enter_context(nc.semaphore("s%d" % i))
        block = stack.enter_context(nc.Block())
        engmap = {'pe': block.tensor, 'act': block.scalar, 'dve': block.vector,
                  'pool': block.gpsimd, 'sp': block.sync}
        for e in self.ENGS:
            def body(eng, ops=self.ops[e]):
                for waits, fn, tok, inc in ops:
                    for sk, v in waits:
                        eng.wait_ge(sems[sk], v)
                    if fn is None:
                        continue
                    try:
                        ins = getattr(eng, fn[0])(*fn[1], **fn[2])
                    except Exception:
                        print("EMIT FAIL", fn[0], [str(a)[:160] for a in fn[1]], {k: str(v)[:160] for k, v in fn[2].items()})
                        raise
                    if inc is None:
                        ins.then_inc(sems[tok[0]])
                    else:
                        ins.then_inc(sems[tok[0]], inc)
            engmap[e](body)


def build(TL, prog, first=False):
    nc = bass.Bass("TRN2", target_bir_lowering=False)
    S = Sched()
    stack = contextlib.ExitStack()
    TT = min(512, TL)
    NT = TL // TT
    ST = min(512, TL)
    NST = TL // ST
    TPS = ST // TT
    NB = TL // 128

    declared = {}
    SHAPES = {
        "x": [TL, D], "p": [4, TL, DPLE], "vecs": [128, NVEC], "cvec": [128, 17], "ident_in": [128, 128],
        "ffn_w_in": [4, D, 2 * DFF], "ffn_w_out": [4, DFF, D], "ple_w_proj": [4, DPLE, D], "ple_w_gate": [4, D, D],
        "lru_w_in": [2, D, 2 * DR], "lru_w_gate_a": [2, RC * 128, 128], "lru_w_gate_x": [2, RC * 128, 128],
        "lru_w_out": [2, DR, D],
    }

    def W(name):
        if name not in declared:
            declared[name] = nc.dram_tensor(name, list(SHAPES[name]), F32, kind="ExternalInput").ap()
        return declared[name]

    outs_decl = {}

    def OUT(name, shape):
        if name not in outs_decl:
            outs_decl[name] = nc.dram_tensor(name, list(shape), F32, kind="ExternalOutput").ap()
        return outs_decl[name]

    SHAPES.update({"h_in": [128, DC, TL], "halo": [128, 32], "gsumm": [128, NCORES, 32],
                   "rw_w_rkv": [2, 3, D, D], "rw_w_o": [2, D, D], "rw_w1": [2, D, 64], "rw_w2": [2, 64, D],
                   "rw_a1": [2, D, 64], "rw_a2": [2, 64, D], "rw_v1": [1, D, 32], "rw_v2": [1, 32, D],
                   "rw_g1": [2, D, 160], "rw_g2": [2, 160, D], "rconst": [128, 640],
                   "grsumm": [NCORES, 16, 64, 128], "vfirst": [128, DC, TL]})
    out_keys = []

    def sb(name, shape, dt=F32):
        return stack.enter_context(nc.sbuf_tensor(name, list(shape), dt))

    hT = sb("hT", [128, DC, TL])
    xn = sb("xn", [128, DC, HALO + TL], BF16)
    vecs = sb("vecs_sb", [128, NVEC])
    cvec = sb("cvec_sb", [128, 17])
    gath = sb("gath", [128, NCORES, 32])
    smal = sb("smal", [128, 128])
    ident = sb("ident", [128, 128])
    ones_bf = sb("ones_bf", [128, 128], BF16)
    NWB = 2
    wbufs = [sb("wb%d" % i, [128, 4096], BF16) for i in range(NWB)]
    stage = [sb("stage%d" % i, [128, D]) for i in range(2)]
    sq = sb("sq", [128, DC, TT], BF16)
    rs = [sb("rs%d" % i, [128, TT]) for i in range(2)]
    tmpA = [sb("tmpA%d" % i, [128, TT]) for i in range(3)]
    ARENA = max(FC * ST * 2 + 2 * TL * 2, 5 * (TL + 4) * 4 + 2 * TL * 2 + 64, 5 * TL * 2 + 4 * 10430)
    arena = sb("arena", [128, ARENA // 2], BF16)
    big = arena[:, :FC * ST]
    pT = arena[:, FC * ST:FC * ST + 2 * TL].rearrange("p (k t) -> p k t", k=2)
    psums = [stack.enter_context(nc.psum_tensor("ps%d" % i, [128, 512], F32)) for i in range(8)]
    st = {'ps': 0, 'wb': 0, 'stage': 0, 'rs': 0, 'tmpA': 0}

    def rot(name, n):
        i = st[name]
        st[name] = (i + 1) % n
        return i

    def vcol(name, c=0):
        o = VOFF[name] + c
        return vecs[:, o:o + 1]

    S.dma('sp', (), ('vecs',), out=vecs[:], in_=W("vecs"))
    S.dma('sp', (), ('cvec',), out=cvec[:], in_=W("cvec"))
    S.dma('sp', (), ('ident',), out=ident[:], in_=W("ident_in"))
    S.op('dve', 'memset', (), ('ones',), ones_bf[:], 1.0)

    def tsl(t):
        return slice(t * TT, (t + 1) * TT)

    def xsl(t, sh=0):
        return slice(HALO + t * TT - sh, HALO + (t + 1) * TT - sh)

    def wview(i, kc, mb):
        return wbufs[i][:, :kc * mb].rearrange("p (k m) -> p k m", k=kc)

    def wload(i, kc, mb, src, m0=0, mw=None):
        mw = mb if mw is None else mw
        dst = wview(i, kc, mb)[:, :, m0:m0 + mw]
        S.dma('pool', (), (('wb', i),), out=dst, in_=src.rearrange("(k p) m -> p k m", p=128))

    def MM(pi, out, lhsT, rhs, start, stop, reads):
        S.op('pe', 'matmul', reads, (('ps', pi),), out, lhsT, rhs, start=start, stop=stop)

    def TR(pi, out, in_, reads, b0=0):
        np_ = in_.shape[0]
        S.op('pe', 'transpose', tuple(reads) + ('ident',), (('ps', pi),), out, in_, ident[b0:b0 + np_, b0:b0 + np_])

    def load_xT():
        for tb in range(NB):
            si = rot('stage', 2)
            S.dma('sp', (), (('stage', si),), out=stage[si][:], in_=W("x")[tb * 128:(tb + 1) * 128, :])
            for half in range(2):
                pi = rot('ps', 6)
                for j in range(4):
                    c = half * 4 + j
                    TR(pi, psums[pi][:, j * 128:(j + 1) * 128], stage[si][:, c * 128:(c + 1) * 128], (('stage', si),))
                S.op('dve', 'tensor_copy', (('ps', pi),), tuple(('h', half * 4 + j, tb * 128 // TT) for j in range(4)),
                     out=hT[:, half * 4:(half + 1) * 4, tb * 128:(tb + 1) * 128],
                     in_=psums[pi][:].rearrange("p (j t) -> p j t", j=4))

    def rms_tile(t):
        S.op('act', 'activation', tuple(('h', c, t) for c in range(DC)), ('sq',),
             out=sq[:], in_=hT[:, :, tsl(t)], func=AF.Square)
        pi = rot('ps', 6)
        for c in range(DC):
            MM(pi, psums[pi][:, :TT], ones_bf[:], sq[:, c, :], c == 0, c == DC - 1, ('sq', 'ones'))
        ri = rot('rs', 2)
        S.op('act', 'activation', (('ps', pi), 'vecs'), (('rs', ri),),
             out=rs[ri][:], in_=psums[pi][:, :TT], func=AF.Sqrt, bias=vcol('eps'), scale=1.0 / D)
        S.op('dve', 'reciprocal', (('rs', ri),), (('rs', ri),), out=rs[ri][:], in_=rs[ri][:])
        return ri

    def rmsnorm_xn(gname):
        for t in range(NT):
            ri = rms_tile(t)
            for c in range(DC):
                S.op('dve', 'scalar_tensor_tensor', (('h', c, t), ('rs', ri), 'vecs'), (('xn', c, t),),
                     out=xn[:, c, xsl(t)], in0=hT[:, c, tsl(t)], scalar=vcol(gname, c),
                     in1=rs[ri][:], op0=ALU.mult, op1=ALU.mult)

    def ffn(l):
        rmsnorm_xn('norm_ffn%d' % l)
        hid = big[:, :FC * ST].rearrange("p (f t) -> p f t", f=FC)
        for s_ in range(NST):
            for g0 in range(0, FC, 2):
                gw = min(2, FC - g0)
                wi = rot('wb', NWB)
                wload(wi, DC, 512, W("ffn_w_in")[l][:, g0 * 128:(g0 + gw) * 128], 0, gw * 128)
                wload(wi, DC, 512, W("ffn_w_in")[l][:, DFF + g0 * 128:DFF + (g0 + gw) * 128], 256, gw * 128)
                wv = wview(wi, DC, 512)
                for j in range(gw):
                    f = g0 + j
                    for tt in range(TPS):
                        t = s_ * TPS + tt
                        pa, pb = rot('ps', 6), rot('ps', 6)
                        for (pi, off) in ((pa, 0), (pb, 256)):
                            for c in range(DC):
                                MM(pi, psums[pi][:, :TT], wv[:, c, off + j * 128:off + (j + 1) * 128], xn[:, c, xsl(t)],
                                   c == 0, c == DC - 1, (('wb', wi), ('xn', c, t)))
                        ai = rot('tmpA', 3)
                        S.op('act', 'activation', (('ps', pa),), (('tmpA', ai),),
                             out=tmpA[ai][:], in_=psums[pa][:, :TT], func=AF.Silu)
                        S.op('dve', 'tensor_tensor', (('ps', pb), ('tmpA', ai)), (('big', f, tt),),
                             out=hid[:, f, tt * TT:(tt + 1) * TT], in0=psums[pb][:, :TT], in1=tmpA[ai][:], op=ALU.mult)
            for m0 in range(0, DC, 1):
                wi = rot('wb', NWB)
                wload(wi, FC, 128, W("ffn_w_out")[l][:, m0 * 128:(m0 + 1) * 128])
                wv = wview(wi, FC, 128)
                for j in range(1):
                    mo = m0 + j
                    for tt in range(TPS):
                        t = s_ * TPS + tt
                        pi = rot('ps', 6)
                        for f in range(FC):
                            MM(pi, psums[pi][:, :TT], wv[:, f, j * 128:(j + 1) * 128], hid[:, f, tt * TT:(tt + 1) * TT],
                               f == 0, f == FC - 1, (('wb', wi), ('big', f, tt)))
                        S.op('dve', 'tensor_tensor', (('ps', pi), ('h', mo, t)), (('h', mo, t),),
                             out=hT[:, mo, tsl(t)], in0=psums[pi][:, :TT], in1=hT[:, mo, tsl(t)], op=ALU.add)

    def ple(l):
        rmsnorm_xn('norm_ple%d' % l)
        for tb in range(NB):
            si = rot('stage', 2)
            S.dma('sp', (), (('stage', si),), out=stage[si][:, :DPLE], in_=W("p")[l, tb * 128:(tb + 1) * 128, :])
            pi = rot('ps', 6)
            for j in range(2):
                TR(pi, psums[pi][:, j * 128:(j + 1) * 128], stage[si][:, j * 128:(j + 1) * 128], (('stage', si),))
            S.op('act', 'activation', (('ps', pi),), (('pT', tb * 128 // TT),),
                 out=pT[:, :, tb * 128:(tb + 1) * 128], in_=psums[pi][:, :256].rearrange("p (j t) -> p j t", j=2), func=AF.Copy)
        S.barrier()
        wp = 'pleproj'
        wpv = big[:, 0:2048].rearrange("p (k m) -> p k m", k=2)
        S.dma('pool', (), (('wb', wp),), out=wpv, in_=W("ple_w_proj")[l].rearrange("(k p) m -> p k m", p=128))
        for mg in range(2):
            wi = rot('wb', NWB)
            wload(wi, DC, 512, W("ple_w_gate")[l][:, mg * 512:(mg + 1) * 512])
            wv = wview(wi, DC, 512)
            for j in range(4):
                mo = mg * 4 + j
                for t in range(NT):
                    pa, pb = rot('ps', 6), rot('ps', 6)
                    for kc in range(2):
                        MM(pa, psums[pa][:, :TT], wpv[:, kc, mo * 128:(mo + 1) * 128], pT[:, kc, tsl(t)],
                           kc == 0, kc == 1, (('wb', wp), ('pT', t)))
                    for c in range(DC):
                        MM(pb, psums[pb][:, :TT], wv[:, c, j * 128:(j + 1) * 128], xn[:, c, xsl(t)],
                           c == 0, c == DC - 1, (('wb', wi), ('xn', c, t)))
                    ai, bi = rot('tmpA', 3), rot('tmpA', 3)
                    S.op('act', 'activation', (('ps', pb),), (('tmpA', ai),),
                         out=tmpA[ai][:], in_=psums[pb][:, :TT], func=AF.Sigmoid)
                    S.op('dve', 'tensor_tensor', (('ps', pa), ('tmpA', ai)), (('tmpA', bi),),
                         out=tmpA[bi][:], in0=psums[pa][:, :TT], in1=tmpA[ai][:], op=ALU.mult)
                    S.op('dve', 'tensor_tensor', (('tmpA', bi), ('h', mo, t)), (('h', mo, t),),
                         out=hT[:, mo, tsl(t)], in0=hT[:, mo, tsl(t)], in1=tmpA[bi][:], op=ALU.add)

    def load_halo():
        S.dma('sp', (), ('smal',), out=smal[:, :DC * HALO], in_=W("halo"))
        S.op('dve', 'tensor_copy', ('smal',), ('xnhalo',), out=xn[:, :, 0:HALO],
             in_=smal[:, :DC * HALO].rearrange("p (c k) -> p c k", c=DC))

    def tail_out():
        tl = smal[:, :DC * HALO].rearrange("p (c k) -> p c k", c=DC)
        S.op('dve', 'tensor_copy', tuple(('xn', c, NT - 1) for c in range(DC)) + ('smal',), ('smal',), out=tl, in_=xn[:, :, TL:TL + HALO])
        S.dma('sp', ('smal',), ('o_tail',), out=OUT("tail", [128, 32]), in_=smal[:, :32])
        out_keys.append('o_tail')

    def hawk(l, final_pass):
        j = l // 2
        V = lambda n, c=0: vcol('h%d_%s' % (j, n), c)
        rmsnorm_xn('norm_mix%d' % l)
        load_halo()
        S.barrier()
        fa = arena[:].bitcast(F32)
        W4 = TL + 4
        recb = fa[:, 0:W4]
        cv = fa[:, W4:W4 + TL]
        Abuf = fa[:, 2 * W4:2 * W4 + TL]
        Hl = fa[:, 3 * W4:3 * W4 + TL]
        o16 = 5 * W4 * 2
        cvb = arena[:, o16:o16 + TL]
        P1 = arena[:, o16 + TL:o16 + 2 * TL]
        c8 = smal[:, 64:74]
        c16 = smal[:, 74:84]
        rsum = smal[:, 84:84 + NT]
        rtot = smal[:, 92:93]
        hin = smal[:, 96:106]
        carry = smal[:, 106:116]
        S.op('act', 'activation', ('vecs',), ('c8',), out=c8, in_=vecs[:, VOFF['h%d_lam' % j]:VOFF['h%d_lam' % j] + RC], func=AF.Exp, scale=-1.0)
        S.op('act', 'activation', ('c8', 'vecs'), ('c8',), out=c8, in_=c8, func=AF.Ln, bias=vcol('one'))
        S.op('dve', 'tensor_scalar', ('c8',), ('c16',), out=c16, in0=c8, scalar1=-16.0, scalar2=None, op0=ALU.mult)
        S.op('dve', 'tensor_scalar', ('c8', 'c16'), ('c8',), out=c8, in0=c8, scalar1=-8.0, scalar2=None, op0=ALU.mult)
        if True:
            if final_pass:
                S.dma('sp', (), ('gath',), out=gath[:], in_=W("gsumm"))
                S.op('dve', 'memset', (), ('carry',), carry, 0.0)
                S.op('dve', 'memset', (), ('hin',), hin, 0.0)
                for r in range(NCORES):
                    S.op('dve', 'scalar_tensor_tensor', ('carry', 'cvec', 'hin'), ('hin',),
                         out=hin, in0=carry, scalar=cvec[:, r:r + 1], in1=hin, op0=ALU.mult, op1=ALU.add)
                    if r < NCORES - 1:
                        S.op('dve', 'tensor_tensor', ('carry', 'gath'), ('carry',), out=carry, in0=carry, in1=gath[:, r, RC:2 * RC], op=ALU.mult)
                        S.op('dve', 'tensor_tensor', ('carry', 'gath'), ('carry',), out=carry, in0=carry, in1=gath[:, r, 0:RC], op=ALU.add)
            for ct in range(RC):
                wi = rot('wb', NWB)
                wb_ = wbufs[wi]
                wrec = wb_[:, 0:1024].rearrange("p (k m) -> p k m", k=DC)
                wgat = wb_[:, 1024:2048].rearrange("p (k m) -> p k m", k=DC)
                wa = wb_[:, 2048:2176]
                wx = wb_[:, 2176:2304]
                wo = wb_[:, 2304:3328]
                wkey = ('wb', wi)
                S.dma('pool', (), (wkey,), out=wrec, in_=W("lru_w_in")[j][:, DR + ct * 128:DR + (ct + 1) * 128].rearrange("(k p) m -> p k m", p=128))
                S.dma('pool', (), (wkey,), out=wa, in_=W("lru_w_gate_a")[j][ct * 128:(ct + 1) * 128, :])
                S.dma('pool', (), (wkey,), out=wx, in_=W("lru_w_gate_x")[j][ct * 128:(ct + 1) * 128, :])
                if final_pass:
                    S.dma('pool', (), (wkey,), out=wgat, in_=W("lru_w_in")[j][:, ct * 128:(ct + 1) * 128].rearrange("(k p) m -> p k m", p=128))
                    S.dma('pool', (), (wkey,), out=wo, in_=W("lru_w_out")[j][ct * 128:(ct + 1) * 128, :])
                pi = rot('ps', 6)
                for c in range(DC):
                    MM(pi, psums[pi][:, :HALO], wrec[:, c, :], xn[:, c, 0:HALO], c == 0, c == DC - 1, (wkey, 'xnhalo'))
                S.op('dve', 'tensor_scalar', (('ps', pi), 'vecs', 'cvec'), ('recb',), out=recb[:, 0:HALO], in0=psums[pi][:, :HALO],
                     scalar1=V('br', ct), scalar2=cvec[:, 16:17], op0=ALU.add, op1=ALU.mult)
                for t in range(NT):
                    pi = rot('ps', 6)
                    for c in range(DC):
                        MM(pi, psums[pi][:, :TT], wrec[:, c, :], xn[:, c, xsl(t)], c == 0, c == DC - 1, (wkey, ('xn', c, t)))
                    S.op('act', 'activation', (('ps', pi), 'vecs'), ('recb',), out=recb[:, HALO + t * TT:HALO + (t + 1) * TT],
                         in_=psums[pi][:, :TT], func=AF.Identity, bias=V('br', ct))
                S.op('dve', 'tensor_scalar', ('recb', 'vecs'), ('cv',), out=cv, in0=recb[:, 1:1 + TL],
                     scalar1=V('cw0', ct), scalar2=V('cb', ct), op0=ALU.mult, op1=ALU.add)
                for k in range(1, 4):
                    S.op('dve', 'scalar_tensor_tensor', ('recb', 'vecs', 'cv'), ('cv',), out=cv, in0=recb[:, 1 + k:1 + k + TL],
                         scalar=V('cw%d' % k, ct), in1=cv, op0=ALU.mult, op1=ALU.add)
                S.op('act', 'activation', ('cv',), ('cvb',), out=cvb, in_=cv, func=AF.Copy)
                for t in range(NT):
                    pa, pb = rot('ps', 6), rot('ps', 6)
                    MM(pa, psums[pa][:, :TT], wa, cvb[:, tsl(t)], True, True, (wkey, 'cvb'))
                    MM(pb, psums[pb][:, :TT], wx, cvb[:, tsl(t)], True, True, (wkey, 'cvb'))
                    ri_, ii_, mi_ = rot('tmpA', 3), rot('tmpA', 3), rot('tmpA', 3)
                    S.op('act', 'activation', (('ps', pa), 'vecs'), (('tmpA', ri_),), out=tmpA[ri_][:], in_=psums[pa][:, :TT],
                         func=AF.Sigmoid, bias=V('ba', ct))
                    S.op('act', 'activation', (('ps', pb), 'vecs'), (('tmpA', ii_),), out=tmpA[ii_][:], in_=psums[pb][:, :TT],
                         func=AF.Sigmoid, bias=V('bx', ct))
                    S.op('dve', 'tensor_tensor', ('cv', ('tmpA', ii_)), ('cv',), out=cv[:, tsl(t)], in0=cv[:, tsl(t)], in1=tmpA[ii_][:], op=ALU.mult)
                    S.op('act', 'activation', (('tmpA', ri_), 'c8'), ('Abuf',), out=Abuf[:, tsl(t)], in_=tmpA[ri_][:],
                         func=AF.Exp, scale=c8[:, ct:ct + 1])
                    S.op('act', 'activation', (('tmpA', ri_), 'c16'), (('tmpA', mi_),), out=tmpA[mi_][:], in_=tmpA[ri_][:],
                         func=AF.Exp, scale=c16[:, ct:ct + 1])
                    if not final_pass:
                        S.op('dve', 'tensor_reduce', (('tmpA', ri_),), ('rsum',), out=rsum[:, t:t + 1], in_=tmpA[ri_][:], axis=mybir.AxisListType.X, op=ALU.add)
                    S.op('act', 'activation', (('tmpA', mi_), 'vecs'), (('tmpA', mi_),), out=tmpA[mi_][:], in_=tmpA[mi_][:],
                         func=AF.Sqrt, scale=-1.0, bias=vcol('one'))
                    S.op('dve', 'tensor_tensor', ('cv', ('tmpA', mi_)), ('cv',), out=cv[:, tsl(t)], in0=cv[:, tsl(t)], in1=tmpA[mi_][:], op=ALU.mult)
                S.op('dve', 'tensor_tensor_scan', ('Abuf', 'cv', 'hin'), ('Hl',), out=Hl, data0=Abuf, data1=cv,
                     initial=(hin[:, ct:ct + 1] if final_pass else 0.0), op0=ALU.mult, op1=ALU.add)
                if not final_pass:
                    S.op('dve', 'tensor_copy', ('Hl',), ('summ',), out=smal[:, ct:ct + 1], in_=Hl[:, TL - 1:TL])
                    S.op('dve', 'tensor_reduce', ('rsum',), ('rtot',), out=rtot, in_=rsum, axis=mybir.AxisListType.X, op=ALU.add)
                    S.op('act', 'activation', ('rtot', 'c8'), ('summ',), out=smal[:, RC + ct:RC + ct + 1], in_=rtot, func=AF.Exp, scale=c8[:, ct:ct + 1])
                    continue
                for t in range(NT):
                    pi = rot('ps', 6)
                    for c in range(DC):
                        MM(pi, psums[pi][:, :TT], wgat[:, c, :], xn[:, c, xsl(t)], c == 0, c == DC - 1, (wkey, ('xn', c, t)))
                    gi = rot('tmpA', 3)
                    S.op('act', 'activation', (('ps', pi), 'vecs'), (('tmpA', gi),), out=tmpA[gi][:], in_=psums[pi][:, :TT],
                         func=AF.Gelu_apprx_tanh, bias=V('bg', ct))
                    S.op('dve', 'tensor_tensor', ('Hl', ('tmpA', gi)), ('P1',), out=P1[:, tsl(t)], in0=Hl[:, tsl(t)], in1=tmpA[gi][:], op=ALU.mult)
                    for mo in range(DC):
                        pi = rot('ps', 6)
                        MM(pi, psums[pi][:, :TT], wo[:, mo * 128:(mo + 1) * 128], P1[:, tsl(t)], True, True, (wkey, 'P1'))
                        if ct == 0:
                            S.op('dve', 'scalar_tensor_tensor', (('ps', pi), ('h', mo, t), 'vecs'), (('h', mo, t),), out=hT[:, mo, tsl(t)],
                                 in0=psums[pi][:, :TT], scalar=V('bo', mo), in1=hT[:, mo, tsl(t)], op0=ALU.add, op1=ALU.add)
                        else:
                            S.op('dve', 'tensor_tensor', (('ps', pi), ('h', mo, t)), (('h', mo, t),), out=hT[:, mo, tsl(t)],
                                 in0=psums[pi][:, :TT], in1=hT[:, mo, tsl(t)], op=ALU.add)
        if not final_pass:
            S.dma('sp', ('summ',), ('o_summ',), out=OUT("summ", [128, 32]), in_=smal[:, :32])
            out_keys.append('o_summ')
        S.barrier()

    def rwkv(l, final_pass):
        j = l // 2
        RT = 128
        NRT = TL // RT
        V = lambda n, c=0: vcol('r%d_%s' % (j, n), c)
        rmsnorm_xn('norm_mix%d' % l)
        load_halo()
        S.barrier()
        off = [0]

        def A32(n):
            o = off[0]
            off[0] += 2 * n
            assert off[0] <= ARENA // 2, (off[0], ARENA // 2)
            return arena[:, o:o + 2 * n].bitcast(F32)

        def A16(n):
            o = off[0]
            off[0] += n
            assert off[0] <= ARENA // 2, (off[0], ARENA // 2)
            return arena[:, o:o + n]

        LW1, LA1, LV1, LG1a, LG1b = A16(TL), A16(TL), A16(TL), A16(TL), A16(TL)
        WS = A32(1280)
        rc = A32(640)
        rmask, bdones, cmask, I2 = rc[:, 0:320], rc[:, 320:448], rc[:, 448:576], rc[:, 576:640]
        omu = A32(48)
        (R, K, LW, ALR, CUM, WC, WINV, WPRV, KKN, T1, T2) = [A32(RT) for _ in range(11)]
        Vv2, KT2, BT2, BH2, KH2, CB2 = [[A32(RT) for _ in range(2)] for _ in range(6)]
        AR2 = [A32(2 * RT).rearrange("p (a t) -> p a t", a=2) for _ in range(2)]
        WCE2 = [A32(2) for _ in range(2)]
        TM = [A32(512) for _ in range(2)]
        GM = [A32(320) for _ in range(2)]
        NL = [[A32(128) for _ in range(2)] for _ in range(2)]
        Tm = [[A32(64) for _ in range(2)] for _ in range(2)]
        ZA, Vp = [A32(128) for _ in range(2)], [A32(64) for _ in range(2)]
        Us = [A32(128) for _ in range(2)]
        Sa = [A32(128) for _ in range(2)]
        Ytm, YN, O1, O2, STt = A32(128), A32(128), A32(128), A32(128), A32(16)
        G8 = [A32(256).rearrange("p (h n) -> p h n", h=2) for _ in range(2)]
        carry, Sin = [A32(64) for _ in range(2)], [A32(64) for _ in range(2)]
        SUMT = A32(256).rearrange("p (h n) -> p h n", h=2)
        OTb = A16(128)
        i64 = ident[0:64, 0:64]
        cnt = {'TM': 0, 'us': 0, 'g8': 0}

        S.dma('sp', (), ('rc',), out=rc, in_=W("rconst"))
        mu0 = VOFF['r%d_mu0' % j]
        S.op('dve', 'tensor_scalar', ('vecs',), ('omu',), out=omu, in0=vecs[:, mu0:mu0 + 48], scalar1=-1.0, scalar2=1.0, op0=ALU.mult, op1=ALU.add)

        def mixed_weights(src, m, mi, dst1, dst2, key):
            wsv = WS[:, :DC * m].rearrange("p (k m) -> p k m", k=DC)
            S.dma('sp', (), ('WS',), out=wsv, in_=src.rearrange("(k p) m -> p k m", p=128))
            for c in range(DC):
                S.op('dve', 'tensor_scalar', ('WS', 'omu'), (key,), out=dst1[:, c, :], in0=wsv[:, c, :], scalar1=omu[:, mi * 8 + c:mi * 8 + c + 1], scalar2=None, op0=ALU.mult)
                S.op('pool', 'tensor_scalar', ('WS', 'vecs'), (key,), out=dst2[:, c, :], in0=wsv[:, c, :], scalar1=vecs[:, mu0 + mi * 8 + c:mu0 + mi * 8 + c + 1], scalar2=None, op0=ALU.mult)

        def proj16(pi, out, w1v, w2v, m0, m1, cols0, n, wkey):
            for c in range(DC):
                MM(pi, out, w1v[:, c, m0:m1], xn[:, c, HALO + cols0:HALO + cols0 + n], c == 0, False, (wkey, ('xn', c, cols0 // TT), 'xnhalo'))
            for c in range(DC):
                MM(pi, out, w2v[:, c, m0:m1], xn[:, c, HALO + cols0 - 1:HALO + cols0 + n - 1], False, c == DC - 1,
                   (wkey, ('xn', c, cols0 // TT), ('xn', c, max(cols0 - 1, 0) // TT), 'xnhalo'))

        lora_list = [('rw_w1', 64, 1, [(0, 64, LW1, AF.Tanh)]), ('rw_a1', 64, 4, [(0, 64, LA1, AF.Copy)]),
                     ('rw_g1', 160, 5, [(0, 128, LG1a, AF.Sigmoid), (128, 160, LG1b, AF.Sigmoid)])]
        if j == 1:
            lora_list.append(('rw_v1', 32, 3, [(0, 32, LV1, AF.Copy)]))
        for (wn, m, mi, outs) in lora_list:
            wi = rot('wb', NWB)
            wkey = ('wb', wi)
            d1 = wbufs[wi][:, 0:DC * m].rearrange("p (k m) -> p k m", k=DC)
            d2 = wbufs[wi][:, DC * m:2 * DC * m].rearrange("p (k m) -> p k m", k=DC)
            src = W(wn)[0 if wn == 'rw_v1' else j]
            mixed_weights(src, m, mi, d1, d2, wkey)
            for t in range(NT):
                for (m0, m1, dst, fn) in outs:
                    pi = rot('ps', 6)
                    proj16(pi, psums[pi][0:m1 - m0, :TT], d1, d2, m0, m1, t * TT, TT, wkey)
                    S.op('act', 'activation', (('ps', pi),), ('lora',), out=dst[0:m1 - m0, tsl(t)], in_=psums[pi][0:m1 - m0, :TT], func=fn)

        for ct in range(DC):
            wa_i, wb_i = rot('wb', NWB), rot('wb', NWB)
            ka, kb = ('wb', wa_i), ('wb', wb_i)
            bufa, bufb = wbufs[wa_i], wbufs[wb_i]
            wv_ = lambda buf, o: buf[:, o:o + 1024].rearrange("p (k m) -> p k m", k=DC)
            r1, r2, k1, k2 = wv_(bufa, 0), wv_(bufa, 1024), wv_(bufa, 2048), wv_(bufa, 3072)
            v1_, v2_ = wv_(bufb, 0), wv_(bufb, 1024)
            w2c, a2c, v2c = bufb[0:64, 2048:2176], bufb[0:64, 2176:2304], bufb[0:32, 2304:2432]
            g2a, g2b = bufb[:, 2432:2560], bufb[0:32, 2560:2688]
            wo = bufb[:, 2688:3712]
            cs_ = slice(ct * 128, (ct + 1) * 128)
            mixed_weights(W("rw_w_rkv")[j, 0][:, cs_], 128, 0, r1, r2, ka)
            mixed_weights(W("rw_w_rkv")[j, 1][:, cs_], 128, 2, k1, k2, ka)
            mixed_weights(W("rw_w_rkv")[j, 2][:, cs_], 128, 3, v1_, v2_, kb)
            S.dma('pool', (), (kb,), out=w2c, in_=W("rw_w2")[j][:, cs_])
            S.dma('pool', (), (kb,), out=a2c, in_=W("rw_a2")[j][:, cs_])
            if j == 1:
                S.dma('pool', (), (kb,), out=v2c, in_=W("rw_v2")[0][:, cs_])
            if final_pass:
                S.dma('pool', (), (kb,), out=g2a, in_=W("rw_g2")[j][0:128, cs_])
                S.dma('pool', (), (kb,), out=g2b, in_=W("rw_g2")[j][128:160, cs_])
                S.dma('pool', (), (kb,), out=wo, in_=W("rw_w_o")[j][cs_, :])
            S.op('dve', 'memset', (), (('Sa', 0, 0), ('Sa', 0, 1)), Sa[0], 0.0)
            if not final_pass:
                S.op('dve', 'tensor_copy', ('rc', ('Sa', 0, 0), ('Sa', 0, 1)), (('Sa', 0, 0), ('Sa', 0, 1)), out=Sa[0][:, 64:128], in_=I2)
            else:
                for h in range(2):
                    S.op('dve', 'memset', (), (('carry', h),), carry[h][0:64, :], 0.0)
                    S.op('dve', 'memset', (), (('Sin', h),), Sin[h][0:64, :], 0.0)
                for r in range(NCORES):
                    if r < NCORES - 1:
                        gi = cnt['g8'] % 2
                        cnt['g8'] += 1
                        S.dma('sp', (), (('G8', gi),), out=G8[gi][0:64, :, :], in_=W("grsumm")[r, 2 * ct:2 * ct + 2].rearrange("h k n -> k h n"))
                    for h in range(2):
                        S.op('dve', 'scalar_tensor_tensor', (('carry', h), ('Sin', h), 'cvec'), (('Sin', h),), out=Sin[h][0:64, :],
                             in0=carry[h][0:64, :], scalar=cvec[0:64, r:r + 1], in1=Sin[h][0:64, :], op0=ALU.mult, op1=ALU.add)
                        if r < NCORES - 1:
                            pi = rot('ps', 6)
                            MM(pi, psums[pi][0:64, 0:64], G8[gi][0:64, h, 64:128], carry[h][0:64, :], True, True, (('G8', gi), ('carry', h)))
                            S.op('dve', 'tensor_tensor', (('ps', pi), ('G8', gi)), (('carry', h),), out=carry[h][0:64, :],
                                 in0=psums[pi][0:64, 0:64], in1=G8[gi][0:64, h, 0:64], op=ALU.add)
                for h in range(2):
                    S.op('act', 'activation', (('Sin', h), ('Sa', 0, h)), (('Sa', 0, h),), out=Sa[0][h * 64:(h + 1) * 64, 0:64], in_=Sin[h][0:64, :], func=AF.Copy)
            par = [0]
            NA = 64 if final_pass else 128
            def prep(t, b):
                Vv, AR, KT, BT, BH, KH, WCE, CB = Vv2[b], AR2[b], KT2[b], BT2[b], BH2[b], KH2[b], WCE2[b], CB2[b]
                fmV = ('fm', id(Vv))
                c0 = t * RT
                ht = c0 // TT
                tcols = slice(c0, c0 + RT)
                py = [6, 7]
                cs = None
                c0 = t * RT
                ht = c0 // TT
                tcols = slice(c0, c0 + RT)
                for (w1v, w2v, dst, wkey) in ((r1, r2, R, ka), (k1, k2, K, ka), (v1_, v2_, Vv, kb)):
                    pi = rot('ps', 6)
                    proj16(pi, psums[pi][:, :RT], w1v, w2v, 0, 128, c0, RT, wkey)
                    S.op('act', 'activation', (('ps', pi),), (('fm', id(dst)),), out=dst, in_=psums[pi][:, :RT], func=AF.Copy)
                pi = rot('ps', 6)
                MM(pi, psums[pi][:, :RT], w2c, LW1[0:64, tcols], True, True, (kb, 'lora'))
                S.op('act', 'activation', (('ps', pi), 'vecs'), ('LW',), out=LW, in_=psums[pi][:, :RT], func=AF.Sigmoid, bias=V('w0', ct))
                yield
                S.op('dve', 'tensor_scalar', ('LW',), ('LW',), out=LW, in0=LW, scalar1=-0.6065306597126334, scalar2=None, op0=ALU.mult)
                yield
                pi = rot('ps', 6)
                MM(pi, psums[pi][:, :RT], a2c, LA1[0:64, tcols], True, True, (kb, 'lora'))
                S.op('act', 'activation', (('ps', pi), 'vecs'), ('ALR',), out=ALR, in_=psums[pi][:, :RT], func=AF.Sigmoid, bias=V('a0', ct))
                yield
                fmV = ('fm', id(Vv))
                fmK = ('fm', id(K))
                fmR = ('fm', id(R))
                if j == 1:
                    pi = rot('ps', 6)
                    MM(pi, psums[pi][:, :RT], v2c, LV1[0:32, tcols], True, True, (kb, 'lora'))
                    S.op('act', 'activation', (('ps', pi), 'vecs'), ('T2',), out=T2, in_=psums[pi][:, :RT], func=AF.Sigmoid, bias=V('v0', ct))
                    S.dma('sp', (), ('T1',), out=T1, in_=W("vfirst")[:, ct, tcols])
                    S.op('dve', 'tensor_tensor', ('T1', fmV), ('T1',), out=T1, in0=T1, in1=Vv, op=ALU.subtract)
                    S.op('dve', 'tensor_tensor', ('T1', 'T2'), ('T1',), out=T1, in0=T1, in1=T2, op=ALU.mult)
                    S.op('dve', 'tensor_tensor', ('T1', fmV), (fmV,), out=Vv, in0=Vv, in1=T1, op=ALU.add)
                elif final_pass:
                    S.dma('sp', (fmV,), (('o_vf', ct, t),), out=OUT("vfirst_out", [128, DC, TL])[:, ct, tcols], in_=Vv)
                    out_keys.append(('o_vf', ct, t))
                S.op('dve', 'tensor_scalar', (fmK, 'vecs'), ('KKN',), out=KKN, in0=K, scalar1=V('kk', ct), scalar2=None, op0=ALU.mult)
                yield
                S.op('dve', 'tensor_tensor', ('KKN',), ('T1',), out=T1, in0=KKN, in1=KKN, op=ALU.mult)
                yield
                pi = rot('ps', 6)
                MM(pi, psums[pi][:, :RT], bdones, T1, True, True, ('rc', 'T1'))
                S.op('act', 'activation', (('ps', pi),), ('T1',), out=T1, in_=psums[pi][:, :RT], func=AF.Sqrt)
                yield
                S.op('dve', 'tensor_scalar', ('T1',), ('T1',), out=T1, in0=T1, scalar1=1e-12, scalar2=None, op0=ALU.max)
                yield
                S.op('dve', 'reciprocal', ('T1',), ('T1',), out=T1, in_=T1)
                yield
                S.op('dve', 'tensor_tensor', ('KKN', 'T1'), ('KKN',), out=KKN, in0=KKN, in1=T1, op=ALU.mult)
                yield
                S.op('dve', 'tensor_scalar', ('ALR', 'vecs'), ('T1',), out=T1, in0=ALR, scalar1=-1.0, scalar2=V('ka', ct), op0=ALU.add, op1=ALU.mult)
                yield
                S.op('dve', 'scalar_tensor_tensor', ('T1', fmK), (fmK,), out=K, in0=T1, scalar=1.0, in1=K, op0=ALU.add, op1=ALU.mult)
                yield
                S.op('dve', 'tensor_tensor_scan', ('rc', 'LW'), ('CUM',), out=CUM, data0=cmask, data1=LW, initial=0.0, op0=ALU.mult, op1=ALU.add)
                yield
                S.op('act', 'activation', ('CUM',), ('WC',), out=WC, in_=CUM, func=AF.Exp)
                yield
                S.op('act', 'activation', ('CUM',), ('WINV',), out=WINV, in_=CUM, func=AF.Exp, scale=-1.0)
                yield
                S.op('dve', 'tensor_tensor', ('CUM', 'LW'), ('T1',), out=T1, in0=CUM, in1=LW, op=ALU.subtract)
                yield
                S.op('act', 'activation', ('T1',), ('WPRV',), out=WPRV, in_=T1, func=AF.Exp)
                yield
                S.op('dve', 'tensor_copy', ('WC',), (('WCE', b),), out=WCE, in_=WC.rearrange("p (c k) -> p c k", k=64)[:, :, 63])
                yield
                S.op('dve', 'scalar_tensor_tensor', ('KKN', 'WPRV'), (('AR0', b),), out=AR[:, 0, :], in0=KKN, scalar=-1.0, in1=WPRV, op0=ALU.mult, op1=ALU.mult)
                yield
                S.op('dve', 'tensor_tensor', (fmR, 'WC'), (('AR1', b),), out=AR[:, 1, :], in0=R, in1=WC, op=ALU.mult)
                yield
                S.op('dve', 'tensor_tensor', (fmK, 'WINV'), (('KT', b),), out=KT, in0=K, in1=WINV, op=ALU.mult)
                yield
                S.op('dve', 'tensor_tensor', ('KKN', 'ALR'), ('T1',), out=T1, in0=KKN, in1=ALR, op=ALU.mult)
                yield
                S.op('dve', 'tensor_tensor', ('T1', 'WINV'), (('BT', b),), out=BT, in0=T1, in1=WINV, op=ALU.mult)
                yield
                for c in range(2):
                    cs = slice(c * 64, (c + 1) * 64)
                    S.op('dve', 'tensor_scalar', (('BT', b), ('WCE', b)), (('BH', b),), out=BH[:, cs], in0=BT[:, cs], scalar1=WCE[:, c:c + 1], scalar2=None, op0=ALU.mult)
                    S.op('pool', 'tensor_scalar', (('KT', b), ('WCE', b)), (('KH', b),), out=KH[:, cs], in0=KT[:, cs], scalar1=WCE[:, c:c + 1], scalar2=None, op0=ALU.mult)
                if final_pass:
                    S.op('dve', 'tensor_tensor', (fmR, fmK), ('T2',), out=T2, in0=R, in1=K, op=ALU.mult)
                    S.op('dve', 'tensor_scalar', ('T2', 'vecs'), ('T2',), out=T2, in0=T2, scalar1=V('rk', ct), scalar2=None, op0=ALU.mult)
                    pi = rot('ps', 6)
                    MM(pi, psums[pi][:, :RT], bdones, T2, True, True, ('rc', 'T2'))
                    S.op('act', 'activation', (('ps', pi),), (('CB', b),), out=CB, in_=psums[pi][:, :RT], func=AF.Copy)
                    py = [6, 7]
                yield

            def chunks(t, b):
                Vv, AR, KT, BT, BH, KH, WCE, CB = Vv2[b], AR2[b], KT2[b], BT2[b], BH2[b], KH2[b], WCE2[b], CB2[b]
                fmV = ('fm', id(Vv))
                c0 = t * RT
                ht = c0 // TT
                tcols = slice(c0, c0 + RT)
                py = [6, 7]
                cs = None
                hb = [slice(0, 64), slice(64, 128)]
                csl = [slice(0, 64), slice(64, 128)]
                tms = []
                yield
                for c in range(2):
                    cs = csl[c]
                    pi = rot('ps', 6)
                    for q, (src, sk_) in enumerate(((Vv[:, cs], fmV), (AR[:, 0, cs], ('AR0', b)), (BH[:, cs], ('BH', b)), (KH[:, cs], ('KH', b)))):
                        TR(pi, psums[pi][0:64, q * 128:(q + 1) * 128], src, (sk_,))
                    tmi = cnt['TM'] % 2
                    cnt['TM'] += 1
                    tm = TM[tmi]
                    S.op('act', 'activation', (('ps', pi),), (('TM', tmi),), out=tm[0:64, :], in_=psums[pi][0:64, :], func=AF.Copy)
                    S.op('dve', 'tensor_copy', (('ps', pi),), (('TM', tmi),), out=tm[64:128, :], in_=psums[pi][0:64, :])
                    tms.append((tm, ('TM', tmi)))
                yield
                for c in range(2):
                    cs = csl[c]
                    for h in range(2):
                        bh = hb[h]
                        pi = rot('ps', 6)
                        MM(pi, psums[pi][bh, 0:128], BT[bh, cs], AR[bh, :, cs], True, True, (('BT', b), ('AR0', b), ('AR1', b)))
                        MM(pi, psums[pi][bh, 128:256], KT[bh, cs], AR[bh, :, cs], True, True, (('KT', b), ('AR0', b), ('AR1', b)))
                        MM(pi, psums[pi][bh, 256:320], AR[bh, 0, cs], BT[bh, cs], True, True, (('BT', b), ('AR0', b)))
                        S.op('dve', 'tensor_tensor', (('ps', pi), 'rc'), (('GM', c, h),), out=GM[c][bh, :], in0=psums[pi][bh, 0:320], in1=rmask[bh, :], op=ALU.mult)
                yield
                for c in range(2):
                    for h in range(2):
                        bh = hb[h]
                        S.op('dve', 'tensor_tensor', (('GM', c, h), 'rc'), (('Tm', c, 0, h),), out=Tm[c][0][bh, :], in0=GM[c][bh, 0:64], in1=I2[bh, :], op=ALU.add)
                Np = [GM[c][:, 0:64] for c in range(2)]
                Lp = [GM[c][:, 256:320] for c in range(2)]
                nlk = [('GM', c) for c in range(2)]
                yield
                for lev in range(1, 6):
                    nli = lev % 2
                    lo = 0 if lev < 5 else 64
                    yield
                    for c in range(2):
                        for h in range(2):
                            bh = hb[h]
                            pi = rot('ps', 6)
                            if lev < 5:
                                MM(pi, psums[pi][bh, 0:64], Lp[c][bh, :], Np[c][bh, :], True, True, (nlk[c] + (h,),))
                            MM(pi, psums[pi][bh, 64:128], Np[c][bh, :], Lp[c][bh, :], True, True, (nlk[c] + (h,),))
                            S.op('act', 'activation', (('ps', pi),), (('NL', c, nli, h),), out=NL[c][nli][bh, lo:128], in_=psums[pi][bh, lo:128], func=AF.Copy)
                    yield
                    for c in range(2):
                        Np[c], Lp[c], nlk[c] = NL[c][nli][:, 0:64], NL[c][nli][:, 64:128], ('NL', c, nli)
                    tp, tn = (lev - 1) % 2, lev % 2
                    yield
                    for c in range(2):
                        for h in range(2):
                            bh = hb[h]
                            pi = rot('ps', 6)
                            MM(pi, psums[pi][bh, 0:64], Lp[c][bh, :], Tm[c][tp][bh, :], True, True, (nlk[c] + (h,), ('Tm', c, tp, h)))
                            S.op('dve', 'tensor_tensor', (('ps', pi), ('Tm', c, tp, h)), (('Tm', c, tn, h),), out=Tm[c][tn][bh, :], in0=psums[pi][bh, 0:64], in1=Tm[c][tp][bh, :], op=ALU.add)
                yield
                for c in range(2):
                    tm, tmk = tms[c]
                    for h in range(2):
                        bh = hb[h]
                        pi = rot('ps', 6)
                        MM(pi, psums[pi][bh, 0:64], GM[c][bh, 128:192], tm[bh, h * 64:(h + 1) * 64], True, True, (('GM', c, h), tmk))
                        MM(pi, psums[pi][bh, 64:128], tm[bh, 128 + h * 64:128 + (h + 1) * 64], Tm[c][1][bh, :], True, True, (tmk, ('Tm', c, 1, h)))
                        S.op('act', 'activation', (('ps', pi),), (('ZA', c, h),), out=ZA[c][bh, :], in_=psums[pi][bh, 0:128], func=AF.Copy)
                yield
                for c in range(2):
                    for h in range(2):
                        bh = hb[h]
                        pi = rot('ps', 6)
                        MM(pi, psums[pi][bh, 0:64], Tm[c][1][bh, :], ZA[c][bh, 0:64], True, True, (('Tm', c, 1, h), ('ZA', c, h)))
                        S.op('act', 'activation', (('ps', pi),), (('Vp', c, h),), out=Vp[c][bh, :], in_=psums[pi][bh, 0:64], func=AF.Copy)
                yield
                for c in range(2):
                    cs = csl[c]
                    tm, tmk = tms[c]
                    cur, nxt = par[0], 1 - par[0]
                    par[0] = nxt
                    sc, sn = Sa[cur], Sa[nxt]
                    ui = cnt['us'] % 2
                    cnt['us'] += 1
                    us = Us[ui]
                    for h in range(2):
                        bh = hb[h]
                        pi = rot('ps', 6)
                        MM(pi, psums[pi][bh, 0:NA], ZA[c][bh, 64:128], sc[bh, 0:NA], True, True, (('ZA', c, h), ('Sa', cur, h)))
                        S.op('dve', 'tensor_tensor', (('ps', pi), ('Vp', c, h)), (('Us', ui, h),), out=us[bh, 0:64], in0=psums[pi][bh, 0:64], in1=Vp[c][bh, :], op=ALU.add)
                        if not final_pass:
                            S.op('act', 'activation', (('ps', pi),), (('Us', ui, h),), out=us[bh, 64:128], in_=psums[pi][bh, 64:128], func=AF.Copy)
                    if final_pass:
                        for h in range(2):
                            bh = hb[h]
                            yo = psums[py[h]][c * 64:(c + 1) * 64, 0:64]
                            MM(py[h], yo, AR[bh, 1, cs], sc[bh, 0:64], True, False, (('AR1', b), ('Sa', cur, h)))
                            MM(py[h], yo, GM[c][bh, 64:128], us[bh, 0:64], False, False, (('GM', c, h), ('Us', ui, h)))
                            MM(py[h], yo, GM[c][bh, 192:256], tm[bh, h * 64:(h + 1) * 64], False, True, (('GM', c, h), tmk))
                    for h in range(2):
                        bh = hb[h]
                        pi = rot('ps', 6)
                        MM(pi, psums[pi][bh, 0:NA], tm[bh, 256 + h * 64:256 + (h + 1) * 64], us[bh, 0:NA], True, False, (tmk, ('Us', ui, h)))
                        MM(pi, psums[pi][bh, 0:64], tm[bh, 384 + h * 64:384 + (h + 1) * 64], tm[bh, h * 64:(h + 1) * 64], False, True, (tmk,))
                        S.op('dve', 'scalar_tensor_tensor', (('ps', pi), ('Sa', cur, h), ('WCE', b)), (('Sa', nxt, h),), out=sn[bh, 0:NA],
                             in0=sc[bh, 0:NA], scalar=WCE[bh, c:c + 1], in1=psums[pi][bh, 0:NA], op0=ALU.mult, op1=ALU.add)
                if not final_pass:
                    return
                yield
                for h in range(2):
                    S.op('act', 'activation', (('ps', py[h]),), ('Ytm',), out=Ytm[:, h * 64:(h + 1) * 64], in_=psums[py[h]][:, 0:64], func=AF.Copy)
                yield
                for h in range(2):
                    hs = slice(h * 64, (h + 1) * 64)
                    S.op('dve', 'tensor_reduce', ('Ytm',), ('STa',), out=STt[:, h:h + 1], in_=Ytm[:, hs], axis=mybir.AxisListType.X, op=ALU.add)
                    S.op('act', 'activation', ('Ytm',), ('O2', 'STb'), out=O2[:, hs], in_=Ytm[:, hs], func=AF.Square, accum_out=STt[:, 2 + h:3 + h])
                yield
                S.op('dve', 'tensor_scalar', ('STa',), ('STa',), out=STt[:, 0:2], in0=STt[:, 0:2], scalar1=1.0 / 64, scalar2=None, op0=ALU.mult)
                yield
                S.op('dve', 'tensor_tensor', ('STa',), ('STc',), out=STt[:, 4:6], in0=STt[:, 0:2], in1=STt[:, 0:2], op=ALU.mult)
                yield
                S.op('dve', 'scalar_tensor_tensor', ('STb', 'STc'), ('STb',), out=STt[:, 2:4], in0=STt[:, 2:4], scalar=1.0 / 64, in1=STt[:, 4:6], op0=ALU.mult, op1=ALU.subtract)
                yield
                S.op('act', 'activation', ('STb', 'vecs'), ('STb',), out=STt[:, 2:4], in_=STt[:, 2:4], func=AF.Sqrt, bias=vcol('gneps'))
                yield
                S.op('dve', 'reciprocal', ('STb',), ('STb',), out=STt[:, 2:4], in_=STt[:, 2:4])
                yield
                S.op('dve', 'scalar_tensor_tensor', ('STa', 'STb'), ('STc',), out=STt[:, 4:6], in0=STt[:, 0:2], scalar=-1.0, in1=STt[:, 2:4], op0=ALU.mult, op1=ALU.mult)
                yield
                for h in range(2):
                    hs = slice(h * 64, (h + 1) * 64)
                    S.op('act', 'activation', ('Ytm', 'STb', 'STc'), ('YN',), out=YN[:, hs], in_=Ytm[:, hs], func=AF.Identity,
                         scale=STt[:, 2 + h:3 + h], bias=STt[:, 4 + h:5 + h])
                yield
                pi = rot('ps', 6)
                TR(pi, psums[pi][:, 0:128], YN, ('YN',))
                yield
                S.op('dve', 'tensor_scalar', (('ps', pi), 'vecs'), ('O1',), out=O1, in0=psums[pi][:, 0:128], scalar1=V('lnw', ct), scalar2=V('lnb', ct), op0=ALU.mult, op1=ALU.add)
                yield
                S.op('dve', 'tensor_tensor', (('CB', b), fmV), ('O2',), out=O2, in0=CB, in1=Vv, op=ALU.mult)
                yield
                S.op('dve', 'tensor_tensor', ('O1', 'O2'), ('O1',), out=O1, in0=O1, in1=O2, op=ALU.add)
                yield
                pi = rot('ps', 6)
                MM(pi, psums[pi][:, 0:RT], g2a, LG1a[:, tcols], True, False, (kb, 'lora'))
                MM(pi, psums[pi][:, 0:RT], g2b, LG1b[0:32, tcols], False, True, (kb, 'lora'))
                yield
                S.op('dve', 'tensor_tensor', (('ps', pi), 'O1'), ('OTb',), out=OTb, in0=psums[pi][:, 0:RT], in1=O1, op=ALU.mult)
                yield
                for half in range(2):
                    pi = rot('ps', 6)
                    for q in range(4):
                        mo = half * 4 + q
                        MM(pi, psums[pi][:, q * 128:(q + 1) * 128], wo[:, mo * 128:(mo + 1) * 128], OTb, True, True, (kb, 'OTb'))
                    hk = tuple(('h', half * 4 + q, ht) for q in range(4))
                    S.op('dve', 'tensor_tensor', (('ps', pi),) + hk, hk, out=hT[:, half * 4:(half + 1) * 4, tcols],
                         in0=psums[pi][:].rearrange("p (q t) -> p q t", q=4), in1=hT[:, half * 4:(half + 1) * 4, tcols], op=ALU.add)
                yield

            def drive(*gens):
                gens = list(gens)
                while gens:
                    for g in list(gens):
                        try:
                            next(g)
                        except StopIteration:
                            gens.remove(g)

            prev = None
            for t in range(NRT):
                if prev is None:
                    drive(prep(t, t % 2))
                else:
                    drive(prev, prep(t, t % 2))
                prev = chunks(t, t % 2)
            drive(prev)
            if not final_pass:
                sc = Sa[par[0]]
                pi = rot('ps', 6)
                for h in range(2):
                    TR(pi, psums[pi][0:64, h * 64:(h + 1) * 64], sc[h * 64:(h + 1) * 64, 64:128], (('Sa', par[0], h),), b0=h * 64)
                    S.op('act', 'activation', (('Sa', par[0], h), 'SUMT'), ('SUMT',), out=SUMT[0:64, h, 0:64], in_=sc[h * 64:(h + 1) * 64, 0:64], func=AF.Copy)
                S.op('dve', 'tensor_copy', (('ps', pi), 'SUMT'), ('SUMT',), out=SUMT[0:64, :, 64:128], in_=psums[pi][0:64, 0:128].rearrange("p (h n) -> p h n", h=2))
                S.dma('sp', ('SUMT',), (('o_rs', ct),), out=OUT("rsumm", [16, 64, 128])[2 * ct:2 * ct + 2].rearrange("h k n -> k h n"), in_=SUMT[0:64, :, :])
                out_keys.append(('o_rs', ct))
        S.barrier()

    def final():
        xo = big[:, :DC * TT * 2].bitcast(F32).rearrange("p (c t) -> p c t", c=DC)
        outs = []
        for t in range(NT):
            ri = rms_tile(t)
            for c in range(DC):
                S.op('dve', 'scalar_tensor_tensor', (('h', c, t), ('rs', ri), 'vecs'), (('xo', c),),
                     out=xo[:, c, :], in0=hT[:, c, tsl(t)], scalar=vcol('norm_final', c),
                     in1=rs[ri][:], op0=ALU.mult, op1=ALU.mult)
            for b in range(TT // 128):
                si = rot('stage', 2)
                for half in range(2):
                    pi = rot('ps', 6)
                    for j in range(4):
                        c = half * 4 + j
                        TR(pi, psums[pi][:, j * 128:(j + 1) * 128], xo[:, c, b * 128:(b + 1) * 128], (('xo', c),))
                    S.op('act', 'activation', (('ps', pi),), (('stage', si),),
                         out=stage[si][:, half * 512:(half + 1) * 512], in_=psums[pi][:], func=AF.Copy)
                r0 = t * TT + b * 128
                key = ('yout', r0)
                S.dma('sp', (('stage', si),), (key,), out=OUT("y", [TL, D])[r0:r0 + 128, :], in_=stage[si][:])
                out_keys.append(key)

    def load_h():
        for c in range(DC):
            S.dma('sp', (), tuple(('h', c, t) for t in range(NT)), out=hT[:, c, :], in_=W("h_in")[:, c, :])

    def store_h():
        for c in range(DC):
            S.dma('sp', tuple(('h', c, t) for t in range(NT)), (('o_h', c),), out=OUT("h_out", [128, DC, TL])[:, c, :], in_=hT[:, c, :])
            out_keys.append(('o_h', c))

    if first:
        load_xT()
    else:
        load_h()
    for step in prog:
        kind = step[0]
        if kind == 'tail':
            rmsnorm_xn('norm_mix%d' % step[1])
            tail_out()
        elif kind in ('mix1', 'mix2'):
            l = step[1]
            if l % 2 == 0:
                hawk(l, kind == 'mix2')
            else:
                rwkv(l, kind == 'mix2')
        elif kind == 'ffn':
            ffn(step[1])
        elif kind == 'ple':
            ple(step[1])
        elif kind == 'final':
            final()
        elif kind == 'store_h':
            store_h()
    S.finish('sp', out_keys)
    S.emit(nc, stack)
    stack.close()
    nc._declared_inputs = set(declared)
    nc._declared_outputs = set(outs_decl)
    return nc


VOFF = {}
NVEC = 0


def _layout_vecs():
    global NVEC
    o = 0

    def add(name, n):
        nonlocal o
        VOFF[name] = o
        o += n
    add('eps', 1)
    for l in range(4):
        add('norm_mix%d' % l, 8)
        add('norm_ffn%d' % l, 8)
        add('norm_ple%d' % l, 8)
    add('norm_final', 8)
    add('one', 1)
    for j in range(2):
        for n in ('bg', 'br', 'cw0', 'cw1', 'cw2', 'cw3', 'cb', 'ba', 'bx', 'lam'):
            add('h%d_%s' % (j, n), 10)
        add('h%d_bo' % j, 8)
    add('gneps', 1)
    for j in range(2):
        for i in range(6):
            add('r%d_mu%d' % (j, i), 8)
        for n in ('w0', 'a0', 'v0', 'kk', 'ka', 'rk', 'lnw', 'lnb'):
            add('r%d_%s' % (j, n), 8)
    NVEC = o


_layout_vecs()


def colmajor(v):
    v = np.asarray(v, np.float32).reshape(-1)
    return np.ascontiguousarray(v.reshape(-1, 128).T)


def pack_vecs(inp):
    vecs = np.zeros((128, NVEC), np.float32)

    def put(name, v):
        cm = colmajor(v)
        vecs[:, VOFF[name]:VOFF[name] + cm.shape[1]] = cm
    vecs[:, VOFF['eps']] = EPS
    for l in range(4):
        put('norm_mix%d' % l, inp['norm_mix'][l])
        put('norm_ffn%d' % l, inp['norm_ffn'][l])
        put('norm_ple%d' % l, inp['norm_ple'][l])
    put('norm_final', inp['norm_final'])
    vecs[:, VOFF['one']] = 1.0
    for j in range(2):
        put('h%d_bg' % j, inp['lru_b_in'][j][:DR])
        put('h%d_br' % j, inp['lru_b_in'][j][DR:])
        for k in range(4):
            put('h%d_cw%d' % (j, k), inp['lru_conv_w'][j][k])
        put('h%d_cb' % j, inp['lru_conv_b'][j])
        put('h%d_ba' % j, inp['lru_b_gate_a'][j])
        put('h%d_bx' % j, inp['lru_b_gate_x'][j])
        put('h%d_lam' % j, inp['lru_lambda'][j])
        put('h%d_bo' % j, inp['lru_b_out'][j])
    vecs[:, VOFF['gneps']] = 64e-5
    for j in range(2):
        for i in range(6):
            put('r%d_mu%d' % (j, i), inp['rw_mu'][j][i])
        put('r%d_w0' % j, inp['rw_w0'][j])
        put('r%d_a0' % j, inp['rw_a0'][j])
        if j == 1:
            put('r%d_v0' % j, inp['rw_v0'][0])
        put('r%d_kk' % j, inp['rw_k_k'][j])
        put('r%d_ka' % j, inp['rw_k_a'][j])
        put('r%d_rk' % j, np.asarray(inp['rw_r_k'][j]).reshape(-1))
        put('r%d_lnw' % j, inp['rw_ln_w'][j])
        put('r%d_lnb' % j, inp['rw_ln_b'][j])
    return vecs


def launch(TL, prog, first, per_core, shared):
    nc = build(TL, prog, first=first)
    in_maps = []
    for c in range(NCORES):
        m = dict(shared)
        m.update(per_core[c])
        in_maps.append({k: v for k, v in m.items() if k in nc._declared_inputs})
        missing = nc._declared_inputs - set(in_maps[-1])
        assert not missing, missing
    res = run_bass_kernel_spmd(nc, in_maps, core_ids=list(range(NCORES)))
    return res.results


def run(inp, TL, layers=(0, 1, 2, 3), parts=('mix', 'ffn', 'ple')):
    f32 = lambda a: np.ascontiguousarray(np.asarray(a, np.float32))
    x = f32(inp['x']).reshape(-1, D)
    p = f32(inp['p']).reshape(4, -1, DPLE)
    shared = {
        'vecs': pack_vecs(inp), 'ident_in': np.eye(128, dtype=np.float32),
        'ffn_w_in': f32(inp['ffn_w_in']), 'ffn_w_out': f32(inp['ffn_w_out']),
        'ple_w_proj': f32(inp['ple_w_proj']), 'ple_w_gate': f32(inp['ple_w_gate']),
        'lru_w_in': f32(inp['lru_w_in']), 'lru_w_out': f32(inp['lru_w_out']),
        'lru_w_gate_a': f32(inp['lru_w_gate_a']).reshape(2, RC * 128, 128),
        'lru_w_gate_x': f32(inp['lru_w_gate_x']).reshape(2, RC * 128, 128),
    }
    shared.update(rwkv_shared(inp))
    per_core = []
    for c in range(NCORES):
        cv = np.zeros((128, 17), np.float32)
        cv[:, c] = 1.0
        if c > 0:
            cv[:, 8 + c - 1] = 1.0
            cv[:, 16] = 1.0
        per_core.append({'x': np.ascontiguousarray(x[c * TL:(c + 1) * TL]),
                         'p': np.ascontiguousarray(p[:, c * TL:(c + 1) * TL]), 'cvec': cv})

    def route_halo(res):
        for c in range(NCORES):
            per_core[c]['halo'] = res[c - 1]['tail'] if c > 0 else np.zeros((128, 32), np.float32)

    def take_h(res):
        for c in range(NCORES):
            per_core[c]['h_in'] = res[c]['h_out']

    layers = list(layers)
    mix = 'mix' in parts
    tailsteps = lambda l: ([('tail', l)] if mix else [])
    post = lambda l: ([('ffn', l)] if 'ffn' in parts else []) + ([('ple', l)] if 'ple' in parts else [])
    if not layers:
        res = launch(TL, [('final',)], True, per_core, shared)
    else:
        res = launch(TL, tailsteps(layers[0]) + [('store_h',)], True, per_core, shared)
        take_h(res)
        for li, l in enumerate(layers):
            if mix:
                route_halo(res)
                res1 = launch(TL, [('mix1', l)], False, per_core, shared)
                gather_summ(l, res1, per_core)
            nxt = (tailsteps(layers[li + 1]) + [('store_h',)]) if li + 1 < len(layers) else [('final',)]
            res = launch(TL, ([('mix2', l)] if mix else []) + post(l) + nxt, False, per_core, shared)
            if li + 1 < len(layers):
                take_h(res)
                keep_state(l, res, per_core)
    y = np.concatenate([res[c]['y'] for c in range(NCORES)], axis=0)
    return y.reshape(1, NCORES * TL, D).astype(np.float32)


def gather_summ(l, res1, per_core):
    if l % 2 == 0:
        g = np.ascontiguousarray(np.stack([res1[c]['summ'] for c in range(NCORES)], axis=1))
        for c in range(NCORES):
            per_core[c]['gsumm'] = g
    else:
        g = np.ascontiguousarray(np.stack([res1[c]['rsumm'] for c in range(NCORES)], axis=0))
        for c in range(NCORES):
            per_core[c]['grsumm'] = g


def keep_state(l, res, per_core):
    if l == 1:
        for c in range(NCORES):
            per_core[c]['vfirst'] = res[c]['vfirst_out']


def rwkv_shared(inp):
    f32 = lambda a: np.ascontiguousarray(np.asarray(a, np.float32))
    rc = np.zeros((128, 640), np.float32)
    i = np.arange(64)
    su = (i[:, None] < i[None, :]).astype(np.float32)
    iu = (i[:, None] <= i[None, :]).astype(np.float32)
    sl = (i[:, None] > i[None, :]).astype(np.float32)
    rc[0:64, 0:320] = np.concatenate([su, iu, su, iu, sl], axis=1)
    rc[64:128, 0:320] = rc[0:64, 0:320]
    rc[0:64, 576:640] = np.eye(64, dtype=np.float32)
    rc[64:128, 576:640] = np.eye(64, dtype=np.float32)
    rc[0:64, 320:384] = 1.0
    rc[64:128, 384:448] = 1.0
    rc[:, 448:576] = 1.0
    rc[:, 448] = 0.0
    rc[:, 448 + 64] = 0.0
    d = {k: f32(inp[k]) for k in ('rw_w_rkv', 'rw_w_o', 'rw_w1', 'rw_w2', 'rw_a1', 'rw_a2', 'rw_v1', 'rw_v2', 'rw_g1', 'rw_g2')}
    d['rconst'] = rc
    return d


def kernel(**inputs):
    return run(inputs, 2048)
```

```python
import contextlib
import numpy as np
import concourse.bass as bass
import concourse.mybir as mybir
from concourse.bass_utils import run_bass_kernel_spmd

F32 = mybir.dt.float32
BF16 = mybir.dt.bfloat16
AF = mybir.ActivationFunctionType
ALU = mybir.AluOpType
NCORES = 8
D, DC, DR, RC, DFF, FC, DPLE = 1024, 8, 1280, 10, 2816, 22, 256
HALO = 4
EPS = 1e-6


class Sched:
    ENGS = ('pe', 'act', 'dve', 'pool', 'sp')

    def __init__(self):
        self.ops = {e: [] for e in self.ENGS}
        self.cnt = {e: 0 for e in self.ENGS}
        self.waited = {e: {} for e in self.ENGS}
        self.lastw = {}
        self.readers = {}
        self.NS = 8
        self.dma_uses = {}
        self.dma_n = {q: 0 for q in self.ENGS}
        self.ncc = 0
        self.semkeys = set()

    def _deps(self, reads, writes):
        deps = []
        for k in reads:
            t = self.lastw.get(k)
            if t:
                deps.append(t)
        for k in writes:
            t = self.lastw.get(k)
            if t:
                deps.append(t)
            deps.extend(self.readers.get(k, {}).items())
        return deps

    def _waits(self, eng, deps):
        w = {}
        for (sk, v) in deps:
            if eng == 'pe' and sk == ('eng', 'pe'):
                continue
            if v > self.waited[eng].get(sk, 0) and v > w.get(sk, 0):
                w[sk] = v
        for sk, v in w.items():
            self.waited[eng][sk] = v
        return list(w.items())

    def _commit(self, tok, reads, writes):
        for k in writes:
            self.lastw[k] = tok
            self.readers[k] = {}
        for k in reads:
            if k not in writes:
                r = self.readers.setdefault(k, {})
                if tok[1] > r.get(tok[0], 0):
                    r[tok[0]] = tok[1]

    def op(self, eng, fn, reads=(), writes=(), *args, **kwargs):
        fn = (fn, args, kwargs)
        waits = self._waits(eng, self._deps(reads, writes))
        self.cnt[eng] += 1
        tok = (('eng', eng), self.cnt[eng])
        self.semkeys.add(tok[0])
        self.ops[eng].append((waits, fn, tok, 1))
        self._commit(tok, reads, writes)

    def dma(self, q, reads=(), writes=(), **kwargs):
        fn = ('dma_start', (), kwargs)
        slot = self.dma_n[q] % self.NS
        self.dma_n[q] += 1
        sk = ('dma', q, slot)
        self.semkeys.add(sk)
        uses = self.dma_uses.get(sk, 0)
        deps = self._deps(reads, writes)
        if uses:
            deps.append((sk, 16 * uses))
        waits = self._waits(q, deps)
        self.dma_uses[sk] = uses + 1
        tok = (sk, 16 * (uses + 1))
        self.ops[q].append((waits, fn, tok, 16))
        self._commit(tok, reads, writes)

    def cc(self, reads=(), writes=(), *args, **kwargs):
        fn = ('collective_compute', args, kwargs)
        sk = ('cc', self.ncc)
        self.ncc += 1
        self.semkeys.add(sk)
        waits = self._waits('pool', self._deps(reads, writes))
        tok = (sk, 1)
        self.ops['pool'].append((waits, fn, tok, None))
        self._commit(tok, reads, writes)

    def barrier(self):
        toks = [(('eng', e), self.cnt[e]) for e in self.ENGS if self.cnt[e]]
        toks += [(sk, 16 * u) for sk, u in self.dma_uses.items()]
        cct = [(('cc', i), 1) for i in range(self.ncc)]
        for e in self.ENGS:
            waits = self._waits(e, toks + (cct if e == 'pool' else []))
            if waits:
                self.ops[e].append((waits, None, None, 0))

    def finish(self, eng, keys):
        waits = self._waits(eng, self._deps(keys, ()))
        self.ops[eng].append((waits, None, None, 0))

    def emit(self, nc, stack):
        sems = {}
        for i, sk in enumerate(sorted(self.semkeys, key=str)):
            sems[sk] = stack.enter_context(nc.semaphore("s%d" % i))
        block = stack.enter_context(nc.Block())
        engmap = {'pe': block.tensor, 'act': block.scalar, 'dve': block.vector,
                  'pool': block.gpsimd, 'sp': block.sync}
        for e in self.ENGS:
            def body(eng, ops=self.ops[e]):
                for waits, fn, tok, inc in ops:
                    for sk, v in waits:
                        eng.wait_ge(sems[sk], v)
                    if fn is None:
                        continue
                    try:
                        ins = getattr(eng, fn[0])(*fn[1], **fn[2])
                    except Exception:
                        print("EMIT FAIL", fn[0], [str(a)[:160] for a in fn[1]], {k: str(v)[:160] for k, v in fn[2].items()})
                        raise
                    if inc is None:
                        ins.then_inc(sems[tok[0]])
                    else:
                        ins.then_inc(sems[tok[0]], inc)
            engmap[e](body)


def build(TL, prog, first=False):
    nc = bass.Bass("TRN2", target_bir_lowering=False)
    S = Sched()
    stack = contextlib.ExitStack()
    TT = min(512, TL)
    NT = TL // TT
    ST = min(512, TL)
    NST = TL // ST
    TPS = ST // TT
    NB = TL // 128

    declared = {}
    SHAPES = {
        "x": [TL, D], "p": [4, TL, DPLE], "vecs": [128, NVEC], "cvec": [128, 17], "ident_in": [128, 128],
        "ffn_w_in": [4, D, 2 * DFF], "ffn_w_out": [4, DFF, D], "ple_w_proj": [4, DPLE, D], "ple_w_gate": [4, D, D],
        "lru_w_in": [2, D, 2 * DR], "lru_w_gate_a": [2, RC * 128, 128], "lru_w_gate_x": [2, RC * 128, 128],
        "lru_w_out": [2, DR, D],
    }

    def W(name):
        if name not in declared:
            declared[name] = nc.dram_tensor(name, list(SHAPES[name]), F32, kind="ExternalInput").ap()
        return declared[name]

    outs_decl = {}

    def OUT(name, shape):
        if name not in outs_decl:
            outs_decl[name] = nc.dram_tensor(name, list(shape), F32, kind="ExternalOutput").ap()
        return outs_decl[name]

    SHAPES.update({"h_in": [128, DC, TL], "halo": [128, 32], "gsumm": [128, NCORES, 32],
                   "rw_w_rkv": [2, 3, D, D], "rw_w_o": [2, D, D], "rw_w1": [2, D, 64], "rw_w2": [2, 64, D],
                   "rw_a1": [2, D, 64], "rw_a2": [2, 64, D], "rw_v1": [1, D, 32], "rw_v2": [1, 32, D],
                   "rw_g1": [2, D, 160], "rw_g2": [2, 160, D], "rconst": [128, 704],
                   "grsumm": [NCORES, 16, 64, 128], "vfirst": [128, DC, TL]})
    out_keys = []

    def sb(name, shape, dt=F32):
        return stack.enter_context(nc.sbuf_tensor(name, list(shape), dt))

    hT = sb("hT", [128, DC, TL])
    xn = sb("xn", [128, DC, HALO + TL], BF16)
    vecs = sb("vecs_sb", [128, NVEC])
    cvec = sb("cvec_sb", [128, 17])
    gath = sb("gath", [128, NCORES, 32])
    smal = sb("smal", [128, 128])
    ident = sb("ident", [128, 128])
    ones_bf = sb("ones_bf", [128, 128], BF16)
    NWB = 2
    wbufs = [sb("wb%d" % i, [128, 4096], BF16) for i in range(NWB)]
    stage = [sb("stage%d" % i, [128, D]) for i in range(2)]
    sq = sb("sq", [128, DC, TT], BF16)
    rs = [sb("rs%d" % i, [128, TT]) for i in range(2)]
    tmpA = [sb("tmpA%d" % i, [128, TT]) for i in range(3)]
    ARENA = max(FC * ST * 2 + 2 * TL * 2, 5 * (TL + 4) * 4 + 2 * TL * 2 + 64, 5 * TL * 2 + 4 * 11400)
    arena = sb("arena", [128, ARENA // 2], BF16)
    big = arena[:, :FC * ST]
    pT = arena[:, FC * ST:FC * ST + 2 * TL].rearrange("p (k t) -> p k t", k=2)
    psums = [stack.enter_context(nc.psum_tensor("ps%d" % i, [128, 512], F32)) for i in range(8)]
    st = {'ps': 0, 'wb': 0, 'stage': 0, 'rs': 0, 'tmpA': 0}

    def rot(name, n):
        i = st[name]
        st[name] = (i + 1) % n
        return i

    def vcol(name, c=0):
        o = VOFF[name] + c
        return vecs[:, o:o + 1]

    S.dma('sp', (), ('vecs',), out=vecs[:], in_=W("vecs"))
    S.dma('sp', (), ('cvec',), out=cvec[:], in_=W("cvec"))
    S.dma('sp', (), ('ident',), out=ident[:], in_=W("ident_in"))
    S.op('dve', 'memset', (), ('ones',), ones_bf[:], 1.0)

    def tsl(t):
        return slice(t * TT, (t + 1) * TT)

    def xsl(t, sh=0):
        return slice(HALO + t * TT - sh, HALO + (t + 1) * TT - sh)

    def wview(i, kc, mb):
        return wbufs[i][:, :kc * mb].rearrange("p (k m) -> p k m", k=kc)

    def wload(i, kc, mb, src, m0=0, mw=None):
        mw = mb if mw is None else mw
        dst = wview(i, kc, mb)[:, :, m0:m0 + mw]
        S.dma('pool', (), (('wb', i),), out=dst, in_=src.rearrange("(k p) m -> p k m", p=128))

    def MM(pi, out, lhsT, rhs, start, stop, reads):
        S.op('pe', 'matmul', reads, (('ps', pi),), out, lhsT, rhs, start=start, stop=stop)

    def TR(pi, out, in_, reads, b0=0):
        np_ = in_.shape[0]
        S.op('pe', 'transpose', tuple(reads) + ('ident',), (('ps', pi),), out, in_, ident[b0:b0 + np_, b0:b0 + np_])

    def load_xT():
        for tb in range(NB):
            si = rot('stage', 2)
            S.dma('sp', (), (('stage', si),), out=stage[si][:], in_=W("x")[tb * 128:(tb + 1) * 128, :])
            for half in range(2):
                pi = rot('ps', 6)
                for j in range(4):
                    c = half * 4 + j
                    TR(pi, psums[pi][:, j * 128:(j + 1) * 128], stage[si][:, c * 128:(c + 1) * 128], (('stage', si),))
                S.op('dve', 'tensor_copy', (('ps', pi),), tuple(('h', half * 4 + j, tb * 128 // TT) for j in range(4)),
                     out=hT[:, half * 4:(half + 1) * 4, tb * 128:(tb + 1) * 128],
                     in_=psums[pi][:].rearrange("p (j t) -> p j t", j=4))

    def rms_tile(t):
        S.op('act', 'activation', tuple(('h', c, t) for c in range(DC)), ('sq',),
             out=sq[:], in_=hT[:, :, tsl(t)], func=AF.Square)
        pi = rot('ps', 6)
        for c in range(DC):
            MM(pi, psums[pi][:, :TT], ones_bf[:], sq[:, c, :], c == 0, c == DC - 1, ('sq', 'ones'))
        ri = rot('rs', 2)
        S.op('act', 'activation', (('ps', pi), 'vecs'), (('rs', ri),),
             out=rs[ri][:], in_=psums[pi][:, :TT], func=AF.Sqrt, bias=vcol('eps'), scale=1.0 / D)
        S.op('dve', 'reciprocal', (('rs', ri),), (('rs', ri),), out=rs[ri][:], in_=rs[ri][:])
        return ri

    def rmsnorm_xn(gname):
        for t in range(NT):
            ri = rms_tile(t)
            for c in range(DC):
                S.op('dve', 'scalar_tensor_tensor', (('h', c, t), ('rs', ri), 'vecs'), (('xn', c, t),),
                     out=xn[:, c, xsl(t)], in0=hT[:, c, tsl(t)], scalar=vcol(gname, c),
                     in1=rs[ri][:], op0=ALU.mult, op1=ALU.mult)

    def ffn(l):
        rmsnorm_xn('norm_ffn%d' % l)
        hid = big[:, :FC * ST].rearrange("p (f t) -> p f t", f=FC)
        for s_ in range(NST):
            for g0 in range(0, FC, 2):
                gw = min(2, FC - g0)
                wi = rot('wb', NWB)
                wload(wi, DC, 512, W("ffn_w_in")[l][:, g0 * 128:(g0 + gw) * 128], 0, gw * 128)
                wload(wi, DC, 512, W("ffn_w_in")[l][:, DFF + g0 * 128:DFF + (g0 + gw) * 128], 256, gw * 128)
                wv = wview(wi, DC, 512)
                for j in range(gw):
                    f = g0 + j
                    for tt in range(TPS):
                        t = s_ * TPS + tt
                        pa, pb = rot('ps', 6), rot('ps', 6)
                        for (pi, off) in ((pa, 0), (pb, 256)):
                            for c in range(DC):
                                MM(pi, psums[pi][:, :TT], wv[:, c, off + j * 128:off + (j + 1) * 128], xn[:, c, xsl(t)],
                                   c == 0, c == DC - 1, (('wb', wi), ('xn', c, t)))
                        ai = rot('tmpA', 3)
                        S.op('act', 'activation', (('ps', pa),), (('tmpA', ai),),
                             out=tmpA[ai][:], in_=psums[pa][:, :TT], func=AF.Silu)
                        S.op('dve', 'tensor_tensor', (('ps', pb), ('tmpA', ai)), (('big', f, tt),),
                             out=hid[:, f, tt * TT:(tt + 1) * TT], in0=psums[pb][:, :TT], in1=tmpA[ai][:], op=ALU.mult)
            for m0 in range(0, DC, 1):
                wi = rot('wb', NWB)
                wload(wi, FC, 128, W("ffn_w_out")[l][:, m0 * 128:(m0 + 1) * 128])
                wv = wview(wi, FC, 128)
                for j in range(1):
                    mo = m0 + j
                    for tt in range(TPS):
                        t = s_ * TPS + tt
                        pi = rot('ps', 6)
                        for f in range(FC):
                            MM(pi, psums[pi][:, :TT], wv[:, f, j * 128:(j + 1) * 128], hid[:, f, tt * TT:(tt + 1) * TT],
                               f == 0, f == FC - 1, (('wb', wi), ('big', f, tt)))
                        S.op('dve', 'tensor_tensor', (('ps', pi), ('h', mo, t)), (('h', mo, t),),
                             out=hT[:, mo, tsl(t)], in0=psums[pi][:, :TT], in1=hT[:, mo, tsl(t)], op=ALU.add)

    def ple(l):
        rmsnorm_xn('norm_ple%d' % l)
        for tb in range(NB):
            si = rot('stage', 2)
            S.dma('sp', (), (('stage', si),), out=stage[si][:, :DPLE], in_=W("p")[l, tb * 128:(tb + 1) * 128, :])
            pi = rot('ps', 6)
            for j in range(2):
                TR(pi, psums[pi][:, j * 128:(j + 1) * 128], stage[si][:, j * 128:(j + 1) * 128], (('stage', si),))
            S.op('act', 'activation', (('ps', pi),), (('pT', tb * 128 // TT),),
                 out=pT[:, :, tb * 128:(tb + 1) * 128], in_=psums[pi][:, :256].rearrange("p (j t) -> p j t", j=2), func=AF.Copy)
        S.barrier()
        wp = 'pleproj'
        wpv = big[:, 0:2048].rearrange("p (k m) -> p k m", k=2)
        S.dma('pool', (), (('wb', wp),), out=wpv, in_=W("ple_w_proj")[l].rearrange("(k p) m -> p k m", p=128))
        for mg in range(2):
            wi = rot('wb', NWB)
            wload(wi, DC, 512, W("ple_w_gate")[l][:, mg * 512:(mg + 1) * 512])
            wv = wview(wi, DC, 512)
            for j in range(4):
                mo = mg * 4 + j
                for t in range(NT):
                    pa, pb = rot('ps', 6), rot('ps', 6)
                    for kc in range(2):
                        MM(pa, psums[pa][:, :TT], wpv[:, kc, mo * 128:(mo + 1) * 128], pT[:, kc, tsl(t)],
                           kc == 0, kc == 1, (('wb', wp), ('pT', t)))
                    for c in range(DC):
                        MM(pb, psums[pb][:, :TT], wv[:, c, j * 128:(j + 1) * 128], xn[:, c, xsl(t)],
                           c == 0, c == DC - 1, (('wb', wi), ('xn', c, t)))
                    ai, bi = rot('tmpA', 3), rot('tmpA', 3)
                    S.op('act', 'activation', (('ps', pb),), (('tmpA', ai),),
                         out=tmpA[ai][:], in_=psums[pb][:, :TT], func=AF.Sigmoid)
                    S.op('dve', 'tensor_tensor', (('ps', pa), ('tmpA', ai)), (('tmpA', bi),),
                         out=tmpA[bi][:], in0=psums[pa][:, :TT], in1=tmpA[ai][:], op=ALU.mult)
                    S.op('dve', 'tensor_tensor', (('tmpA', bi), ('h', mo, t)), (('h', mo, t),),
                         out=hT[:, mo, tsl(t)], in0=hT[:, mo, tsl(t)], in1=tmpA[bi][:], op=ALU.add)

    def load_halo():
        S.dma('sp', (), ('smal',), out=smal[:, :DC * HALO], in_=W("halo"))
        S.op('dve', 'tensor_copy', ('smal',), ('xnhalo',), out=xn[:, :, 0:HALO],
             in_=smal[:, :DC * HALO].rearrange("p (c k) -> p c k", c=DC))

    def tail_out():
        tl = smal[:, :DC * HALO].rearrange("p (c k) -> p c k", c=DC)
        S.op('dve', 'tensor_copy', tuple(('xn', c, NT - 1) for c in range(DC)) + ('smal',), ('smal',), out=tl, in_=xn[:, :, TL:TL + HALO])
        S.dma('sp', ('smal',), ('o_tail',), out=OUT("tail", [128, 32]), in_=smal[:, :32])
        out_keys.append('o_tail')

    def hawk(l, final_pass):
        j = l // 2
        V = lambda n, c=0: vcol('h%d_%s' % (j, n), c)
        rmsnorm_xn('norm_mix%d' % l)
        load_halo()
        S.barrier()
        fa = arena[:].bitcast(F32)
        W4 = TL + 4
        recb = fa[:, 0:W4]
        cv = fa[:, W4:W4 + TL]
        Abuf = fa[:, 2 * W4:2 * W4 + TL]
        Hl = fa[:, 3 * W4:3 * W4 + TL]
        o16 = 5 * W4 * 2
        cvb = arena[:, o16:o16 + TL]
        P1 = arena[:, o16 + TL:o16 + 2 * TL]
        c8 = smal[:, 64:74]
        c16 = smal[:, 74:84]
        rsum = smal[:, 84:84 + NT]
        rtot = smal[:, 92:93]
        hin = smal[:, 96:106]
        carry = smal[:, 106:116]
        S.op('act', 'activation', ('vecs',), ('c8',), out=c8, in_=vecs[:, VOFF['h%d_lam' % j]:VOFF['h%d_lam' % j] + RC], func=AF.Exp, scale=-1.0)
        S.op('act', 'activation', ('c8', 'vecs'), ('c8',), out=c8, in_=c8, func=AF.Ln, bias=vcol('one'))
        S.op('dve', 'tensor_scalar', ('c8',), ('c16',), out=c16, in0=c8, scalar1=-16.0, scalar2=None, op0=ALU.mult)
        S.op('dve', 'tensor_scalar', ('c8', 'c16'), ('c8',), out=c8, in0=c8, scalar1=-8.0, scalar2=None, op0=ALU.mult)
        if True:
            if final_pass:
                S.dma('sp', (), ('gath',), out=gath[:], in_=W("gsumm"))
                S.op('dve', 'memset', (), ('carry',), carry, 0.0)
                S.op('dve', 'memset', (), ('hin',), hin, 0.0)
                for r in range(NCORES):
                    S.op('dve', 'scalar_tensor_tensor', ('carry', 'cvec', 'hin'), ('hin',),
                         out=hin, in0=carry, scalar=cvec[:, r:r + 1], in1=hin, op0=ALU.mult, op1=ALU.add)
                    if r < NCORES - 1:
                        S.op('dve', 'tensor_tensor', ('carry', 'gath'), ('carry',), out=carry, in0=carry, in1=gath[:, r, RC:2 * RC], op=ALU.mult)
                        S.op('dve', 'tensor_tensor', ('carry', 'gath'), ('carry',), out=carry, in0=carry, in1=gath[:, r, 0:RC], op=ALU.add)
            for ct in range(RC):
                wi = rot('wb', NWB)
                wb_ = wbufs[wi]
                wrec = wb_[:, 0:1024].rearrange("p (k m) -> p k m", k=DC)
                wgat = wb_[:, 1024:2048].rearrange("p (k m) -> p k m", k=DC)
                wa = wb_[:, 2048:2176]
                wx = wb_[:, 2176:2304]
                wo = wb_[:, 2304:3328]
                wkey = ('wb', wi)
                S.dma('pool', (), (wkey,), out=wrec, in_=W("lru_w_in")[j][:, DR + ct * 128:DR + (ct + 1) * 128].rearrange("(k p) m -> p k m", p=128))
                S.dma('pool', (), (wkey,), out=wa, in_=W("lru_w_gate_a")[j][ct * 128:(ct + 1) * 128, :])
                S.dma('pool', (), (wkey,), out=wx, in_=W("lru_w_gate_x")[j][ct * 128:(ct + 1) * 128, :])
                if final_pass:
                    S.dma('pool', (), (wkey,), out=wgat, in_=W("lru_w_in")[j][:, ct * 128:(ct + 1) * 128].rearrange("(k p) m -> p k m", p=128))
                    S.dma('pool', (), (wkey,), out=wo, in_=W("lru_w_out")[j][ct * 128:(ct + 1) * 128, :])
                pi = rot('ps', 6)
                for c in range(DC):
                    MM(pi, psums[pi][:, :HALO], wrec[:, c, :], xn[:, c, 0:HALO], c == 0, c == DC - 1, (wkey, 'xnhalo'))
                S.op('dve', 'tensor_scalar', (('ps', pi), 'vecs', 'cvec'), ('recb',), out=recb[:, 0:HALO], in0=psums[pi][:, :HALO],
                     scalar1=V('br', ct), scalar2=cvec[:, 16:17], op0=ALU.add, op1=ALU.mult)
                for t in range(NT):
                    pi = rot('ps', 6)
                    for c in range(DC):
                        MM(pi, psums[pi][:, :TT], wrec[:, c, :], xn[:, c, xsl(t)], c == 0, c == DC - 1, (wkey, ('xn', c, t)))
                    S.op('act', 'activation', (('ps', pi), 'vecs'), ('recb',), out=recb[:, HALO + t * TT:HALO + (t + 1) * TT],
                         in_=psums[pi][:, :TT], func=AF.Identity, bias=V('br', ct))
                S.op('dve', 'tensor_scalar', ('recb', 'vecs'), ('cv',), out=cv, in0=recb[:, 1:1 + TL],
                     scalar1=V('cw0', ct), scalar2=V('cb', ct), op0=ALU.mult, op1=ALU.add)
                for k in range(1, 4):
                    S.op('dve', 'scalar_tensor_tensor', ('recb', 'vecs', 'cv'), ('cv',), out=cv, in0=recb[:, 1 + k:1 + k + TL],
                         scalar=V('cw%d' % k, ct), in1=cv, op0=ALU.mult, op1=ALU.add)
                S.op('act', 'activation', ('cv',), ('cvb',), out=cvb, in_=cv, func=AF.Copy)
                for t in range(NT):
                    pa, pb = rot('ps', 6), rot('ps', 6)
                    MM(pa, psums[pa][:, :TT], wa, cvb[:, tsl(t)], True, True, (wkey, 'cvb'))
                    MM(pb, psums[pb][:, :TT], wx, cvb[:, tsl(t)], True, True, (wkey, 'cvb'))
                    ri_, ii_, mi_ = rot('tmpA', 3), rot('tmpA', 3), rot('tmpA', 3)
                    S.op('act', 'activation', (('ps', pa), 'vecs'), (('tmpA', ri_),), out=tmpA[ri_][:], in_=psums[pa][:, :TT],
                         func=AF.Sigmoid, bias=V('ba', ct))
                    S.op('act', 'activation', (('ps', pb), 'vecs'), (('tmpA', ii_),), out=tmpA[ii_][:], in_=psums[pb][:, :TT],
                         func=AF.Sigmoid, bias=V('bx', ct))
                    S.op('dve', 'tensor_tensor', ('cv', ('tmpA', ii_)), ('cv',), out=cv[:, tsl(t)], in0=cv[:, tsl(t)], in1=tmpA[ii_][:], op=ALU.mult)
                    S.op('act', 'activation', (('tmpA', ri_), 'c8'), ('Abuf',), out=Abuf[:, tsl(t)], in_=tmpA[ri_][:],
                         func=AF.Exp, scale=c8[:, ct:ct + 1])
                    S.op('act', 'activation', (('tmpA', ri_), 'c16'), (('tmpA', mi_),), out=tmpA[mi_][:], in_=tmpA[ri_][:],
                         func=AF.Exp, scale=c16[:, ct:ct + 1])
                    if not final_pass:
                        S.op('dve', 'tensor_reduce', (('tmpA', ri_),), ('rsum',), out=rsum[:, t:t + 1], in_=tmpA[ri_][:], axis=mybir.AxisListType.X, op=ALU.add)
                    S.op('act', 'activation', (('tmpA', mi_), 'vecs'), (('tmpA', mi_),), out=tmpA[mi_][:], in_=tmpA[mi_][:],
                         func=AF.Sqrt, scale=-1.0, bias=vcol('one'))
                    S.op('dve', 'tensor_tensor', ('cv', ('tmpA', mi_)), ('cv',), out=cv[:, tsl(t)], in0=cv[:, tsl(t)], in1=tmpA[mi_][:], op=ALU.mult)
                S.op('dve', 'tensor_tensor_scan', ('Abuf', 'cv', 'hin'), ('Hl',), out=Hl, data0=Abuf, data1=cv,
                     initial=(hin[:, ct:ct + 1] if final_pass else 0.0), op0=ALU.mult, op1=ALU.add)
                if not final_pass:
                    S.op('dve', 'tensor_copy', ('Hl',), ('summ',), out=smal[:, ct:ct + 1], in_=Hl[:, TL - 1:TL])
                    S.op('dve', 'tensor_reduce', ('rsum',), ('rtot',), out=rtot, in_=rsum, axis=mybir.AxisListType.X, op=ALU.add)
                    S.op('act', 'activation', ('rtot', 'c8'), ('summ',), out=smal[:, RC + ct:RC + ct + 1], in_=rtot, func=AF.Exp, scale=c8[:, ct:ct + 1])
                    continue
                for t in range(NT):
                    pi = rot('ps', 6)
                    for c in range(DC):
                        MM(pi, psums[pi][:, :TT], wgat[:, c, :], xn[:, c, xsl(t)], c == 0, c == DC - 1, (wkey, ('xn', c, t)))
                    gi = rot('tmpA', 3)
                    S.op('act', 'activation', (('ps', pi), 'vecs'), (('tmpA', gi),), out=tmpA[gi][:], in_=psums[pi][:, :TT],
                         func=AF.Gelu_apprx_tanh, bias=V('bg', ct))
                    S.op('dve', 'tensor_tensor', ('Hl', ('tmpA', gi)), ('P1',), out=P1[:, tsl(t)], in0=Hl[:, tsl(t)], in1=tmpA[gi][:], op=ALU.mult)
                    for mo in range(DC):
                        pi = rot('ps', 6)
                        MM(pi, psums[pi][:, :TT], wo[:, mo * 128:(mo + 1) * 128], P1[:, tsl(t)], True, True, (wkey, 'P1'))
                        if ct == 0:
                            S.op('dve', 'scalar_tensor_tensor', (('ps', pi), ('h', mo, t), 'vecs'), (('h', mo, t),), out=hT[:, mo, tsl(t)],
                                 in0=psums[pi][:, :TT], scalar=V('bo', mo), in1=hT[:, mo, tsl(t)], op0=ALU.add, op1=ALU.add)
                        else:
                            S.op('dve', 'tensor_tensor', (('ps', pi), ('h', mo, t)), (('h', mo, t),), out=hT[:, mo, tsl(t)],
                                 in0=psums[pi][:, :TT], in1=hT[:, mo, tsl(t)], op=ALU.add)
        if not final_pass:
            S.dma('sp', ('summ',), ('o_summ',), out=OUT("summ", [128, 32]), in_=smal[:, :32])
            out_keys.append('o_summ')
        S.barrier()

    def rwkv(l, final_pass):
        j = l // 2
        RT = 128
        NRT = TL // RT
        V = lambda n, c=0: vcol('r%d_%s' % (j, n), c)
        rmsnorm_xn('norm_mix%d' % l)
        load_halo()
        S.barrier()
        off = [0]

        def A32(n):
            o = off[0]
            off[0] += 2 * n
            assert off[0] <= ARENA // 2, (off[0], ARENA // 2)
            return arena[:, o:o + 2 * n].bitcast(F32)

        def A16(n):
            o = off[0]
            off[0] += n
            assert off[0] <= ARENA // 2, (off[0], ARENA // 2)
            return arena[:, o:o + n]

        LW1, LA1, LV1, LG1a, LG1b = A16(TL), A16(TL), A16(TL), A16(TL), A16(TL)
        WS = A32(1280)
        rc = A32(704)
        maskA, bdones, cmask, I2 = rc[:, 0:384], rc[:, 384:512], rc[:, 512:640], rc[:, 640:704]
        omu = A32(48)
        (R, K, LW, ALR, CUM, WC, WINV, WPRV, KKN, T1, T2) = [A32(RT) for _ in range(11)]
        Vv2, KT2, BT2, BH2, KH2, CB2 = [[A32(RT) for _ in range(2)] for _ in range(6)]
        AR2 = [A32(2 * RT).rearrange("p (a t) -> p a t", a=2) for _ in range(2)]
        WCE2 = [A32(2) for _ in range(2)]
        TM = A32(512)
        GMa = A32(768).rearrange("p (h n) -> p h n", h=2)
        GMb = A32(512).rearrange("p (h n) -> p h n", h=2)
        NL = [A32(512).rearrange("p (h n) -> p h n", h=2) for _ in range(2)]
        Tm = [A32(256).rearrange("p (h n) -> p h n", h=2) for _ in range(2)]
        Zs, Aps, Vp = A32(128), A32(128), A32(128)
        Us = [A32(128) for _ in range(2)]
        Sa = [A32(128) for _ in range(2)]
        Ytm, YN, O1, O2, STt = A32(128), A32(128), A32(128), A32(128), A32(16)
        G8 = [A32(256).rearrange("p (h n) -> p h n", h=2) for _ in range(2)]
        carry, Sin = [A32(64) for _ in range(2)], [A32(64) for _ in range(2)]
        SUMT = A32(256).rearrange("p (h n) -> p h n", h=2)
        OTb = A16(128)
        i64 = ident[0:64, 0:64]
        cnt = {'TM': 0, 'us': 0, 'g8': 0}

        S.dma('sp', (), ('rc',), out=rc, in_=W("rconst"))
        mu0 = VOFF['r%d_mu0' % j]
        S.op('dve', 'tensor_scalar', ('vecs',), ('omu',), out=omu, in0=vecs[:, mu0:mu0 + 48], scalar1=-1.0, scalar2=1.0, op0=ALU.mult, op1=ALU.add)

        def mixed_weights(src, m, mi, dst1, dst2, key):
            wsv = WS[:, :DC * m].rearrange("p (k m) -> p k m", k=DC)
            S.dma('sp', (), ('WS',), out=wsv, in_=src.rearrange("(k p) m -> p k m", p=128))
            for c in range(DC):
                S.op('dve', 'tensor_scalar', ('WS', 'omu'), (key,), out=dst1[:, c, :], in0=wsv[:, c, :], scalar1=omu[:, mi * 8 + c:mi * 8 + c + 1], scalar2=None, op0=ALU.mult)
                S.op('pool', 'tensor_scalar', ('WS', 'vecs'), (key,), out=dst2[:, c, :], in0=wsv[:, c, :], scalar1=vecs[:, mu0 + mi * 8 + c:mu0 + mi * 8 + c + 1], scalar2=None, op0=ALU.mult)

        def proj16(pi, out, w1v, w2v, m0, m1, cols0, n, wkey):
            for c in range(DC):
                MM(pi, out, w1v[:, c, m0:m1], xn[:, c, HALO + cols0:HALO + cols0 + n], c == 0, False, (wkey, ('xn', c, cols0 // TT), 'xnhalo'))
            for c in range(DC):
                MM(pi, out, w2v[:, c, m0:m1], xn[:, c, HALO + cols0 - 1:HALO + cols0 + n - 1], False, c == DC - 1,
                   (wkey, ('xn', c, cols0 // TT), ('xn', c, max(cols0 - 1, 0) // TT), 'xnhalo'))

        lora_list = [('rw_w1', 64, 1, [(0, 64, LW1, AF.Tanh)]), ('rw_a1', 64, 4, [(0, 64, LA1, AF.Copy)]),
                     ('rw_g1', 160, 5, [(0, 128, LG1a, AF.Sigmoid), (128, 160, LG1b, AF.Sigmoid)])]
        if j == 1:
            lora_list.append(('rw_v1', 32, 3, [(0, 32, LV1, AF.Copy)]))
        for (wn, m, mi, outs) in lora_list:
            wi = rot('wb', NWB)
            wkey = ('wb', wi)
            d1 = wbufs[wi][:, 0:DC * m].rearrange("p (k m) -> p k m", k=DC)
            d2 = wbufs[wi][:, DC * m:2 * DC * m].rearrange("p (k m) -> p k m", k=DC)
            src = W(wn)[0 if wn == 'rw_v1' else j]
            mixed_weights(src, m, mi, d1, d2, wkey)
            for t in range(NT):
                for (m0, m1, dst, fn) in outs:
                    pi = rot('ps', 6)
                    proj16(pi, psums[pi][0:m1 - m0, :TT], d1, d2, m0, m1, t * TT, TT, wkey)
                    S.op('act', 'activation', (('ps', pi),), ('lora',), out=dst[0:m1 - m0, tsl(t)], in_=psums[pi][0:m1 - m0, :TT], func=fn)

        for ct in range(DC):
            wa_i, wb_i = rot('wb', NWB), rot('wb', NWB)
            ka, kb = ('wb', wa_i), ('wb', wb_i)
            bufa, bufb = wbufs[wa_i], wbufs[wb_i]
            wv_ = lambda buf, o: buf[:, o:o + 1024].rearrange("p (k m) -> p k m", k=DC)
            r1, r2, k1, k2 = wv_(bufa, 0), wv_(bufa, 1024), wv_(bufa, 2048), wv_(bufa, 3072)
            v1_, v2_ = wv_(bufb, 0), wv_(bufb, 1024)
            w2c, a2c, v2c = bufb[0:64, 2048:2176], bufb[0:64, 2176:2304], bufb[0:32, 2304:2432]
            g2a, g2b = bufb[:, 2432:2560], bufb[0:32, 2560:2688]
            wo = bufb[:, 2688:3712]
            cs_ = slice(ct * 128, (ct + 1) * 128)
            mixed_weights(W("rw_w_rkv")[j, 0][:, cs_], 128, 0, r1, r2, ka)
            mixed_weights(W("rw_w_rkv")[j, 1][:, cs_], 128, 2, k1, k2, ka)
            mixed_weights(W("rw_w_rkv")[j, 2][:, cs_], 128, 3, v1_, v2_, kb)
            S.dma('pool', (), (kb,), out=w2c, in_=W("rw_w2")[j][:, cs_])
            S.dma('pool', (), (kb,), out=a2c, in_=W("rw_a2")[j][:, cs_])
            if j == 1:
                S.dma('pool', (), (kb,), out=v2c, in_=W("rw_v2")[0][:, cs_])
            if final_pass:
                S.dma('pool', (), (kb,), out=g2a, in_=W("rw_g2")[j][0:128, cs_])
                S.dma('pool', (), (kb,), out=g2b, in_=W("rw_g2")[j][128:160, cs_])
                S.dma('pool', (), (kb,), out=wo, in_=W("rw_w_o")[j][cs_, :])
            S.op('dve', 'memset', (), (('Sa', 0, 0), ('Sa', 0, 1)), Sa[0], 0.0)
            if not final_pass:
                S.op('dve', 'tensor_copy', ('rc', ('Sa', 0, 0), ('Sa', 0, 1)), (('Sa', 0, 0), ('Sa', 0, 1)), out=Sa[0][:, 64:128], in_=I2)
            else:
                for h in range(2):
                    S.op('dve', 'memset', (), (('carry', h),), carry[h][0:64, :], 0.0)
                    S.op('dve', 'memset', (), (('Sin', h),), Sin[h][0:64, :], 0.0)
                for r in range(NCORES):
                    if r < NCORES - 1:
                        gi = cnt['g8'] % 2
                        cnt['g8'] += 1
                        S.dma('sp', (), (('G8', gi),), out=G8[gi][0:64, :, :], in_=W("grsumm")[r, 2 * ct:2 * ct + 2].rearrange("h k n -> k h n"))
                    for h in range(2):
                        S.op('dve', 'scalar_tensor_tensor', (('carry', h), ('Sin', h), 'cvec'), (('Sin', h),), out=Sin[h][0:64, :],
                             in0=carry[h][0:64, :], scalar=cvec[0:64, r:r + 1], in1=Sin[h][0:64, :], op0=ALU.mult, op1=ALU.add)
                        if r < NCORES - 1:
                            pi = rot('ps', 6)
                            MM(pi, psums[pi][0:64, 0:64], G8[gi][0:64, h, 64:128], carry[h][0:64, :], True, True, (('G8', gi), ('carry', h)))
                            S.op('dve', 'tensor_tensor', (('ps', pi), ('G8', gi)), (('carry', h),), out=carry[h][0:64, :],
                                 in0=psums[pi][0:64, 0:64], in1=G8[gi][0:64, h, 0:64], op=ALU.add)
                for h in range(2):
                    S.op('act', 'activation', (('Sin', h), ('Sa', 0, h)), (('Sa', 0, h),), out=Sa[0][h * 64:(h + 1) * 64, 0:64], in_=Sin[h][0:64, :], func=AF.Copy)
            par = [0]
            NA = 64 if final_pass else 128
            def prep(t, b):
                Vv, AR, KT, BT, BH, KH, WCE, CB = Vv2[b], AR2[b], KT2[b], BT2[b], BH2[b], KH2[b], WCE2[b], CB2[b]
                fmV = ('fm', id(Vv))
                c0 = t * RT
                ht = c0 // TT
                tcols = slice(c0, c0 + RT)
                py = [6, 7]
                cs = None
                c0 = t * RT
                ht = c0 // TT
                tcols = slice(c0, c0 + RT)
                for (w1v, w2v, dst, wkey) in ((r1, r2, R, ka), (k1, k2, K, ka), (v1_, v2_, Vv, kb)):
                    pi = rot('ps', 6)
                    proj16(pi, psums[pi][:, :RT], w1v, w2v, 0, 128, c0, RT, wkey)
                    S.op('act', 'activation', (('ps', pi),), (('fm', id(dst)),), out=dst, in_=psums[pi][:, :RT], func=AF.Copy)
                pi = rot('ps', 6)
                MM(pi, psums[pi][:, :RT], w2c, LW1[0:64, tcols], True, True, (kb, 'lora'))
                S.op('act', 'activation', (('ps', pi), 'vecs'), ('LW',), out=LW, in_=psums[pi][:, :RT], func=AF.Sigmoid, bias=V('w0', ct))
                yield
                S.op('dve', 'tensor_scalar', ('LW',), ('LW',), out=LW, in0=LW, scalar1=-0.6065306597126334, scalar2=None, op0=ALU.mult)
                yield
                pi = rot('ps', 6)
                MM(pi, psums[pi][:, :RT], a2c, LA1[0:64, tcols], True, True, (kb, 'lora'))
                S.op('act', 'activation', (('ps', pi), 'vecs'), ('ALR',), out=ALR, in_=psums[pi][:, :RT], func=AF.Sigmoid, bias=V('a0', ct))
                yield
                fmV = ('fm', id(Vv))
                fmK = ('fm', id(K))
                fmR = ('fm', id(R))
                if j == 1:
                    pi = rot('ps', 6)
                    MM(pi, psums[pi][:, :RT], v2c, LV1[0:32, tcols], True, True, (kb, 'lora'))
                    S.op('act', 'activation', (('ps', pi), 'vecs'), ('T2',), out=T2, in_=psums[pi][:, :RT], func=AF.Sigmoid, bias=V('v0', ct))
                    S.dma('sp', (), ('T1',), out=T1, in_=W("vfirst")[:, ct, tcols])
                    S.op('dve', 'tensor_tensor', ('T1', fmV), ('T1',), out=T1, in0=T1, in1=Vv, op=ALU.subtract)
                    S.op('dve', 'tensor_tensor', ('T1', 'T2'), ('T1',), out=T1, in0=T1, in1=T2, op=ALU.mult)
                    S.op('dve', 'tensor_tensor', ('T1', fmV), (fmV,), out=Vv, in0=Vv, in1=T1, op=ALU.add)
                elif final_pass:
                    S.dma('sp', (fmV,), (('o_vf', ct, t),), out=OUT("vfirst_out", [128, DC, TL])[:, ct, tcols], in_=Vv)
                    out_keys.append(('o_vf', ct, t))
                S.op('dve', 'tensor_scalar', (fmK, 'vecs'), ('KKN',), out=KKN, in0=K, scalar1=V('kk', ct), scalar2=None, op0=ALU.mult)
                yield
                S.op('dve', 'tensor_tensor', ('KKN',), ('T1',), out=T1, in0=KKN, in1=KKN, op=ALU.mult)
                yield
                pi = rot('ps', 6)
                MM(pi, psums[pi][:, :RT], bdones, T1, True, True, ('rc', 'T1'))
                S.op('act', 'activation', (('ps', pi),), ('T1',), out=T1, in_=psums[pi][:, :RT], func=AF.Sqrt)
                yield
                S.op('dve', 'tensor_scalar', ('T1',), ('T1',), out=T1, in0=T1, scalar1=1e-12, scalar2=None, op0=ALU.max)
                yield
                S.op('dve', 'reciprocal', ('T1',), ('T1',), out=T1, in_=T1)
                yield
                S.op('dve', 'tensor_tensor', ('KKN', 'T1'), ('KKN',), out=KKN, in0=KKN, in1=T1, op=ALU.mult)
                yield
                S.op('dve', 'tensor_scalar', ('ALR', 'vecs'), ('T1',), out=T1, in0=ALR, scalar1=-1.0, scalar2=V('ka', ct), op0=ALU.add, op1=ALU.mult)
                yield
                S.op('dve', 'scalar_tensor_tensor', ('T1', fmK), (fmK,), out=K, in0=T1, scalar=1.0, in1=K, op0=ALU.add, op1=ALU.mult)
                yield
                S.op('dve', 'tensor_tensor_scan', ('rc', 'LW'), ('CUM',), out=CUM, data0=cmask, data1=LW, initial=0.0, op0=ALU.mult, op1=ALU.add)
                yield
                S.op('act', 'activation', ('CUM',), ('WC',), out=WC, in_=CUM, func=AF.Exp)
                yield
                S.op('act', 'activation', ('CUM',), ('WINV',), out=WINV, in_=CUM, func=AF.Exp, scale=-1.0)
                yield
                S.op('dve', 'tensor_tensor', ('CUM', 'LW'), ('T1',), out=T1, in0=CUM, in1=LW, op=ALU.subtract)
                yield
                S.op('act', 'activation', ('T1',), ('WPRV',), out=WPRV, in_=T1, func=AF.Exp)
                yield
                S.op('dve', 'tensor_copy', ('WC',), (('WCE', b),), out=WCE, in_=WC.rearrange("p (c k) -> p c k", k=64)[:, :, 63])
                yield
                S.op('dve', 'scalar_tensor_tensor', ('KKN', 'WPRV'), (('AR0', b),), out=AR[:, 0, :], in0=KKN, scalar=-1.0, in1=WPRV, op0=ALU.mult, op1=ALU.mult)
                yield
                S.op('dve', 'tensor_tensor', (fmR, 'WC'), (('AR1', b),), out=AR[:, 1, :], in0=R, in1=WC, op=ALU.mult)
                yield
                S.op('dve', 'tensor_tensor', (fmK, 'WINV'), (('KT', b),), out=KT, in0=K, in1=WINV, op=ALU.mult)
                yield
                S.op('dve', 'tensor_tensor', ('KKN', 'ALR'), ('T1',), out=T1, in0=KKN, in1=ALR, op=ALU.mult)
                yield
                S.op('dve', 'tensor_tensor', ('T1', 'WINV'), (('BT', b),), out=BT, in0=T1, in1=WINV, op=ALU.mult)
                yield
                for c in range(2):
                    cs = slice(c * 64, (c + 1) * 64)
                    S.op('dve', 'tensor_scalar', (('BT', b), ('WCE', b)), (('BH', b),), out=BH[:, cs], in0=BT[:, cs], scalar1=WCE[:, c:c + 1], scalar2=None, op0=ALU.mult)
                    S.op('pool', 'tensor_scalar', (('KT', b), ('WCE', b)), (('KH', b),), out=KH[:, cs], in0=KT[:, cs], scalar1=WCE[:, c:c + 1], scalar2=None, op0=ALU.mult)
                if final_pass:
                    S.op('dve', 'tensor_tensor', (fmR, fmK), ('T2',), out=T2, in0=R, in1=K, op=ALU.mult)
                    S.op('dve', 'tensor_scalar', ('T2', 'vecs'), ('T2',), out=T2, in0=T2, scalar1=V('rk', ct), scalar2=None, op0=ALU.mult)
                    pi = rot('ps', 6)
                    MM(pi, psums[pi][:, :RT], bdones, T2, True, True, ('rc', 'T2'))
                    S.op('act', 'activation', (('ps', pi),), (('CB', b),), out=CB, in_=psums[pi][:, :RT], func=AF.Copy)
                    py = [6, 7]
                yield

            def chunks(t, b):
                Vv, AR, KT, BT, BH, KH, WCE, CB = Vv2[b], AR2[b], KT2[b], BT2[b], BH2[b], KH2[b], WCE2[b], CB2[b]
                fmV = ('fm', id(Vv))
                c0 = t * RT
                ht = c0 // TT
                tcols = slice(c0, c0 + RT)
                py = [6, 7]
                cs = None
                hb = [slice(0, 64), slice(64, 128)]
                csl = [slice(0, 64), slice(64, 128)]
                hc = lambda h: slice(h * 64, (h + 1) * 64)
                pi = rot('ps', 6)
                for q, (src, sk_) in enumerate(((Vv, fmV), (AR[:, 0, :], ('AR0', b)), (BH, ('BH', b)), (KH, ('KH', b)))):
                    TR(pi, psums[pi][:, q * 128:(q + 1) * 128], src, (sk_,))
                S.op('act', 'activation', (('ps', pi),), ('TM',), out=TM[:, 0:256], in_=psums[pi][:, 0:256], func=AF.Copy)
                S.op('dve', 'tensor_copy', (('ps', pi),), ('TM',), out=TM[:, 256:512], in_=psums[pi][:, 256:512])
                Vtm = lambda h: TM[:, h * 64:(h + 1) * 64]
                Atm = lambda h: TM[:, 128 + h * 64:128 + (h + 1) * 64]
                Bhtm = lambda h: TM[:, 256 + h * 64:256 + (h + 1) * 64]
                Khtm = lambda h: TM[:, 384 + h * 64:384 + (h + 1) * 64]
                yield
                for h in range(2):
                    bh = hb[h]
                    pa = rot('ps', 6)
                    MM(pa, psums[pa][:, 0:256], BT[bh, :], AR[bh, :, :], True, True, (('BT', b), ('AR0', b), ('AR1', b)))
                    MM(pa, psums[pa][:, 256:384], AR[bh, 0, :], BT[bh, :], True, True, (('BT', b), ('AR0', b)))
                    S.op('dve', 'tensor_tensor', (('ps', pa), 'rc'), (('GMa', h),), out=GMa[:, h, :], in0=psums[pa][:, 0:384], in1=maskA, op=ALU.mult)
                    pb_ = rot('ps', 6)
                    MM(pb_, psums[pb_][:, 0:256], KT[bh, :], AR[bh, :, :], True, True, (('KT', b), ('AR0', b), ('AR1', b)))
                    S.op('pool' if False else 'dve', 'tensor_tensor', (('ps', pb_), 'rc'), (('GMb', h),), out=GMb[:, h, :], in0=psums[pb_][:, 0:256], in1=maskA[:, 0:256], op=ALU.mult)
                yield
                for h in range(2):
                    S.op('dve', 'tensor_tensor', (('GMa', h), 'ident'), (('Tm', 0, h),), out=Tm[0][:, h, :], in0=GMa[:, h, 0:128], in1=ident[:], op=ALU.add)
                Np = [GMa[:, h, 0:128] for h in range(2)]
                Lp = [GMa[:, h, 256:384] for h in range(2)]
                nlk = [('GMa', h) for h in range(2)]
                for lev in range(1, 6):
                    nli = lev % 2
                    yield
                    pi = rot('ps', 6)
                    for h in range(2):
                        if lev < 5:
                            MM(pi, psums[pi][:, h * 256:h * 256 + 128], Lp[h], Np[h], True, True, (nlk[h],))
                        MM(pi, psums[pi][:, h * 256 + 128:h * 256 + 256], Np[h], Lp[h], True, True, (nlk[h],))
                    if lev < 5:
                        S.op('act', 'activation', (('ps', pi),), (('NL', nli, 0), ('NL', nli, 1)), out=NL[nli],
                             in_=psums[pi][:].rearrange("p (h n) -> p h n", h=2), func=AF.Copy)
                    else:
                        S.op('act', 'activation', (('ps', pi),), (('NL', nli, 0), ('NL', nli, 1)), out=NL[nli][:, :, 128:256],
                             in_=psums[pi][:].rearrange("p (h n) -> p h n", h=2)[:, :, 128:256], func=AF.Copy)
                    for h in range(2):
                        Np[h], Lp[h], nlk[h] = NL[nli][:, h, 0:128], NL[nli][:, h, 128:256], ('NL', nli, h)
                    tp, tn = (lev - 1) % 2, lev % 2
                    yield
                    pi = rot('ps', 6)
                    for h in range(2):
                        MM(pi, psums[pi][:, h * 128:(h + 1) * 128], Lp[h], Tm[tp][:, h, :], True, True, (nlk[h], ('Tm', tp, h)))
                    S.op('dve', 'tensor_tensor', (('ps', pi), ('Tm', tp, 0), ('Tm', tp, 1)), (('Tm', tn, 0), ('Tm', tn, 1)), out=Tm[tn],
                         in0=psums[pi][:, 0:256].rearrange("p (h n) -> p h n", h=2), in1=Tm[tp], op=ALU.add)
                Tf = lambda h: Tm[1][:, h, :]
                yield
                pz = rot('ps', 6)
                for h in range(2):
                    MM(pz, psums[pz][:, hc(h)], GMb[:, h, 0:128], Vtm(h), True, True, (('GMb', h), 'TM'))
                S.op('act', 'activation', (('ps', pz),), ('Zs',), out=Zs, in_=psums[pz][:, 0:128], func=AF.Copy)
                for h in range(2):
                    pa_ = rot('ps', 6)
                    MM(pa_, psums[pa_][hb[h], 0:128], Atm(h), Tf(h), True, True, ('TM', ('Tm', 1, h)))
                    S.op('dve', 'tensor_copy', (('ps', pa_),), (('Aps', h),), out=Aps[hb[h], :], in_=psums[pa_][hb[h], 0:128])
                yield
                pi = rot('ps', 6)
                for h in range(2):
                    MM(pi, psums[pi][:, hc(h)], Tf(h), Zs[:, hc(h)], True, True, (('Tm', 1, h), 'Zs'))
                S.op('act', 'activation', (('ps', pi),), ('Vp',), out=Vp, in_=psums[pi][:, 0:128], func=AF.Copy)
                for c in range(2):
                    yield
                    cs = csl[c]
                    cur, nxt = par[0], 1 - par[0]
                    par[0] = nxt
                    sc, sn = Sa[cur], Sa[nxt]
                    for h in range(2):
                        bh = hb[h]
                        pi = rot('ps', 6)
                        MM(pi, psums[pi][cs, 0:NA], Aps[bh, cs], sc[bh, 0:NA], True, True, (('Aps', h), ('Sa', cur, h)))
                        S.op('dve', 'tensor_tensor', (('ps', pi), 'Vp'), (('Us', h),), out=Us[h][cs, 0:64], in0=psums[pi][cs, 0:64], in1=Vp[cs, hc(h)], op=ALU.add)
                        if not final_pass:
                            S.op('act', 'activation', (('ps', pi),), (('Us', h),), out=Us[h][cs, 64:128], in_=psums[pi][cs, 64:128], func=AF.Copy)
                    if final_pass:
                        for h in range(2):
                            bh = hb[h]
                            MM(py[h], psums[py[h]][cs, 0:64], AR[bh, 1, cs], sc[bh, 0:64], True, True, (('AR1', b), ('Sa', cur, h)))
                    for h in range(2):
                        bh = hb[h]
                        pi = rot('ps', 6)
                        MM(pi, psums[pi][bh, 0:NA], Bhtm(h)[cs, :], Us[h][cs, 0:NA], True, False, ('TM', ('Us', h)))
                        MM(pi, psums[pi][bh, 0:64], Khtm(h)[cs, :], Vtm(h)[cs, :], False, True, ('TM',))
                        S.op('dve', 'scalar_tensor_tensor', (('ps', pi), ('Sa', cur, h), ('WCE', b)), (('Sa', nxt, h),), out=sn[bh, 0:NA],
                             in0=sc[bh, 0:NA], scalar=WCE[bh, c:c + 1], in1=psums[pi][bh, 0:NA], op0=ALU.mult, op1=ALU.add)
                if not final_pass:
                    return
                yield
                py2 = rot('ps', 6)
                for h in range(2):
                    MM(py2, psums[py2][:, hc(h)], GMa[:, h, 128:256], Us[h][:, 0:64], True, False, (('GMa', h), ('Us', h)))
                    MM(py2, psums[py2][:, hc(h)], GMb[:, h, 128:256], Vtm(h), False, True, (('GMb', h), 'TM'))
                yield
                for h in range(2):
                    S.op('act', 'activation', (('ps', py[h]),), ('Ytm',), out=Ytm[:, h * 64:(h + 1) * 64], in_=psums[py[h]][:, 0:64], func=AF.Copy)
                S.op('dve', 'tensor_tensor', (('ps', py2), 'Ytm'), ('Ytm',), out=Ytm, in0=psums[py2][:, 0:128], in1=Ytm, op=ALU.add)
                yield
                for h in range(2):
                    hs = slice(h * 64, (h + 1) * 64)
                    S.op('dve', 'tensor_reduce', ('Ytm',), ('STa',), out=STt[:, h:h + 1], in_=Ytm[:, hs], axis=mybir.AxisListType.X, op=ALU.add)
                    S.op('act', 'activation', ('Ytm',), ('O2', 'STb'), out=O2[:, hs], in_=Ytm[:, hs], func=AF.Square, accum_out=STt[:, 2 + h:3 + h])
                yield
                S.op('dve', 'tensor_scalar', ('STa',), ('STa',), out=STt[:, 0:2], in0=STt[:, 0:2], scalar1=1.0 / 64, scalar2=None, op0=ALU.mult)
                yield
                S.op('dve', 'tensor_tensor', ('STa',), ('STc',), out=STt[:, 4:6], in0=STt[:, 0:2], in1=STt[:, 0:2], op=ALU.mult)
                yield
                S.op('dve', 'scalar_tensor_tensor', ('STb', 'STc'), ('STb',), out=STt[:, 2:4], in0=STt[:, 2:4], scalar=1.0 / 64, in1=STt[:, 4:6], op0=ALU.mult, op1=ALU.subtract)
                yield
                S.op('act', 'activation', ('STb', 'vecs'), ('STb',), out=STt[:, 2:4], in_=STt[:, 2:4], func=AF.Sqrt, bias=vcol('gneps'))
                yield
                S.op('dve', 'reciprocal', ('STb',), ('STb',), out=STt[:, 2:4], in_=STt[:, 2:4])
                yield
                S.op('dve', 'scalar_tensor_tensor', ('STa', 'STb'), ('STc',), out=STt[:, 4:6], in0=STt[:, 0:2], scalar=-1.0, in1=STt[:, 2:4], op0=ALU.mult, op1=ALU.mult)
                yield
                for h in range(2):
                    hs = slice(h * 64, (h + 1) * 64)
                    S.op('act', 'activation', ('Ytm', 'STb', 'STc'), ('YN',), out=YN[:, hs], in_=Ytm[:, hs], func=AF.Identity,
                         scale=STt[:, 2 + h:3 + h], bias=STt[:, 4 + h:5 + h])
                yield
                pi = rot('ps', 6)
                TR(pi, psums[pi][:, 0:128], YN, ('YN',))
                yield
                S.op('dve', 'tensor_scalar', (('ps', pi), 'vecs'), ('O1',), out=O1, in0=psums[pi][:, 0:128], scalar1=V('lnw', ct), scalar2=V('lnb', ct), op0=ALU.mult, op1=ALU.add)
                yield
                S.op('dve', 'tensor_tensor', (('CB', b), fmV), ('O2',), out=O2, in0=CB, in1=Vv, op=ALU.mult)
                yield
                S.op('dve', 'tensor_tensor', ('O1', 'O2'), ('O1',), out=O1, in0=O1, in1=O2, op=ALU.add)
                yield
                pi = rot('ps', 6)
                MM(pi, psums[pi][:, 0:RT], g2a, LG1a[:, tcols], True, False, (kb, 'lora'))
                MM(pi, psums[pi][:, 0:RT], g2b, LG1b[0:32, tcols], False, True, (kb, 'lora'))
                yield
                S.op('dve', 'tensor_tensor', (('ps', pi), 'O1'), ('OTb',), out=OTb, in0=psums[pi][:, 0:RT], in1=O1, op=ALU.mult)
                yield
                for half in range(2):
                    pi = rot('ps', 6)
                    for q in range(4):
                        mo = half * 4 + q
                        MM(pi, psums[pi][:, q * 128:(q + 1) * 128], wo[:, mo * 128:(mo + 1) * 128], OTb, True, True, (kb, 'OTb'))
                    hk = tuple(('h', half * 4 + q, ht) for q in range(4))
                    S.op('dve', 'tensor_tensor', (('ps', pi),) + hk, hk, out=hT[:, half * 4:(half + 1) * 4, tcols],
                         in0=psums[pi][:].rearrange("p (q t) -> p q t", q=4), in1=hT[:, half * 4:(half + 1) * 4, tcols], op=ALU.add)
                yield

            def drive(*gens):
                gens = list(gens)
                while gens:
                    for g in list(gens):
                        try:
                            next(g)
                        except StopIteration:
                            gens.remove(g)

            prev = None
            for t in range(NRT):
                if prev is None:
                    drive(prep(t, t % 2))
                else:
                    drive(prev, prep(t, t % 2))
                prev = chunks(t, t % 2)
            drive(prev)
            if not final_pass:
                sc = Sa[par[0]]
                pi = rot('ps', 6)
                for h in range(2):
                    TR(pi, psums[pi][0:64, h * 64:(h + 1) * 64], sc[h * 64:(h + 1) * 64, 64:128], (('Sa', par[0], h),), b0=h * 64)
                    S.op('act', 'activation', (('Sa', par[0], h), 'SUMT'), ('SUMT',), out=SUMT[0:64, h, 0:64], in_=sc[h * 64:(h + 1) * 64, 0:64], func=AF.Copy)
                S.op('dve', 'tensor_copy', (('ps', pi), 'SUMT'), ('SUMT',), out=SUMT[0:64, :, 64:128], in_=psums[pi][0:64, 0:128].rearrange("p (h n) -> p h n", h=2))
                S.dma('sp', ('SUMT',), (('o_rs', ct),), out=OUT("rsumm", [16, 64, 128])[2 * ct:2 * ct + 2].rearrange("h k n -> k h n"), in_=SUMT[0:64, :, :])
                out_keys.append(('o_rs', ct))
        S.barrier()

    def final():
        xo = big[:, :DC * TT * 2].bitcast(F32).rearrange("p (c t) -> p c t", c=DC)
        outs = []
        for t in range(NT):
            ri = rms_tile(t)
            for c in range(DC):
                S.op('dve', 'scalar_tensor_tensor', (('h', c, t), ('rs', ri), 'vecs'), (('xo', c),),
                     out=xo[:, c, :], in0=hT[:, c, tsl(t)], scalar=vcol('norm_final', c),
                     in1=rs[ri][:], op0=ALU.mult, op1=ALU.mult)
            for b in range(TT // 128):
                si = rot('stage', 2)
                for half in range(2):
                    pi = rot('ps', 6)
                    for j in range(4):
                        c = half * 4 + j
                        TR(pi, psums[pi][:, j * 128:(j + 1) * 128], xo[:, c, b * 128:(b + 1) * 128], (('xo', c),))
                    S.op('act', 'activation', (('ps', pi),), (('stage', si),),
                         out=stage[si][:, half * 512:(half + 1) * 512], in_=psums[pi][:], func=AF.Copy)
                r0 = t * TT + b * 128
                key = ('yout', r0)
                S.dma('sp', (('stage', si),), (key,), out=OUT("y", [TL, D])[r0:r0 + 128, :], in_=stage[si][:])
                out_keys.append(key)

    def load_h():
        for c in range(DC):
            S.dma('sp', (), tuple(('h', c, t) for t in range(NT)), out=hT[:, c, :], in_=W("h_in")[:, c, :])

    def store_h():
        for c in range(DC):
            S.dma('sp', tuple(('h', c, t) for t in range(NT)), (('o_h', c),), out=OUT("h_out", [128, DC, TL])[:, c, :], in_=hT[:, c, :])
            out_keys.append(('o_h', c))

    if first:
        load_xT()
    else:
        load_h()
    for step in prog:
        kind = step[0]
        if kind == 'tail':
            rmsnorm_xn('norm_mix%d' % step[1])
            tail_out()
        elif kind in ('mix1', 'mix2'):
            l = step[1]
            if l % 2 == 0:
                hawk(l, kind == 'mix2')
            else:
                rwkv(l, kind == 'mix2')
        elif kind == 'ffn':
            ffn(step[1])
        elif kind == 'ple':
            ple(step[1])
        elif kind == 'final':
            final()
        elif kind == 'store_h':
            store_h()
    S.finish('sp', out_keys)
    S.emit(nc, stack)
    stack.close()
    nc._declared_inputs = set(declared)
    nc._declared_outputs = set(outs_decl)
    return nc


VOFF = {}
NVEC = 0


def _layout_vecs():
    global NVEC
    o = 0

    def add(name, n):
        nonlocal o
        VOFF[name] = o
        o += n
    add('eps', 1)
    for l in range(4):
        add('norm_mix%d' % l, 8)
        add('norm_ffn%d' % l, 8)
        add('norm_ple%d' % l, 8)
    add('norm_final', 8)
    add('one', 1)
    for j in range(2):
        for n in ('bg', 'br', 'cw0', 'cw1', 'cw2', 'cw3', 'cb', 'ba', 'bx', 'lam'):
            add('h%d_%s' % (j, n), 10)
        add('h%d_bo' % j, 8)
    add('gneps', 1)
    for j in range(2):
        for i in range(6):
            add('r%d_mu%d' % (j, i), 8)
        for n in ('w0', 'a0', 'v0', 'kk', 'ka', 'rk', 'lnw', 'lnb'):
            add('r%d_%s' % (j, n), 8)
    NVEC = o


_layout_vecs()


def colmajor(v):
    v = np.asarray(v, np.float32).reshape(-1)
    return np.ascontiguousarray(v.reshape(-1, 128).T)


def pack_vecs(inp):
    vecs = np.zeros((128, NVEC), np.float32)

    def put(name, v):
        cm = colmajor(v)
        vecs[:, VOFF[name]:VOFF[name] + cm.shape[1]] = cm
    vecs[:, VOFF['eps']] = EPS
    for l in range(4):
        put('norm_mix%d' % l, inp['norm_mix'][l])
        put('norm_ffn%d' % l, inp['norm_ffn'][l])
        put('norm_ple%d' % l, inp['norm_ple'][l])
    put('norm_final', inp['norm_final'])
    vecs[:, VOFF['one']] = 1.0
    for j in range(2):
        put('h%d_bg' % j, inp['lru_b_in'][j][:DR])
        put('h%d_br' % j, inp['lru_b_in'][j][DR:])
        for k in range(4):
            put('h%d_cw%d' % (j, k), inp['lru_conv_w'][j][k])
        put('h%d_cb' % j, inp['lru_conv_b'][j])
        put('h%d_ba' % j, inp['lru_b_gate_a'][j])
        put('h%d_bx' % j, inp['lru_b_gate_x'][j])
        put('h%d_lam' % j, inp['lru_lambda'][j])
        put('h%d_bo' % j, inp['lru_b_out'][j])
    vecs[:, VOFF['gneps']] = 64e-5
    for j in range(2):
        for i in range(6):
            put('r%d_mu%d' % (j, i), inp['rw_mu'][j][i])
        put('r%d_w0' % j, inp['rw_w0'][j])
        put('r%d_a0' % j, inp['rw_a0'][j])
        if j == 1:
            put('r%d_v0' % j, inp['rw_v0'][0])
        put('r%d_kk' % j, inp['rw_k_k'][j])
        put('r%d_ka' % j, inp['rw_k_a'][j])
        put('r%d_rk' % j, np.asarray(inp['rw_r_k'][j]).reshape(-1))
        put('r%d_lnw' % j, inp['rw_ln_w'][j])
        put('r%d_lnb' % j, inp['rw_ln_b'][j])
    return vecs


def launch(TL, prog, first, per_core, shared):
    nc = build(TL, prog, first=first)
    in_maps = []
    for c in range(NCORES):
        m = dict(shared)
        m.update(per_core[c])
        in_maps.append({k: v for k, v in m.items() if k in nc._declared_inputs})
        missing = nc._declared_inputs - set(in_maps[-1])
        assert not missing, missing
    res = run_bass_kernel_spmd(nc, in_maps, core_ids=list(range(NCORES)))
    return res.results


def run(inp, TL, layers=(0, 1, 2, 3), parts=('mix', 'ffn', 'ple')):
    f32 = lambda a: np.ascontiguousarray(np.asarray(a, np.float32))
    x = f32(inp['x']).reshape(-1, D)
    p = f32(inp['p']).reshape(4, -1, DPLE)
    shared = {
        'vecs': pack_vecs(inp), 'ident_in': np.eye(128, dtype=np.float32),
        'ffn_w_in': f32(inp['ffn_w_in']), 'ffn_w_out': f32(inp['ffn_w_out']),
        'ple_w_proj': f32(inp['ple_w_proj']), 'ple_w_gate': f32(inp['ple_w_gate']),
        'lru_w_in': f32(inp['lru_w_in']), 'lru_w_out': f32(inp['lru_w_out']),
        'lru_w_gate_a': f32(inp['lru_w_gate_a']).reshape(2, RC * 128, 128),
        'lru_w_gate_x': f32(inp['lru_w_gate_x']).reshape(2, RC * 128, 128),
    }
    shared.update(rwkv_shared(inp))
    per_core = []
    for c in range(NCORES):
        cv = np.zeros((128, 17), np.float32)
        cv[:, c] = 1.0
        if c > 0:
            cv[:, 8 + c - 1] = 1.0
            cv[:, 16] = 1.0
        per_core.append({'x': np.ascontiguousarray(x[c * TL:(c + 1) * TL]),
                         'p': np.ascontiguousarray(p[:, c * TL:(c + 1) * TL]), 'cvec': cv})

    def route_halo(res):
        for c in range(NCORES):
            per_core[c]['halo'] = res[c - 1]['tail'] if c > 0 else np.zeros((128, 32), np.float32)

    def take_h(res):
        for c in range(NCORES):
            per_core[c]['h_in'] = res[c]['h_out']

    layers = list(layers)
    mix = 'mix' in parts
    tailsteps = lambda l: ([('tail', l)] if mix else [])
    post = lambda l: ([('ffn', l)] if 'ffn' in parts else []) + ([('ple', l)] if 'ple' in parts else [])
    if not layers:
        res = launch(TL, [('final',)], True, per_core, shared)
    else:
        res = launch(TL, tailsteps(layers[0]) + [('store_h',)], True, per_core, shared)
        take_h(res)
        for li, l in enumerate(layers):
            if mix:
                route_halo(res)
                res1 = launch(TL, [('mix1', l)], False, per_core, shared)
                gather_summ(l, res1, per_core)
            nxt = (tailsteps(layers[li + 1]) + [('store_h',)]) if li + 1 < len(layers) else [('final',)]
            res = launch(TL, ([('mix2', l)] if mix else []) + post(l) + nxt, False, per_core, shared)
            if li + 1 < len(layers):
                take_h(res)
                keep_state(l, res, per_core)
    y = np.concatenate([res[c]['y'] for c in range(NCORES)], axis=0)
    return y.reshape(1, NCORES * TL, D).astype(np.float32)


def gather_summ(l, res1, per_core):
    if l % 2 == 0:
        g = np.ascontiguousarray(np.stack([res1[c]['summ'] for c in range(NCORES)], axis=1))
        for c in range(NCORES):
            per_core[c]['gsumm'] = g
    else:
        g = np.ascontiguousarray(np.stack([res1[c]['rsumm'] for c in range(NCORES)], axis=0))
        for c in range(NCORES):
            per_core[c]['grsumm'] = g


def keep_state(l, res, per_core):
    if l == 1:
        for c in range(NCORES):
            per_core[c]['vfirst'] = res[c]['vfirst_out']


def rwkv_shared(inp):
    f32 = lambda a: np.ascontiguousarray(np.asarray(a, np.float32))
    rc = np.zeros((128, 704), np.float32)
    i = np.arange(128)
    same = (i[:, None] // 64) == (i[None, :] // 64)
    su = ((i[:, None] < i[None, :]) & same).astype(np.float32)
    iu = ((i[:, None] <= i[None, :]) & same).astype(np.float32)
    sl = ((i[:, None] > i[None, :]) & same).astype(np.float32)
    rc[:, 0:384] = np.concatenate([su, iu, sl], axis=1)
    rc[0:64, 384:448] = 1.0
    rc[64:128, 448:512] = 1.0
    rc[:, 512:640] = 1.0
    rc[:, 512] = 0.0
    rc[:, 512 + 64] = 0.0
    rc[0:64, 640:704] = np.eye(64, dtype=np.float32)
    rc[64:128, 640:704] = np.eye(64, dtype=np.float32)
    d = {k: f32(inp[k]) for k in ('rw_w_rkv', 'rw_w_o', 'rw_w1', 'rw_w2', 'rw_a1', 'rw_a2', 'rw_v1', 'rw_v2', 'rw_g1', 'rw_g2')}
    d['rconst'] = rc
    return d


def kernel(**inputs):
    return run(inputs, 2048)
```

```python
import contextlib
import numpy as np
import concourse.bass as bass
import concourse.mybir as mybir
from concourse.bass_utils import run_bass_kernel_spmd

F32 = mybir.dt.float32
BF16 = mybir.dt.bfloat16
AF = mybir.ActivationFunctionType
ALU = mybir.AluOpType
NCORES = 8
D, DC, DR, RC, DFF, FC, DPLE = 1024, 8, 1280, 10, 2816, 22, 256
HALO = 4
SAME_ENGINE_SYNC = True
EPS = 1e-6


class Sched:
    ENGS = ('pe', 'act', 'dve', 'pool', 'sp')

    def __init__(self):
        self.ops = {e: [] for e in self.ENGS}
        self.cnt = {e: 0 for e in self.ENGS}
        self.waited = {e: {} for e in self.ENGS}
        self.lastw = {}
        self.readers = {}
        self.NS = 8
        self.dma_uses = {}
        self.dma_n = {q: 0 for q in self.ENGS}
        self.ncc = 0
        self.semkeys = set()

    def _deps(self, reads, writes):
        deps = []
        for k in reads:
            t = self.lastw.get(k)
            if t:
                deps.append(t)
        for k in writes:
            t = self.lastw.get(k)
            if t:
                deps.append(t)
            deps.extend(self.readers.get(k, {}).items())
        return deps

    def _waits(self, eng, deps):
        w = {}
        for (sk, v) in deps:
            if sk == ('eng', eng) and (eng == 'pe' or not SAME_ENGINE_SYNC):
                continue
            if v > self.waited[eng].get(sk, 0) and v > w.get(sk, 0):
                w[sk] = v
        for sk, v in w.items():
            self.waited[eng][sk] = v
        return list(w.items())

    def _commit(self, tok, reads, writes):
        for k in writes:
            self.lastw[k] = tok
            self.readers[k] = {}
        for k in reads:
            if k not in writes:
                r = self.readers.setdefault(k, {})
                if tok[1] > r.get(tok[0], 0):
                    r[tok[0]] = tok[1]

    def op(self, eng, fn, reads=(), writes=(), *args, **kwargs):
        fn = (fn, args, kwargs)
        waits = self._waits(eng, self._deps(reads, writes))
        self.cnt[eng] += 1
        tok = (('eng', eng), self.cnt[eng])
        self.semkeys.add(tok[0])
        self.ops[eng].append((waits, fn, tok, 1))
        self._commit(tok, reads, writes)

    def dma(self, q, reads=(), writes=(), **kwargs):
        fn = ('dma_start', (), kwargs)
        slot = self.dma_n[q] % self.NS
        self.dma_n[q] += 1
        sk = ('dma', q, slot)
        self.semkeys.add(sk)
        uses = self.dma_uses.get(sk, 0)
        deps = self._deps(reads, writes)
        if uses:
            deps.append((sk, 16 * uses))
        waits = self._waits(q, deps)
        self.dma_uses[sk] = uses + 1
        tok = (sk, 16 * (uses + 1))
        self.ops[q].append((waits, fn, tok, 16))
        self._commit(tok, reads, writes)

    def cc(self, reads=(), writes=(), *args, **kwargs):
        fn = ('collective_compute', args, kwargs)
        sk = ('cc', self.ncc)
        self.ncc += 1
        self.semkeys.add(sk)
        waits = self._waits('pool', self._deps(reads, writes))
        tok = (sk, 1)
        self.ops['pool'].append((waits, fn, tok, None))
        self._commit(tok, reads, writes)

    def barrier(self):
        toks = [(('eng', e), self.cnt[e]) for e in self.ENGS if self.cnt[e]]
        toks += [(sk, 16 * u) for sk, u in self.dma_uses.items()]
        cct = [(('cc', i), 1) for i in range(self.ncc)]
        for e in self.ENGS:
            waits = self._waits(e, toks + (cct if e == 'pool' else []))
            if waits:
                self.ops[e].append((waits, None, None, 0))

    def finish(self, eng, keys):
        waits = self._waits(eng, self._deps(keys, ()))
        self.ops[eng].append((waits, None, None, 0))

    def emit(self, nc, stack):
        sems = {}
        for i, sk in enumerate(sorted(self.semkeys, key=str)):
            sems[sk] = stack.enter_context(nc.semaphore("s%d" % i))
        block = stack.enter_context(nc.Block())
        engmap = {'pe': block.tensor, 'act': block.scalar, 'dve': block.vector,
                  'pool': block.gpsimd, 'sp': block.sync}
        for e in self.ENGS:
            def body(eng, ops=self.ops[e]):
                for waits, fn, tok, inc in ops:
                    for sk, v in waits:
                        eng.wait_ge(sems[sk], v)
                    if fn is None:
                        continue
                    try:
                        ins = getattr(eng, fn[0])(*fn[1], **fn[2])
                    except Exception:
                        print("EMIT FAIL", fn[0], [str(a)[:160] for a in fn[1]], {k: str(v)[:160] for k, v in fn[2].items()})
                        raise
                    if inc is None:
                        ins.then_inc(sems[tok[0]])
                    else:
                        ins.then_inc(sems[tok[0]], inc)
            engmap[e](body)


def build(TL, prog, first=False):
    nc = bass.Bass("TRN2", target_bir_lowering=False)
    S = Sched()
    stack = contextlib.ExitStack()
    TT = min(512, TL)
    NT = TL // TT
    ST = min(1024, TL)
    NST = TL // ST
    TPS = ST // TT
    NB = TL // 128

    declared = {}
    SHAPES = {
        "x": [TL, D], "p": [4, TL, DPLE], "vecs": [128, NVEC], "cvec": [128, 17], "ident_in": [128, 128],
        "ffn_w_in": [4, D, 2 * DFF], "ffn_w_out": [4, DFF, D], "ple_w_proj": [4, DPLE, D], "ple_w_gate": [4, D, D],
        "lru_w_in": [2, D, 2 * DR], "lru_w_gate_a": [2, RC * 128, 128], "lru_w_gate_x": [2, RC * 128, 128],
        "lru_w_out": [2, DR, D],
    }

    def W(name):
        if name not in declared:
            declared[name] = nc.dram_tensor(name, list(SHAPES[name]), F32, kind="ExternalInput").ap()
        return declared[name]

    outs_decl = {}

    def OUT(name, shape):
        if name not in outs_decl:
            outs_decl[name] = nc.dram_tensor(name, list(shape), F32, kind="ExternalOutput").ap()
        return outs_decl[name]

    SHAPES.update({"h_in": [128, DC, TL], "halo": [128, 32], "gsumm": [128, NCORES, 32],
                   "rw_w_rkv": [2, 3, D, D], "rw_w_o": [2, D, D], "rw_w1": [2, D, 64], "rw_w2": [2, 64, D],
                   "rw_a1": [2, D, 64], "rw_a2": [2, 64, D], "rw_v1": [1, D, 32], "rw_v2": [1, 32, D],
                   "rw_g1": [2, D, 160], "rw_g2": [2, 160, D], "rconst": [128, 704],
                   "grsumm": [NCORES, 16, 64, 128], "vfirst": [128, DC, TL]})
    out_keys = []

    def sb(name, shape, dt=F32):
        return stack.enter_context(nc.sbuf_tensor(name, list(shape), dt))

    hT = sb("hT", [128, DC, TL])
    xn = sb("xn", [128, DC, HALO + TL], BF16)
    vecs = sb("vecs_sb", [128, NVEC])
    cvec = sb("cvec_sb", [128, 17])
    gath = sb("gath", [128, NCORES, 32])
    smal = sb("smal", [128, 128])
    ident = sb("ident", [128, 128])
    ones_bf = sb("ones_bf", [128, 128], BF16)
    NWB = 2
    wbufs = [sb("wb%d" % i, [128, 4096], BF16) for i in range(NWB)]
    stage = [sb("stage%d" % i, [128, D]) for i in range(2)]
    sq = sb("sq", [128, DC, TT], BF16)
    rs = [sb("rs%d" % i, [128, TT]) for i in range(2)]
    tmpA = [sb("tmpA%d" % i, [128, TT]) for i in range(3)]
    ARENA = max(FC * ST * 2 + 2 * TL * 2, 5 * (TL + 4) * 4 + 2 * TL * 2 + 64, 5 * TL * 2 + 4 * 11400)
    arena = sb("arena", [128, ARENA // 2], BF16)
    big = arena[:, :FC * ST]
    pT = arena[:, FC * ST:FC * ST + 2 * TL].rearrange("p (k t) -> p k t", k=2)
    psums = [stack.enter_context(nc.psum_tensor("ps%d" % i, [128, 512], F32)) for i in range(8)]
    st = {'ps': 0, 'wb': 0, 'stage': 0, 'rs': 0, 'tmpA': 0}

    def rot(name, n):
        i = st[name]
        st[name] = (i + 1) % n
        return i

    def vcol(name, c=0):
        o = VOFF[name] + c
        return vecs[:, o:o + 1]

    S.dma('sp', (), ('vecs',), out=vecs[:], in_=W("vecs"))
    S.dma('sp', (), ('cvec',), out=cvec[:], in_=W("cvec"))
    S.dma('sp', (), ('ident',), out=ident[:], in_=W("ident_in"))
    S.op('dve', 'memset', (), ('ones',), ones_bf[:], 1.0)

    def tsl(t):
        return slice(t * TT, (t + 1) * TT)

    def xsl(t, sh=0):
        return slice(HALO + t * TT - sh, HALO + (t + 1) * TT - sh)

    def wview(i, kc, mb):
        return wbufs[i][:, :kc * mb].rearrange("p (k m) -> p k m", k=kc)

    def wload(i, kc, mb, src, m0=0, mw=None):
        mw = mb if mw is None else mw
        dst = wview(i, kc, mb)[:, :, m0:m0 + mw]
        S.dma('pool', (), (('wb', i),), out=dst, in_=src.rearrange("(k p) m -> p k m", p=128))

    def MM(pi, out, lhsT, rhs, start, stop, reads):
        S.op('pe', 'matmul', reads, (('ps', pi),), out, lhsT, rhs, start=start, stop=stop)

    def TR(pi, out, in_, reads, b0=0):
        np_ = in_.shape[0]
        S.op('pe', 'transpose', tuple(reads) + ('ident',), (('ps', pi),), out, in_, ident[b0:b0 + np_, b0:b0 + np_])

    def load_xT():
        for tb in range(NB):
            si = rot('stage', 2)
            S.dma('sp', (), (('stage', si),), out=stage[si][:], in_=W("x")[tb * 128:(tb + 1) * 128, :])
            for half in range(2):
                pi = rot('ps', 6)
                for j in range(4):
                    c = half * 4 + j
                    TR(pi, psums[pi][:, j * 128:(j + 1) * 128], stage[si][:, c * 128:(c + 1) * 128], (('stage', si),))
                S.op('dve', 'tensor_copy', (('ps', pi),), tuple(('h', half * 4 + j, tb * 128 // TT) for j in range(4)),
                     out=hT[:, half * 4:(half + 1) * 4, tb * 128:(tb + 1) * 128],
                     in_=psums[pi][:].rearrange("p (j t) -> p j t", j=4))

    def rms_tile(t):
        S.op('act', 'activation', tuple(('h', c, t) for c in range(DC)), ('sq',),
             out=sq[:], in_=hT[:, :, tsl(t)], func=AF.Square)
        pi = rot('ps', 6)
        for c in range(DC):
            MM(pi, psums[pi][:, :TT], ones_bf[:], sq[:, c, :], c == 0, c == DC - 1, ('sq', 'ones'))
        ri = rot('rs', 2)
        S.op('act', 'activation', (('ps', pi), 'vecs'), (('rs', ri),),
             out=rs[ri][:], in_=psums[pi][:, :TT], func=AF.Sqrt, bias=vcol('eps'), scale=1.0 / D)
        S.op('dve', 'reciprocal', (('rs', ri),), (('rs', ri),), out=rs[ri][:], in_=rs[ri][:])
        return ri

    def rmsnorm_xn(gname):
        for t in range(NT):
            ri = rms_tile(t)
            for c in range(DC):
                S.op('dve', 'scalar_tensor_tensor', (('h', c, t), ('rs', ri), 'vecs'), (('xn', c, t),),
                     out=xn[:, c, xsl(t)], in0=hT[:, c, tsl(t)], scalar=vcol(gname, c),
                     in1=rs[ri][:], op0=ALU.mult, op1=ALU.mult)

    def ffn(l):
        rmsnorm_xn('norm_ffn%d' % l)
        hid = big[:, :FC * ST].rearrange("p (f t) -> p f t", f=FC)
        for s_ in range(NST):
            for g0 in range(0, FC, 2):
                gw = min(2, FC - g0)
                wi = rot('wb', NWB)
                wload(wi, DC, 512, W("ffn_w_in")[l][:, g0 * 128:(g0 + gw) * 128], 0, gw * 128)
                wload(wi, DC, 512, W("ffn_w_in")[l][:, DFF + g0 * 128:DFF + (g0 + gw) * 128], 256, gw * 128)
                wv = wview(wi, DC, 512)
                for j in range(gw):
                    f = g0 + j
                    for tt in range(TPS):
                        t = s_ * TPS + tt
                        pa, pb = rot('ps', 6), rot('ps', 6)
                        for (pi, off) in ((pa, 0), (pb, 256)):
                            for c in range(DC):
                                MM(pi, psums[pi][:, :TT], wv[:, c, off + j * 128:off + (j + 1) * 128], xn[:, c, xsl(t)],
                                   c == 0, c == DC - 1, (('wb', wi), ('xn', c, t)))
                        ai = rot('tmpA', 3)
                        S.op('act', 'activation', (('ps', pa),), (('tmpA', ai),),
                             out=tmpA[ai][:], in_=psums[pa][:, :TT], func=AF.Silu)
                        S.op('dve', 'tensor_tensor', (('ps', pb), ('tmpA', ai)), (('big', f, tt),),
                             out=hid[:, f, tt * TT:(tt + 1) * TT], in0=psums[pb][:, :TT], in1=tmpA[ai][:], op=ALU.mult)
            for m0 in range(0, DC, 1):
                wi = rot('wb', NWB)
                wload(wi, FC, 128, W("ffn_w_out")[l][:, m0 * 128:(m0 + 1) * 128])
                wv = wview(wi, FC, 128)
                for j in range(1):
                    mo = m0 + j
                    for tt in range(TPS):
                        t = s_ * TPS + tt
                        pi = rot('ps', 6)
                        for f in range(FC):
                            MM(pi, psums[pi][:, :TT], wv[:, f, j * 128:(j + 1) * 128], hid[:, f, tt * TT:(tt + 1) * TT],
                               f == 0, f == FC - 1, (('wb', wi), ('big', f, tt)))
                        S.op('dve', 'tensor_tensor', (('ps', pi), ('h', mo, t)), (('h', mo, t),),
                             out=hT[:, mo, tsl(t)], in0=psums[pi][:, :TT], in1=hT[:, mo, tsl(t)], op=ALU.add)

    def ple(l):
        rmsnorm_xn('norm_ple%d' % l)
        for tb in range(NB):
            si = rot('stage', 2)
            S.dma('sp', (), (('stage', si),), out=stage[si][:, :DPLE], in_=W("p")[l, tb * 128:(tb + 1) * 128, :])
            pi = rot('ps', 6)
            for j in range(2):
                TR(pi, psums[pi][:, j * 128:(j + 1) * 128], stage[si][:, j * 128:(j + 1) * 128], (('stage', si),))
            S.op('act', 'activation', (('ps', pi),), (('pT', tb * 128 // TT),),
                 out=pT[:, :, tb * 128:(tb + 1) * 128], in_=psums[pi][:, :256].rearrange("p (j t) -> p j t", j=2), func=AF.Copy)
        S.barrier()
        wp = 'pleproj'
        wpv = big[:, 0:2048].rearrange("p (k m) -> p k m", k=2)
        S.dma('pool', (), (('wb', wp),), out=wpv, in_=W("ple_w_proj")[l].rearrange("(k p) m -> p k m", p=128))
        for mg in range(2):
            wi = rot('wb', NWB)
            wload(wi, DC, 512, W("ple_w_gate")[l][:, mg * 512:(mg + 1) * 512])
            wv = wview(wi, DC, 512)
            for j in range(4):
                mo = mg * 4 + j
                for t in range(NT):
                    pa, pb = rot('ps', 6), rot('ps', 6)
                    for kc in range(2):
                        MM(pa, psums[pa][:, :TT], wpv[:, kc, mo * 128:(mo + 1) * 128], pT[:, kc, tsl(t)],
                           kc == 0, kc == 1, (('wb', wp), ('pT', t)))
                    for c in range(DC):
                        MM(pb, psums[pb][:, :TT], wv[:, c, j * 128:(j + 1) * 128], xn[:, c, xsl(t)],
                           c == 0, c == DC - 1, (('wb', wi), ('xn', c, t)))
                    ai, bi = rot('tmpA', 3), rot('tmpA', 3)
                    S.op('act', 'activation', (('ps', pb),), (('tmpA', ai),),
                         out=tmpA[ai][:], in_=psums[pb][:, :TT], func=AF.Sigmoid)
                    S.op('dve', 'tensor_tensor', (('ps', pa), ('tmpA', ai)), (('tmpA', bi),),
                         out=tmpA[bi][:], in0=psums[pa][:, :TT], in1=tmpA[ai][:], op=ALU.mult)
                    S.op('dve', 'tensor_tensor', (('tmpA', bi), ('h', mo, t)), (('h', mo, t),),
                         out=hT[:, mo, tsl(t)], in0=hT[:, mo, tsl(t)], in1=tmpA[bi][:], op=ALU.add)

    def load_halo():
        S.dma('sp', (), ('smal',), out=smal[:, :DC * HALO], in_=W("halo"))
        S.op('dve', 'tensor_copy', ('smal',), ('xnhalo',), out=xn[:, :, 0:HALO],
             in_=smal[:, :DC * HALO].rearrange("p (c k) -> p c k", c=DC))

    def tail_out():
        tl = smal[:, :DC * HALO].rearrange("p (c k) -> p c k", c=DC)
        S.op('dve', 'tensor_copy', tuple(('xn', c, NT - 1) for c in range(DC)) + ('smal',), ('smal',), out=tl, in_=xn[:, :, TL:TL + HALO])
        S.dma('sp', ('smal',), ('o_tail',), out=OUT("tail", [128, 32]), in_=smal[:, :32])
        out_keys.append('o_tail')

    def hawk(l, final_pass):
        j = l // 2
        V = lambda n, c=0: vcol('h%d_%s' % (j, n), c)
        rmsnorm_xn('norm_mix%d' % l)
        load_halo()
        S.barrier()
        fa = arena[:].bitcast(F32)
        W4 = TL + 4
        recb = fa[:, 0:W4]
        cv = fa[:, W4:W4 + TL]
        Abuf = fa[:, 2 * W4:2 * W4 + TL]
        Hl = fa[:, 3 * W4:3 * W4 + TL]
        o16 = 5 * W4 * 2
        cvb = arena[:, o16:o16 + TL]
        P1 = arena[:, o16 + TL:o16 + 2 * TL]
        c8 = smal[:, 64:74]
        c16 = smal[:, 74:84]
        rsum = smal[:, 84:84 + NT]
        rtot = smal[:, 92:93]
        hin = smal[:, 96:106]
        carry = smal[:, 106:116]
        S.op('act', 'activation', ('vecs',), ('c8',), out=c8, in_=vecs[:, VOFF['h%d_lam' % j]:VOFF['h%d_lam' % j] + RC], func=AF.Exp, scale=-1.0)
        S.op('act', 'activation', ('c8', 'vecs'), ('c8',), out=c8, in_=c8, func=AF.Ln, bias=vcol('one'))
        S.op('dve', 'tensor_scalar', ('c8',), ('c16',), out=c16, in0=c8, scalar1=-16.0, scalar2=None, op0=ALU.mult)
        S.op('dve', 'tensor_scalar', ('c8', 'c16'), ('c8',), out=c8, in0=c8, scalar1=-8.0, scalar2=None, op0=ALU.mult)
        if True:
            if final_pass:
                S.dma('sp', (), ('gath',), out=gath[:], in_=W("gsumm"))
                S.op('dve', 'memset', (), ('carry',), carry, 0.0)
                S.op('dve', 'memset', (), ('hin',), hin, 0.0)
                for r in range(NCORES):
                    S.op('dve', 'scalar_tensor_tensor', ('carry', 'cvec', 'hin'), ('hin',),
                         out=hin, in0=carry, scalar=cvec[:, r:r + 1], in1=hin, op0=ALU.mult, op1=ALU.add)
                    if r < NCORES - 1:
                        S.op('dve', 'tensor_tensor', ('carry', 'gath'), ('carry',), out=carry, in0=carry, in1=gath[:, r, RC:2 * RC], op=ALU.mult)
                        S.op('dve', 'tensor_tensor', ('carry', 'gath'), ('carry',), out=carry, in0=carry, in1=gath[:, r, 0:RC], op=ALU.add)
            for ct in range(RC):
                wi = rot('wb', NWB)
                wb_ = wbufs[wi]
                wrec = wb_[:, 0:1024].rearrange("p (k m) -> p k m", k=DC)
                wgat = wb_[:, 1024:2048].rearrange("p (k m) -> p k m", k=DC)
                wa = wb_[:, 2048:2176]
                wx = wb_[:, 2176:2304]
                wo = wb_[:, 2304:3328]
                wkey = ('wb', wi)
                S.dma('pool', (), (wkey,), out=wrec, in_=W("lru_w_in")[j][:, DR + ct * 128:DR + (ct + 1) * 128].rearrange("(k p) m -> p k m", p=128))
                S.dma('pool', (), (wkey,), out=wa, in_=W("lru_w_gate_a")[j][ct * 128:(ct + 1) * 128, :])
                S.dma('pool', (), (wkey,), out=wx, in_=W("lru_w_gate_x")[j][ct * 128:(ct + 1) * 128, :])
                if final_pass:
                    S.dma('pool', (), (wkey,), out=wgat, in_=W("lru_w_in")[j][:, ct * 128:(ct + 1) * 128].rearrange("(k p) m -> p k m", p=128))
                    S.dma('pool', (), (wkey,), out=wo, in_=W("lru_w_out")[j][ct * 128:(ct + 1) * 128, :])
                pi = rot('ps', 6)
                for c in range(DC):
                    MM(pi, psums[pi][:, :HALO], wrec[:, c, :], xn[:, c, 0:HALO], c == 0, c == DC - 1, (wkey, 'xnhalo'))
                S.op('dve', 'tensor_scalar', (('ps', pi), 'vecs', 'cvec'), ('recb',), out=recb[:, 0:HALO], in0=psums[pi][:, :HALO],
                     scalar1=V('br', ct), scalar2=cvec[:, 16:17], op0=ALU.add, op1=ALU.mult)
                for t in range(NT):
                    pi = rot('ps', 6)
                    for c in range(DC):
                        MM(pi, psums[pi][:, :TT], wrec[:, c, :], xn[:, c, xsl(t)], c == 0, c == DC - 1, (wkey, ('xn', c, t)))
                    S.op('act', 'activation', (('ps', pi), 'vecs'), ('recb',), out=recb[:, HALO + t * TT:HALO + (t + 1) * TT],
                         in_=psums[pi][:, :TT], func=AF.Identity, bias=V('br', ct))
                S.op('dve', 'tensor_scalar', ('recb', 'vecs'), ('cv',), out=cv, in0=recb[:, 1:1 + TL],
                     scalar1=V('cw0', ct), scalar2=V('cb', ct), op0=ALU.mult, op1=ALU.add)
                for k in range(1, 4):
                    S.op('dve', 'scalar_tensor_tensor', ('recb', 'vecs', 'cv'), ('cv',), out=cv, in0=recb[:, 1 + k:1 + k + TL],
                         scalar=V('cw%d' % k, ct), in1=cv, op0=ALU.mult, op1=ALU.add)
                S.op('act', 'activation', ('cv',), ('cvb',), out=cvb, in_=cv, func=AF.Copy)
                for t in range(NT):
                    pa, pb = rot('ps', 6), rot('ps', 6)
                    MM(pa, psums[pa][:, :TT], wa, cvb[:, tsl(t)], True, True, (wkey, 'cvb'))
                    MM(pb, psums[pb][:, :TT], wx, cvb[:, tsl(t)], True, True, (wkey, 'cvb'))
                    ri_, ii_, mi_ = rot('tmpA', 3), rot('tmpA', 3), rot('tmpA', 3)
                    S.op('act', 'activation', (('ps', pa), 'vecs'), (('tmpA', ri_),), out=tmpA[ri_][:], in_=psums[pa][:, :TT],
                         func=AF.Sigmoid, bias=V('ba', ct))
                    S.op('act', 'activation', (('ps', pb), 'vecs'), (('tmpA', ii_),), out=tmpA[ii_][:], in_=psums[pb][:, :TT],
                         func=AF.Sigmoid, bias=V('bx', ct))
                    S.op('dve', 'tensor_tensor', ('cv', ('tmpA', ii_)), ('cv',), out=cv[:, tsl(t)], in0=cv[:, tsl(t)], in1=tmpA[ii_][:], op=ALU.mult)
                    S.op('act', 'activation', (('tmpA', ri_), 'c8'), ('Abuf',), out=Abuf[:, tsl(t)], in_=tmpA[ri_][:],
                         func=AF.Exp, scale=c8[:, ct:ct + 1])
                    S.op('act', 'activation', (('tmpA', ri_), 'c16'), (('tmpA', mi_),), out=tmpA[mi_][:], in_=tmpA[ri_][:],
                         func=AF.Exp, scale=c16[:, ct:ct + 1])
                    if not final_pass:
                        S.op('dve', 'tensor_reduce', (('tmpA', ri_),), ('rsum',), out=rsum[:, t:t + 1], in_=tmpA[ri_][:], axis=mybir.AxisListType.X, op=ALU.add)
                    S.op('act', 'activation', (('tmpA', mi_), 'vecs'), (('tmpA', mi_),), out=tmpA[mi_][:], in_=tmpA[mi_][:],
                         func=AF.Sqrt, scale=-1.0, bias=vcol('one'))
                    S.op('dve', 'tensor_tensor', ('cv', ('tmpA', mi_)), ('cv',), out=cv[:, tsl(t)], in0=cv[:, tsl(t)], in1=tmpA[mi_][:], op=ALU.mult)
                S.op('dve', 'tensor_tensor_scan', ('Abuf', 'cv', 'hin'), ('Hl',), out=Hl, data0=Abuf, data1=cv,
                     initial=(hin[:, ct:ct + 1] if final_pass else 0.0), op0=ALU.mult, op1=ALU.add)
                if not final_pass:
                    S.op('dve', 'tensor_copy', ('Hl',), ('summ',), out=smal[:, ct:ct + 1], in_=Hl[:, TL - 1:TL])
                    S.op('dve', 'tensor_reduce', ('rsum',), ('rtot',), out=rtot, in_=rsum, axis=mybir.AxisListType.X, op=ALU.add)
                    S.op('act', 'activation', ('rtot', 'c8'), ('summ',), out=smal[:, RC + ct:RC + ct + 1], in_=rtot, func=AF.Exp, scale=c8[:, ct:ct + 1])
                    continue
                for t in range(NT):
                    pi = rot('ps', 6)
                    for c in range(DC):
                        MM(pi, psums[pi][:, :TT], wgat[:, c, :], xn[:, c, xsl(t)], c == 0, c == DC - 1, (wkey, ('xn', c, t)))
                    gi = rot('tmpA', 3)
                    S.op('act', 'activation', (('ps', pi), 'vecs'), (('tmpA', gi),), out=tmpA[gi][:], in_=psums[pi][:, :TT],
                         func=AF.Gelu_apprx_tanh, bias=V('bg', ct))
                    S.op('dve', 'tensor_tensor', ('Hl', ('tmpA', gi)), ('P1',), out=P1[:, tsl(t)], in0=Hl[:, tsl(t)], in1=tmpA[gi][:], op=ALU.mult)
                    for mo in range(DC):
                        pi = rot('ps', 6)
                        MM(pi, psums[pi][:, :TT], wo[:, mo * 128:(mo + 1) * 128], P1[:, tsl(t)], True, True, (wkey, 'P1'))
                        if ct == 0:
                            S.op('dve', 'scalar_tensor_tensor', (('ps', pi), ('h', mo, t), 'vecs'), (('h', mo, t),), out=hT[:, mo, tsl(t)],
                                 in0=psums[pi][:, :TT], scalar=V('bo', mo), in1=hT[:, mo, tsl(t)], op0=ALU.add, op1=ALU.add)
                        else:
                            S.op('dve', 'tensor_tensor', (('ps', pi), ('h', mo, t)), (('h', mo, t),), out=hT[:, mo, tsl(t)],
                                 in0=psums[pi][:, :TT], in1=hT[:, mo, tsl(t)], op=ALU.add)
        if not final_pass:
            S.dma('sp', ('summ',), ('o_summ',), out=OUT("summ", [128, 32]), in_=smal[:, :32])
            out_keys.append('o_summ')
        S.barrier()

    def rwkv(l, final_pass):
        j = l // 2
        RT = 128
        NRT = TL // RT
        V = lambda n, c=0: vcol('r%d_%s' % (j, n), c)
        rmsnorm_xn('norm_mix%d' % l)
        load_halo()
        S.barrier()
        off = [0]

        def A32(n):
            o = off[0]
            off[0] += 2 * n
            assert off[0] <= ARENA // 2, (off[0], ARENA // 2)
            return arena[:, o:o + 2 * n].bitcast(F32)

        def A16(n):
            o = off[0]
            off[0] += n
            assert off[0] <= ARENA // 2, (off[0], ARENA // 2)
            return arena[:, o:o + n]

        LW1, LA1, LV1, LG1a, LG1b = A16(TL), A16(TL), A16(TL), A16(TL), A16(TL)
        WS = A32(1280)
        rc = A32(704)
        maskA, bdones, cmask, I2 = rc[:, 0:384], rc[:, 384:512], rc[:, 512:640], rc[:, 640:704]
        omu = A32(48)
        (R, K, LW, ALR, CUM, WC, WINV, WPRV, KKN, T1, T2) = [A32(RT) for _ in range(11)]
        Vv2, KT2, BT2, BH2, KH2, CB2 = [[A32(RT) for _ in range(2)] for _ in range(6)]
        AR2 = [A32(2 * RT).rearrange("p (a t) -> p a t", a=2) for _ in range(2)]
        WCE2 = [A32(2) for _ in range(2)]
        TM = A32(512)
        GMa = A32(768).rearrange("p (h n) -> p h n", h=2)
        GMb = A32(512).rearrange("p (h n) -> p h n", h=2)
        NL = [A32(512).rearrange("p (h n) -> p h n", h=2) for _ in range(2)]
        Tm = [A32(256).rearrange("p (h n) -> p h n", h=2) for _ in range(2)]
        Zs, Aps, Vp = A32(128), A32(128), A32(128)
        Us = [A32(128) for _ in range(2)]
        Sa = [A32(128) for _ in range(2)]
        Ytm, YN, O1, O2, STt = A32(128), A32(128), A32(128), A32(128), A32(16)
        G8 = [A32(256).rearrange("p (h n) -> p h n", h=2) for _ in range(2)]
        carry, Sin = [A32(64) for _ in range(2)], [A32(64) for _ in range(2)]
        SUMT = A32(256).rearrange("p (h n) -> p h n", h=2)
        OTb = A16(128)
        i64 = ident[0:64, 0:64]
        cnt = {'TM': 0, 'us': 0, 'g8': 0}

        S.dma('sp', (), ('rc',), out=rc, in_=W("rconst"))
        mu0 = VOFF['r%d_mu0' % j]
        S.op('dve', 'tensor_scalar', ('vecs',), ('omu',), out=omu, in0=vecs[:, mu0:mu0 + 48], scalar1=-1.0, scalar2=1.0, op0=ALU.mult, op1=ALU.add)

        def mixed_weights(src, m, mi, dst1, dst2, key):
            wsv = WS[:, :DC * m].rearrange("p (k m) -> p k m", k=DC)
            S.dma('sp', (), ('WS',), out=wsv, in_=src.rearrange("(k p) m -> p k m", p=128))
            for c in range(DC):
                S.op('dve', 'tensor_scalar', ('WS', 'omu'), (key,), out=dst1[:, c, :], in0=wsv[:, c, :], scalar1=omu[:, mi * 8 + c:mi * 8 + c + 1], scalar2=None, op0=ALU.mult)
                S.op('pool', 'tensor_scalar', ('WS', 'vecs'), (key,), out=dst2[:, c, :], in0=wsv[:, c, :], scalar1=vecs[:, mu0 + mi * 8 + c:mu0 + mi * 8 + c + 1], scalar2=None, op0=ALU.mult)

        def proj16(pi, out, w1v, w2v, m0, m1, cols0, n, wkey):
            for c in range(DC):
                MM(pi, out, w1v[:, c, m0:m1], xn[:, c, HALO + cols0:HALO + cols0 + n], c == 0, False, (wkey, ('xn', c, cols0 // TT), 'xnhalo'))
            for c in range(DC):
                MM(pi, out, w2v[:, c, m0:m1], xn[:, c, HALO + cols0 - 1:HALO + cols0 + n - 1], False, c == DC - 1,
                   (wkey, ('xn', c, cols0 // TT), ('xn', c, max(cols0 - 1, 0) // TT), 'xnhalo'))

        lora_list = [('rw_w1', 64, 1, [(0, 64, LW1, AF.Tanh)]), ('rw_a1', 64, 4, [(0, 64, LA1, AF.Copy)]),
                     ('rw_g1', 160, 5, [(0, 128, LG1a, AF.Sigmoid), (128, 160, LG1b, AF.Sigmoid)])]
        if j == 1:
            lora_list.append(('rw_v1', 32, 3, [(0, 32, LV1, AF.Copy)]))
        for (wn, m, mi, outs) in lora_list:
            wi = rot('wb', NWB)
            wkey = ('wb', wi)
            d1 = wbufs[wi][:, 0:DC * m].rearrange("p (k m) -> p k m", k=DC)
            d2 = wbufs[wi][:, DC * m:2 * DC * m].rearrange("p (k m) -> p k m", k=DC)
            src = W(wn)[0 if wn == 'rw_v1' else j]
            mixed_weights(src, m, mi, d1, d2, wkey)
            for t in range(NT):
                for (m0, m1, dst, fn) in outs:
                    pi = rot('ps', 6)
                    proj16(pi, psums[pi][0:m1 - m0, :TT], d1, d2, m0, m1, t * TT, TT, wkey)
                    S.op('act', 'activation', (('ps', pi),), ('lora',), out=dst[0:m1 - m0, tsl(t)], in_=psums[pi][0:m1 - m0, :TT], func=fn)

        for ct in range(DC):
            wa_i, wb_i = rot('wb', NWB), rot('wb', NWB)
            ka, kb = ('wb', wa_i), ('wb', wb_i)
            bufa, bufb = wbufs[wa_i], wbufs[wb_i]
            wv_ = lambda buf, o: buf[:, o:o + 1024].rearrange("p (k m) -> p k m", k=DC)
            r1, r2, k1, k2 = wv_(bufa, 0), wv_(bufa, 1024), wv_(bufa, 2048), wv_(bufa, 3072)
            v1_, v2_ = wv_(bufb, 0), wv_(bufb, 1024)
            w2c, a2c, v2c = bufb[0:64, 2048:2176], bufb[0:64, 2176:2304], bufb[0:32, 2304:2432]
            g2a, g2b = bufb[:, 2432:2560], bufb[0:32, 2560:2688]
            wo = bufb[:, 2688:3712]
            cs_ = slice(ct * 128, (ct + 1) * 128)
            mixed_weights(W("rw_w_rkv")[j, 0][:, cs_], 128, 0, r1, r2, ka)
            mixed_weights(W("rw_w_rkv")[j, 1][:, cs_], 128, 2, k1, k2, ka)
            mixed_weights(W("rw_w_rkv")[j, 2][:, cs_], 128, 3, v1_, v2_, kb)
            S.dma('pool', (), (kb,), out=w2c, in_=W("rw_w2")[j][:, cs_])
            S.dma('pool', (), (kb,), out=a2c, in_=W("rw_a2")[j][:, cs_])
            if j == 1:
                S.dma('pool', (), (kb,), out=v2c, in_=W("rw_v2")[0][:, cs_])
            if final_pass:
                S.dma('pool', (), (kb,), out=g2a, in_=W("rw_g2")[j][0:128, cs_])
                S.dma('pool', (), (kb,), out=g2b, in_=W("rw_g2")[j][128:160, cs_])
                S.dma('pool', (), (kb,), out=wo, in_=W("rw_w_o")[j][cs_, :])
            S.op('dve', 'memset', (), (('Sa', 0, 0), ('Sa', 0, 1)), Sa[0], 0.0)
            if not final_pass:
                S.op('dve', 'tensor_copy', ('rc', ('Sa', 0, 0), ('Sa', 0, 1)), (('Sa', 0, 0), ('Sa', 0, 1)), out=Sa[0][:, 64:128], in_=I2)
            else:
                for h in range(2):
                    S.op('dve', 'memset', (), (('carry', h),), carry[h][0:64, :], 0.0)
                    S.op('dve', 'memset', (), (('Sin', h),), Sin[h][0:64, :], 0.0)
                for r in range(NCORES):
                    if r < NCORES - 1:
                        gi = cnt['g8'] % 2
                        cnt['g8'] += 1
                        S.dma('sp', (), (('G8', gi),), out=G8[gi][0:64, :, :], in_=W("grsumm")[r, 2 * ct:2 * ct + 2].rearrange("h k n -> k h n"))
                    for h in range(2):
                        S.op('dve', 'scalar_tensor_tensor', (('carry', h), ('Sin', h), 'cvec'), (('Sin', h),), out=Sin[h][0:64, :],
                             in0=carry[h][0:64, :], scalar=cvec[0:64, r:r + 1], in1=Sin[h][0:64, :], op0=ALU.mult, op1=ALU.add)
                        if r < NCORES - 1:
                            pi = rot('ps', 6)
                            MM(pi, psums[pi][0:64, 0:64], G8[gi][0:64, h, 64:128], carry[h][0:64, :], True, True, (('G8', gi), ('carry', h)))
                            S.op('dve', 'tensor_tensor', (('ps', pi), ('G8', gi)), (('carry', h),), out=carry[h][0:64, :],
                                 in0=psums[pi][0:64, 0:64], in1=G8[gi][0:64, h, 0:64], op=ALU.add)
                for h in range(2):
                    S.op('act', 'activation', (('Sin', h), ('Sa', 0, h)), (('Sa', 0, h),), out=Sa[0][h * 64:(h + 1) * 64, 0:64], in_=Sin[h][0:64, :], func=AF.Copy)
            par = [0]
            NA = 64 if final_pass else 128
            def prep(t, b):
                Vv, AR, KT, BT, BH, KH, WCE, CB = Vv2[b], AR2[b], KT2[b], BT2[b], BH2[b], KH2[b], WCE2[b], CB2[b]
                fmV = ('fm', id(Vv))
                c0 = t * RT
                ht = c0 // TT
                tcols = slice(c0, c0 + RT)
                py = [6, 7]
                cs = None
                c0 = t * RT
                ht = c0 // TT
                tcols = slice(c0, c0 + RT)
                for (w1v, w2v, dst, wkey) in ((r1, r2, R, ka), (k1, k2, K, ka), (v1_, v2_, Vv, kb)):
                    pi = rot('ps', 6)
                    proj16(pi, psums[pi][:, :RT], w1v, w2v, 0, 128, c0, RT, wkey)
                    S.op('act', 'activation', (('ps', pi),), (('fm', id(dst)),), out=dst, in_=psums[pi][:, :RT], func=AF.Copy)
                pi = rot('ps', 6)
                MM(pi, psums[pi][:, :RT], w2c, LW1[0:64, tcols], True, True, (kb, 'lora'))
                S.op('act', 'activation', (('ps', pi), 'vecs'), ('LW',), out=LW, in_=psums[pi][:, :RT], func=AF.Sigmoid, bias=V('w0', ct))
                yield
                S.op('dve', 'tensor_scalar', ('LW',), ('LW',), out=LW, in0=LW, scalar1=-0.6065306597126334, scalar2=None, op0=ALU.mult)
                yield
                pi = rot('ps', 6)
                MM(pi, psums[pi][:, :RT], a2c, LA1[0:64, tcols], True, True, (kb, 'lora'))
                S.op('act', 'activation', (('ps', pi), 'vecs'), ('ALR',), out=ALR, in_=psums[pi][:, :RT], func=AF.Sigmoid, bias=V('a0', ct))
                yield
                fmV = ('fm', id(Vv))
                fmK = ('fm', id(K))
                fmR = ('fm', id(R))
                if j == 1:
                    pi = rot('ps', 6)
                    MM(pi, psums[pi][:, :RT], v2c, LV1[0:32, tcols], True, True, (kb, 'lora'))
                    S.op('act', 'activation', (('ps', pi), 'vecs'), ('T2',), out=T2, in_=psums[pi][:, :RT], func=AF.Sigmoid, bias=V('v0', ct))
                    S.dma('sp', (), ('T1',), out=T1, in_=W("vfirst")[:, ct, tcols])
                    S.op('dve', 'tensor_tensor', ('T1', fmV), ('T1',), out=T1, in0=T1, in1=Vv, op=ALU.subtract)
                    S.op('dve', 'tensor_tensor', ('T1', 'T2'), ('T1',), out=T1, in0=T1, in1=T2, op=ALU.mult)
                    S.op('dve', 'tensor_tensor', ('T1', fmV), (fmV,), out=Vv, in0=Vv, in1=T1, op=ALU.add)
                elif final_pass:
                    S.dma('sp', (fmV,), (('o_vf', ct, t),), out=OUT("vfirst_out", [128, DC, TL])[:, ct, tcols], in_=Vv)
                    out_keys.append(('o_vf', ct, t))
                S.op('pool', 'tensor_scalar', (fmK, 'vecs'), ('KKN',), out=KKN, in0=K, scalar1=V('kk', ct), scalar2=None, op0=ALU.mult)
                yield
                S.op('pool', 'tensor_tensor', ('KKN',), ('T1',), out=T1, in0=KKN, in1=KKN, op=ALU.mult)
                yield
                pi = rot('ps', 6)
                MM(pi, psums[pi][:, :RT], bdones, T1, True, True, ('rc', 'T1'))
                S.op('act', 'activation', (('ps', pi),), ('T1',), out=T1, in_=psums[pi][:, :RT], func=AF.Sqrt)
                yield
                S.op('dve', 'tensor_scalar', ('T1',), ('T1',), out=T1, in0=T1, scalar1=1e-12, scalar2=None, op0=ALU.max)
                yield
                S.op('dve', 'reciprocal', ('T1',), ('T1',), out=T1, in_=T1)
                yield
                S.op('dve', 'tensor_tensor', ('KKN', 'T1'), ('KKN',), out=KKN, in0=KKN, in1=T1, op=ALU.mult)
                yield
                S.op('dve', 'tensor_scalar', ('ALR', 'vecs'), ('T1',), out=T1, in0=ALR, scalar1=-1.0, scalar2=V('ka', ct), op0=ALU.add, op1=ALU.mult)
                yield
                S.op('dve', 'scalar_tensor_tensor', ('T1', fmK), (fmK,), out=K, in0=T1, scalar=1.0, in1=K, op0=ALU.add, op1=ALU.mult)
                yield
                S.op('dve', 'tensor_tensor_scan', ('rc', 'LW'), ('CUM',), out=CUM, data0=cmask, data1=LW, initial=0.0, op0=ALU.mult, op1=ALU.add)
                yield
                S.op('act', 'activation', ('CUM',), ('WC',), out=WC, in_=CUM, func=AF.Exp)
                yield
                S.op('act', 'activation', ('CUM',), ('WINV',), out=WINV, in_=CUM, func=AF.Exp, scale=-1.0)
                yield
                S.op('dve', 'tensor_tensor', ('CUM', 'LW'), ('T1',), out=T1, in0=CUM, in1=LW, op=ALU.subtract)
                yield
                S.op('act', 'activation', ('T1',), ('WPRV',), out=WPRV, in_=T1, func=AF.Exp)
                yield
                S.op('dve', 'tensor_copy', ('WC',), (('WCE', b),), out=WCE, in_=WC.rearrange("p (c k) -> p c k", k=64)[:, :, 63])
                yield
                S.op('dve', 'scalar_tensor_tensor', ('KKN', 'WPRV'), (('AR0', b),), out=AR[:, 0, :], in0=KKN, scalar=-1.0, in1=WPRV, op0=ALU.mult, op1=ALU.mult)
                yield
                S.op('pool', 'tensor_tensor', (fmR, 'WC'), (('AR1', b),), out=AR[:, 1, :], in0=R, in1=WC, op=ALU.mult)
                yield
                S.op('pool', 'tensor_tensor', (fmK, 'WINV'), (('KT', b),), out=KT, in0=K, in1=WINV, op=ALU.mult)
                yield
                S.op('dve', 'tensor_tensor', ('KKN', 'ALR'), ('T1',), out=T1, in0=KKN, in1=ALR, op=ALU.mult)
                yield
                S.op('dve', 'tensor_tensor', ('T1', 'WINV'), (('BT', b),), out=BT, in0=T1, in1=WINV, op=ALU.mult)
                yield
                for c in range(2):
                    cs = slice(c * 64, (c + 1) * 64)
                    S.op('dve', 'tensor_scalar', (('BT', b), ('WCE', b)), (('BH', b),), out=BH[:, cs], in0=BT[:, cs], scalar1=WCE[:, c:c + 1], scalar2=None, op0=ALU.mult)
                    S.op('pool', 'tensor_scalar', (('KT', b), ('WCE', b)), (('KH', b),), out=KH[:, cs], in0=KT[:, cs], scalar1=WCE[:, c:c + 1], scalar2=None, op0=ALU.mult)
                if final_pass:
                    S.op('pool', 'tensor_tensor', (fmR, fmK), ('T2',), out=T2, in0=R, in1=K, op=ALU.mult)
                    S.op('pool', 'tensor_scalar', ('T2', 'vecs'), ('T2',), out=T2, in0=T2, scalar1=V('rk', ct), scalar2=None, op0=ALU.mult)
                    pi = rot('ps', 6)
                    MM(pi, psums[pi][:, :RT], bdones, T2, True, True, ('rc', 'T2'))
                    S.op('act', 'activation', (('ps', pi),), (('CB', b),), out=CB, in_=psums[pi][:, :RT], func=AF.Copy)
                    py = [6, 7]
                yield

            def chunks(t, b):
                Vv, AR, KT, BT, BH, KH, WCE, CB = Vv2[b], AR2[b], KT2[b], BT2[b], BH2[b], KH2[b], WCE2[b], CB2[b]
                fmV = ('fm', id(Vv))
                c0 = t * RT
                ht = c0 // TT
                tcols = slice(c0, c0 + RT)
                py = [6, 7]
                cs = None
                hb = [slice(0, 64), slice(64, 128)]
                csl = [slice(0, 64), slice(64, 128)]
                hc = lambda h: slice(h * 64, (h + 1) * 64)
                pi = rot('ps', 6)
                for q, (src, sk_) in enumerate(((Vv, fmV), (AR[:, 0, :], ('AR0', b)), (BH, ('BH', b)), (KH, ('KH', b)))):
                    TR(pi, psums[pi][:, q * 128:(q + 1) * 128], src, (sk_,))
                S.op('act', 'activation', (('ps', pi),), ('TM',), out=TM[:, 0:256], in_=psums[pi][:, 0:256], func=AF.Copy)
                S.op('dve', 'tensor_copy', (('ps', pi),), ('TM',), out=TM[:, 256:512], in_=psums[pi][:, 256:512])
                Vtm = lambda h: TM[:, h * 64:(h + 1) * 64]
                Atm = lambda h: TM[:, 128 + h * 64:128 + (h + 1) * 64]
                Bhtm = lambda h: TM[:, 256 + h * 64:256 + (h + 1) * 64]
                Khtm = lambda h: TM[:, 384 + h * 64:384 + (h + 1) * 64]
                yield
                for h in range(2):
                    bh = hb[h]
                    pa = rot('ps', 6)
                    MM(pa, psums[pa][:, 0:256], BT[bh, :], AR[bh, :, :], True, True, (('BT', b), ('AR0', b), ('AR1', b)))
                    MM(pa, psums[pa][:, 256:384], AR[bh, 0, :], BT[bh, :], True, True, (('BT', b), ('AR0', b)))
                    S.op('dve', 'tensor_tensor', (('ps', pa), 'rc'), (('GMa', h),), out=GMa[:, h, :], in0=psums[pa][:, 0:384], in1=maskA, op=ALU.mult)
                    pb_ = rot('ps', 6)
                    MM(pb_, psums[pb_][:, 0:256], KT[bh, :], AR[bh, :, :], True, True, (('KT', b), ('AR0', b), ('AR1', b)))
                    S.op('pool' if False else 'dve', 'tensor_tensor', (('ps', pb_), 'rc'), (('GMb', h),), out=GMb[:, h, :], in0=psums[pb_][:, 0:256], in1=maskA[:, 0:256], op=ALU.mult)
                yield
                for h in range(2):
                    S.op('dve', 'tensor_tensor', (('GMa', h), 'ident'), (('Tm', 0, h),), out=Tm[0][:, h, :], in0=GMa[:, h, 0:128], in1=ident[:], op=ALU.add)
                Np = [GMa[:, h, 0:128] for h in range(2)]
                Lp = [GMa[:, h, 256:384] for h in range(2)]
                nlk = [('GMa', h) for h in range(2)]
                for lev in range(1, 6):
                    nli = lev % 2
                    yield
                    pi = rot('ps', 6)
                    for h in range(2):
                        if lev < 5:
                            MM(pi, psums[pi][:, h * 256:h * 256 + 128], Lp[h], Np[h], True, True, (nlk[h],))
                        MM(pi, psums[pi][:, h * 256 + 128:h * 256 + 256], Np[h], Lp[h], True, True, (nlk[h],))
                    if lev < 5:
                        S.op('act', 'activation', (('ps', pi),), (('NL', nli, 0), ('NL', nli, 1)), out=NL[nli],
                             in_=psums[pi][:].rearrange("p (h n) -> p h n", h=2), func=AF.Copy)
                    else:
                        S.op('act', 'activation', (('ps', pi),), (('NL', nli, 0), ('NL', nli, 1)), out=NL[nli][:, :, 128:256],
                             in_=psums[pi][:].rearrange("p (h n) -> p h n", h=2)[:, :, 128:256], func=AF.Copy)
                    for h in range(2):
                        Np[h], Lp[h], nlk[h] = NL[nli][:, h, 0:128], NL[nli][:, h, 128:256], ('NL', nli, h)
                    tp, tn = (lev - 1) % 2, lev % 2
                    yield
                    pi = rot('ps', 6)
                    for h in range(2):
                        MM(pi, psums[pi][:, h * 128:(h + 1) * 128], Lp[h], Tm[tp][:, h, :], True, True, (nlk[h], ('Tm', tp, h)))
                    S.op('dve', 'tensor_tensor', (('ps', pi), ('Tm', tp, 0), ('Tm', tp, 1)), (('Tm', tn, 0), ('Tm', tn, 1)), out=Tm[tn],
                         in0=psums[pi][:, 0:256].rearrange("p (h n) -> p h n", h=2), in1=Tm[tp], op=ALU.add)
                Tf = lambda h: Tm[1][:, h, :]
                yield
                pz = rot('ps', 6)
                for h in range(2):
                    MM(pz, psums[pz][:, hc(h)], GMb[:, h, 0:128], Vtm(h), True, True, (('GMb', h), 'TM'))
                S.op('act', 'activation', (('ps', pz),), ('Zs',), out=Zs, in_=psums[pz][:, 0:128], func=AF.Copy)
                for h in range(2):
                    pa_ = rot('ps', 6)
                    MM(pa_, psums[pa_][hb[h], 0:128], Atm(h), Tf(h), True, True, ('TM', ('Tm', 1, h)))
                    S.op('dve', 'tensor_copy', (('ps', pa_),), (('Aps', h),), out=Aps[hb[h], :], in_=psums[pa_][hb[h], 0:128])
                yield
                pi = rot('ps', 6)
                for h in range(2):
                    MM(pi, psums[pi][:, hc(h)], Tf(h), Zs[:, hc(h)], True, True, (('Tm', 1, h), 'Zs'))
                S.op('act', 'activation', (('ps', pi),), ('Vp',), out=Vp, in_=psums[pi][:, 0:128], func=AF.Copy)
                for c in range(2):
                    yield
                    cs = csl[c]
                    cur, nxt = par[0], 1 - par[0]
                    par[0] = nxt
                    sc, sn = Sa[cur], Sa[nxt]
                    for h in range(2):
                        bh = hb[h]
                        pi = rot('ps', 6)
                        MM(pi, psums[pi][cs, 0:NA], Aps[bh, cs], sc[bh, 0:NA], True, True, (('Aps', h), ('Sa', cur, h)))
                        S.op('dve', 'tensor_tensor', (('ps', pi), 'Vp'), (('Us', h),), out=Us[h][cs, 0:64], in0=psums[pi][cs, 0:64], in1=Vp[cs, hc(h)], op=ALU.add)
                        if not final_pass:
                            S.op('act', 'activation', (('ps', pi),), (('Us', h),), out=Us[h][cs, 64:128], in_=psums[pi][cs, 64:128], func=AF.Copy)
                    if final_pass:
                        for h in range(2):
                            bh = hb[h]
                            MM(py[h], psums[py[h]][cs, 0:64], AR[bh, 1, cs], sc[bh, 0:64], True, True, (('AR1', b), ('Sa', cur, h)))
                    for h in range(2):
                        bh = hb[h]
                        pi = rot('ps', 6)
                        MM(pi, psums[pi][bh, 0:NA], Bhtm(h)[cs, :], Us[h][cs, 0:NA], True, False, ('TM', ('Us', h)))
                        MM(pi, psums[pi][bh, 0:64], Khtm(h)[cs, :], Vtm(h)[cs, :], False, True, ('TM',))
                        S.op('dve', 'scalar_tensor_tensor', (('ps', pi), ('Sa', cur, h), ('WCE', b)), (('Sa', nxt, h),), out=sn[bh, 0:NA],
                             in0=sc[bh, 0:NA], scalar=WCE[bh, c:c + 1], in1=psums[pi][bh, 0:NA], op0=ALU.mult, op1=ALU.add)
                if not final_pass:
                    return
                yield
                py2 = rot('ps', 6)
                for h in range(2):
                    MM(py2, psums[py2][:, hc(h)], GMa[:, h, 128:256], Us[h][:, 0:64], True, False, (('GMa', h), ('Us', h)))
                    MM(py2, psums[py2][:, hc(h)], GMb[:, h, 128:256], Vtm(h), False, True, (('GMb', h), 'TM'))
                yield
                for h in range(2):
                    S.op('act', 'activation', (('ps', py[h]),), ('Ytm',), out=Ytm[:, h * 64:(h + 1) * 64], in_=psums[py[h]][:, 0:64], func=AF.Copy)
                S.op('dve', 'tensor_tensor', (('ps', py2), 'Ytm'), ('Ytm',), out=Ytm, in0=psums[py2][:, 0:128], in1=Ytm, op=ALU.add)
                yield
                for h in range(2):
                    hs = slice(h * 64, (h + 1) * 64)
                    S.op('dve', 'tensor_reduce', ('Ytm',), ('STa',), out=STt[:, h:h + 1], in_=Ytm[:, hs], axis=mybir.AxisListType.X, op=ALU.add)
                    S.op('act', 'activation', ('Ytm',), ('O2', 'STb'), out=O2[:, hs], in_=Ytm[:, hs], func=AF.Square, accum_out=STt[:, 2 + h:3 + h])
                yield
                S.op('dve', 'tensor_scalar', ('STa',), ('STa',), out=STt[:, 0:2], in0=STt[:, 0:2], scalar1=1.0 / 64, scalar2=None, op0=ALU.mult)
                yield
                S.op('dve', 'tensor_tensor', ('STa',), ('STc',), out=STt[:, 4:6], in0=STt[:, 0:2], in1=STt[:, 0:2], op=ALU.mult)
                yield
                S.op('dve', 'scalar_tensor_tensor', ('STb', 'STc'), ('STb',), out=STt[:, 2:4], in0=STt[:, 2:4], scalar=1.0 / 64, in1=STt[:, 4:6], op0=ALU.mult, op1=ALU.subtract)
                yield
                S.op('act', 'activation', ('STb', 'vecs'), ('STb',), out=STt[:, 2:4], in_=STt[:, 2:4], func=AF.Sqrt, bias=vcol('gneps'))
                yield
                S.op('dve', 'reciprocal', ('STb',), ('STb',), out=STt[:, 2:4], in_=STt[:, 2:4])
                yield
                S.op('dve', 'scalar_tensor_tensor', ('STa', 'STb'), ('STc',), out=STt[:, 4:6], in0=STt[:, 0:2], scalar=-1.0, in1=STt[:, 2:4], op0=ALU.mult, op1=ALU.mult)
                yield
                for h in range(2):
                    hs = slice(h * 64, (h + 1) * 64)
                    S.op('act', 'activation', ('Ytm', 'STb', 'STc'), ('YN',), out=YN[:, hs], in_=Ytm[:, hs], func=AF.Identity,
                         scale=STt[:, 2 + h:3 + h], bias=STt[:, 4 + h:5 + h])
                yield
                pi = rot('ps', 6)
                TR(pi, psums[pi][:, 0:128], YN, ('YN',))
                yield
                S.op('dve', 'tensor_scalar', (('ps', pi), 'vecs'), ('O1',), out=O1, in0=psums[pi][:, 0:128], scalar1=V('lnw', ct), scalar2=V('lnb', ct), op0=ALU.mult, op1=ALU.add)
                yield
                S.op('dve', 'tensor_tensor', (('CB', b), fmV), ('O2',), out=O2, in0=CB, in1=Vv, op=ALU.mult)
                yield
                S.op('dve', 'tensor_tensor', ('O1', 'O2'), ('O1',), out=O1, in0=O1, in1=O2, op=ALU.add)
                yield
                pi = rot('ps', 6)
                MM(pi, psums[pi][:, 0:RT], g2a, LG1a[:, tcols], True, False, (kb, 'lora'))
                MM(pi, psums[pi][:, 0:RT], g2b, LG1b[0:32, tcols], False, True, (kb, 'lora'))
                yield
                S.op('dve', 'tensor_tensor', (('ps', pi), 'O1'), ('OTb',), out=OTb, in0=psums[pi][:, 0:RT], in1=O1, op=ALU.mult)
                yield
                for half in range(2):
                    pi = rot('ps', 6)
                    for q in range(4):
                        mo = half * 4 + q
                        MM(pi, psums[pi][:, q * 128:(q + 1) * 128], wo[:, mo * 128:(mo + 1) * 128], OTb, True, True, (kb, 'OTb'))
                    hk = tuple(('h', half * 4 + q, ht) for q in range(4))
                    S.op('dve', 'tensor_tensor', (('ps', pi),) + hk, hk, out=hT[:, half * 4:(half + 1) * 4, tcols],
                         in0=psums[pi][:].rearrange("p (q t) -> p q t", q=4), in1=hT[:, half * 4:(half + 1) * 4, tcols], op=ALU.add)
                yield

            def drive(*gens):
                gens = list(gens)
                while gens:
                    for g in list(gens):
                        try:
                            next(g)
                        except StopIteration:
                            gens.remove(g)

            prev = None
            for t in range(NRT):
                if prev is None:
                    drive(prep(t, t % 2))
                else:
                    drive(prev, prep(t, t % 2))
                prev = chunks(t, t % 2)
            drive(prev)
            if not final_pass:
                sc = Sa[par[0]]
                pi = rot('ps', 6)
                for h in range(2):
                    TR(pi, psums[pi][0:64, h * 64:(h + 1) * 64], sc[h * 64:(h + 1) * 64, 64:128], (('Sa', par[0], h),), b0=h * 64)
                    S.op('act', 'activation', (('Sa', par[0], h), 'SUMT'), ('SUMT',), out=SUMT[0:64, h, 0:64], in_=sc[h * 64:(h + 1) * 64, 0:64], func=AF.Copy)
                S.op('dve', 'tensor_copy', (('ps', pi), 'SUMT'), ('SUMT',), out=SUMT[0:64, :, 64:128], in_=psums[pi][0:64, 0:128].rearrange("p (h n) -> p h n", h=2))
                S.dma('sp', ('SUMT',), (('o_rs', ct),), out=OUT("rsumm", [16, 64, 128])[2 * ct:2 * ct + 2].rearrange("h k n -> k h n"), in_=SUMT[0:64, :, :])
                out_keys.append(('o_rs', ct))
        S.barrier()

    def final():
        xo = big[:, :DC * TT * 2].bitcast(F32).rearrange("p (c t) -> p c t", c=DC)
        outs = []
        for t in range(NT):
            ri = rms_tile(t)
            for c in range(DC):
                S.op('dve', 'scalar_tensor_tensor', (('h', c, t), ('rs', ri), 'vecs'), (('xo', c),),
                     out=xo[:, c, :], in0=hT[:, c, tsl(t)], scalar=vcol('norm_final', c),
                     in1=rs[ri][:], op0=ALU.mult, op1=ALU.mult)
            for b in range(TT // 128):
                si = rot('stage', 2)
                for half in range(2):
                    pi = rot('ps', 6)
                    for j in range(4):
                        c = half * 4 + j
                        TR(pi, psums[pi][:, j * 128:(j + 1) * 128], xo[:, c, b * 128:(b + 1) * 128], (('xo', c),))
                    S.op('act', 'activation', (('ps', pi),), (('stage', si),),
                         out=stage[si][:, half * 512:(half + 1) * 512], in_=psums[pi][:], func=AF.Copy)
                r0 = t * TT + b * 128
                key = ('yout', r0)
                S.dma('sp', (('stage', si),), (key,), out=OUT("y", [TL, D])[r0:r0 + 128, :], in_=stage[si][:])
                out_keys.append(key)

    def load_h():
        for c in range(DC):
            S.dma('sp', (), tuple(('h', c, t) for t in range(NT)), out=hT[:, c, :], in_=W("h_in")[:, c, :])

    def store_h():
        for c in range(DC):
            S.dma('sp', tuple(('h', c, t) for t in range(NT)), (('o_h', c),), out=OUT("h_out", [128, DC, TL])[:, c, :], in_=hT[:, c, :])
            out_keys.append(('o_h', c))

    if first:
        load_xT()
    else:
        load_h()
    for step in prog:
        kind = step[0]
        if kind == 'tail':
            rmsnorm_xn('norm_mix%d' % step[1])
            tail_out()
        elif kind in ('mix1', 'mix2'):
            l = step[1]
            if l % 2 == 0:
                hawk(l, kind == 'mix2')
            else:
                rwkv(l, kind == 'mix2')
        elif kind == 'ffn':
            ffn(step[1])
        elif kind == 'ple':
            ple(step[1])
        elif kind == 'final':
            final()
        elif kind == 'store_h':
            store_h()
    S.finish('sp', out_keys)
    S.emit(nc, stack)
    stack.close()
    nc._declared_inputs = set(declared)
    nc._declared_outputs = set(outs_decl)
    return nc


VOFF = {}
NVEC = 0


def _layout_vecs():
    global NVEC
    o = 0

    def add(name, n):
        nonlocal o
        VOFF[name] = o
        o += n
    add('eps', 1)
    for l in range(4):
        add('norm_mix%d' % l, 8)
        add('norm_ffn%d' % l, 8)
        add('norm_ple%d' % l, 8)
    add('norm_final', 8)
    add('one', 1)
    for j in range(2):
        for n in ('bg', 'br', 'cw0', 'cw1', 'cw2', 'cw3', 'cb', 'ba', 'bx', 'lam'):
            add('h%d_%s' % (j, n), 10)
        add('h%d_bo' % j, 8)
    add('gneps', 1)
    for j in range(2):
        for i in range(6):
            add('r%d_mu%d' % (j, i), 8)
        for n in ('w0', 'a0', 'v0', 'kk', 'ka', 'rk', 'lnw', 'lnb'):
            add('r%d_%s' % (j, n), 8)
    NVEC = o


_layout_vecs()


def colmajor(v):
    v = np.asarray(v, np.float32).reshape(-1)
    return np.ascontiguousarray(v.reshape(-1, 128).T)


def pack_vecs(inp):
    vecs = np.zeros((128, NVEC), np.float32)

    def put(name, v):
        cm = colmajor(v)
        vecs[:, VOFF[name]:VOFF[name] + cm.shape[1]] = cm
    vecs[:, VOFF['eps']] = EPS
    for l in range(4):
        put('norm_mix%d' % l, inp['norm_mix'][l])
        put('norm_ffn%d' % l, inp['norm_ffn'][l])
        put('norm_ple%d' % l, inp['norm_ple'][l])
    put('norm_final', inp['norm_final'])
    vecs[:, VOFF['one']] = 1.0
    for j in range(2):
        put('h%d_bg' % j, inp['lru_b_in'][j][:DR])
        put('h%d_br' % j, inp['lru_b_in'][j][DR:])
        for k in range(4):
            put('h%d_cw%d' % (j, k), inp['lru_conv_w'][j][k])
        put('h%d_cb' % j, inp['lru_conv_b'][j])
        put('h%d_ba' % j, inp['lru_b_gate_a'][j])
        put('h%d_bx' % j, inp['lru_b_gate_x'][j])
        put('h%d_lam' % j, inp['lru_lambda'][j])
        put('h%d_bo' % j, inp['lru_b_out'][j])
    vecs[:, VOFF['gneps']] = 64e-5
    for j in range(2):
        for i in range(6):
            put('r%d_mu%d' % (j, i), inp['rw_mu'][j][i])
        put('r%d_w0' % j, inp['rw_w0'][j])
        put('r%d_a0' % j, inp['rw_a0'][j])
        if j == 1:
            put('r%d_v0' % j, inp['rw_v0'][0])
        put('r%d_kk' % j, inp['rw_k_k'][j])
        put('r%d_ka' % j, inp['rw_k_a'][j])
        put('r%d_rk' % j, np.asarray(inp['rw_r_k'][j]).reshape(-1))
        put('r%d_lnw' % j, inp['rw_ln_w'][j])
        put('r%d_lnb' % j, inp['rw_ln_b'][j])
    return vecs


def launch(TL, prog, first, per_core, shared):
    nc = build(TL, prog, first=first)
    in_maps = []
    for c in range(NCORES):
        m = dict(shared)
        m.update(per_core[c])
        in_maps.append({k: v for k, v in m.items() if k in nc._declared_inputs})
        missing = nc._declared_inputs - set(in_maps[-1])
        assert not missing, missing
    res = run_bass_kernel_spmd(nc, in_maps, core_ids=list(range(NCORES)))
    return res.results


def run(inp, TL, layers=(0, 1, 2, 3), parts=('mix', 'ffn', 'ple')):
    f32 = lambda a: np.ascontiguousarray(np.asarray(a, np.float32))
    x = f32(inp['x']).reshape(-1, D)
    p = f32(inp['p']).reshape(4, -1, DPLE)
    shared = {
        'vecs': pack_vecs(inp), 'ident_in': np.eye(128, dtype=np.float32),
        'ffn_w_in': f32(inp['ffn_w_in']), 'ffn_w_out': f32(inp['ffn_w_out']),
        'ple_w_proj': f32(inp['ple_w_proj']), 'ple_w_gate': f32(inp['ple_w_gate']),
        'lru_w_in': f32(inp['lru_w_in']), 'lru_w_out': f32(inp['lru_w_out']),
        'lru_w_gate_a': f32(inp['lru_w_gate_a']).reshape(2, RC * 128, 128),
        'lru_w_gate_x': f32(inp['lru_w_gate_x']).reshape(2, RC * 128, 128),
    }
    shared.update(rwkv_shared(inp))
    per_core = []
    for c in range(NCORES):
        cv = np.zeros((128, 17), np.float32)
        cv[:, c] = 1.0
        if c > 0:
            cv[:, 8 + c - 1] = 1.0
            cv[:, 16] = 1.0
        per_core.append({'x': np.ascontiguousarray(x[c * TL:(c + 1) * TL]),
                         'p': np.ascontiguousarray(p[:, c * TL:(c + 1) * TL]), 'cvec': cv})

    def route_halo(res):
        for c in range(NCORES):
            per_core[c]['halo'] = res[c - 1]['tail'] if c > 0 else np.zeros((128, 32), np.float32)

    def take_h(res):
        for c in range(NCORES):
            per_core[c]['h_in'] = res[c]['h_out']

    layers = list(layers)
    mix = 'mix' in parts
    tailsteps = lambda l: ([('tail', l)] if mix else [])
    post = lambda l: ([('ffn', l)] if 'ffn' in parts else []) + ([('ple', l)] if 'ple' in parts else [])
    if not layers:
        res = launch(TL, [('final',)], True, per_core, shared)
    else:
        res = launch(TL, tailsteps(layers[0]) + [('store_h',)], True, per_core, shared)
        take_h(res)
        for li, l in enumerate(layers):
            if mix:
                route_halo(res)
                res1 = launch(TL, [('mix1', l)], False, per_core, shared)
                gather_summ(l, res1, per_core)
            nxt = (tailsteps(layers[li + 1]) + [('store_h',)]) if li + 1 < len(layers) else [('final',)]
            res = launch(TL, ([('mix2', l)] if mix else []) + post(l) + nxt, False, per_core, shared)
            if li + 1 < len(layers):
                take_h(res)
                keep_state(l, res, per_core)
    y = np.concatenate([res[c]['y'] for c in range(NCORES)], axis=0)
    return y.reshape(1, NCORES * TL, D).astype(np.float32)


def gather_summ(l, res1, per_core):
    if l % 2 == 0:
        g = np.ascontiguousarray(np.stack([res1[c]['summ'] for c in range(NCORES)], axis=1))
        for c in range(NCORES):
            per_core[c]['gsumm'] = g
    else:
        g = np.ascontiguousarray(np.stack([res1[c]['rsumm'] for c in range(NCORES)], axis=0))
        for c in range(NCORES):
            per_core[c]['grsumm'] = g


def keep_state(l, res, per_core):
    if l == 1:
        for c in range(NCORES):
            per_core[c]['vfirst'] = res[c]['vfirst_out']


def rwkv_shared(inp):
    f32 = lambda a: np.ascontiguousarray(np.asarray(a, np.float32))
    rc = np.zeros((128, 704), np.float32)
    i = np.arange(128)
    same = (i[:, None] // 64) == (i[None, :] // 64)
    su = ((i[:, None] < i[None, :]) & same).astype(np.float32)
    iu = ((i[:, None] <= i[None, :]) & same).astype(np.float32)
    sl = ((i[:, None] > i[None, :]) & same).astype(np.float32)
    rc[:, 0:384] = np.concatenate([su, iu, sl], axis=1)
    rc[0:64, 384:448] = 1.0
    rc[64:128, 448:512] = 1.0
    rc[:, 512:640] = 1.0
    rc[:, 512] = 0.0
    rc[:, 512 + 64] = 0.0
    rc[0:64, 640:704] = np.eye(64, dtype=np.float32)
    rc[64:128, 640:704] = np.eye(64, dtype=np.float32)
    d = {k: f32(inp[k]) for k in ('rw_w_rkv', 'rw_w_o', 'rw_w1', 'rw_w2', 'rw_a1', 'rw_a2', 'rw_v1', 'rw_v2', 'rw_g1', 'rw_g2')}
    d['rconst'] = rc
    return d


def kernel(**inputs):
    return run(inputs, 2048)
```
